# Optimizing a Trainium2 kernel written in Bass

```python
import math
import jax
import jax.numpy as jnp
from jax import lax
import numpy as np

D_MODEL = 1024
BATCH = 1
SEQ = 16384
DEPTH = 2
DEC_BATCH = 128
DEC_SEQ = 8
PAST_LEN = 16384
PAGE_SIZE = 128

N_HEADS = 16
N_KV_HEADS = 4
GROUP = N_HEADS // N_KV_HEADS
HEAD_DIM = D_MODEL // N_HEADS
WINDOW = 128
ROPE_THETA = 10000.0
ATTN_SCALE = 1.0 / math.sqrt(HEAD_DIM)
RET_HEADS = 4
RET_QK_DIM = D_MODEL // RET_HEADS
RET_V_DIM = 2 * D_MODEL // RET_HEADS
RET_CHUNK = 128
RET_THETA = 10000.0
FFN_DIM = 2816
LN_EPS = 1e-5
ALPHA = (2.0 * DEPTH) ** 0.25
BETA = (8.0 * DEPTH) ** -0.25
N_MIXERS = 2
N_ATTN_LAYERS = (DEPTH + 1) // 2
N_RET_LAYERS = DEPTH // 2

kernel_name = "hybrid_swa_sink_retention_macaron_deepnorm_step"

F32 = jnp.float32


def _layer_norm(x, g, b):
    xf = x.astype(F32)
    mu = jnp.mean(xf, axis=-1, keepdims=True)
    var = jnp.mean(jnp.square(xf - mu), axis=-1, keepdims=True)
    y = (xf - mu) * lax.rsqrt(var + LN_EPS)
    return (y * g.astype(F32) + b.astype(F32)).astype(x.dtype)


def _swiglu(x, wg, wu, wd):
    return (jax.nn.silu(x @ wg) * (x @ wu)) @ wd


def _half_ffn(x, wg, wu, wd, g, b):
    return _layer_norm(ALPHA * x + 0.5 * _swiglu(x, wg, wu, wd), g, b)


def _rope_half(x, pos):
    d = x.shape[-1]
    inv = ROPE_THETA ** (-jnp.arange(0, d, 2, dtype=F32) / d)
    ang = pos.astype(F32)[:, None] * inv[None, :]
    c = jnp.cos(ang)[:, None, :]
    s = jnp.sin(ang)[:, None, :]
    xf = x.astype(F32)
    x1, x2 = xf[..., : d // 2], xf[..., d // 2:]
    return jnp.concatenate([x1 * c - x2 * s, x2 * c + x1 * s], axis=-1).astype(x.dtype)


def _xpos_rotate(x, pos):
    d = x.shape[-1]
    inv = 1.0 / (RET_THETA ** jnp.linspace(0.0, 1.0, d // 2, dtype=F32))
    ang = pos.astype(F32)[:, None] * inv[None, :]
    c = jnp.cos(ang)[:, None, :]
    s = jnp.sin(ang)[:, None, :]
    xf = x.astype(F32).reshape(x.shape[:-1] + (d // 2, 2))
    xe, xo = xf[..., 0], xf[..., 1]
    return jnp.stack([xe * c - xo * s, xo * c + xe * s], axis=-1).reshape(x.shape)


def _sink_softmax(scores, valid, sinks):
    sink_b = sinks.astype(F32).reshape(N_KV_HEADS, GROUP)[:, :, None, None]
    s = jnp.where(valid, scores, -jnp.inf)
    m = jnp.maximum(jnp.max(s, axis=-1, keepdims=True), sink_b)
    e = jnp.exp(s - m)
    return e / (jnp.sum(e, axis=-1, keepdims=True) + jnp.exp(sink_b - m))


def _attn_proj(x, pos, w_qkv):
    B, T, _ = x.shape
    h = x @ w_qkv
    q = h[..., : N_HEADS * HEAD_DIM].reshape(B, T, N_HEADS, HEAD_DIM)
    k = h[..., N_HEADS * HEAD_DIM:(N_HEADS + N_KV_HEADS) * HEAD_DIM].reshape(B, T, N_KV_HEADS, HEAD_DIM)
    v = h[..., (N_HEADS + N_KV_HEADS) * HEAD_DIM:].reshape(B, T, N_KV_HEADS, HEAD_DIM)
    return _rope_half(q, pos), _rope_half(k, pos), v


def _swa_prompt(x, w_qkv, w_o, sinks):
    B, T, _ = x.shape
    q, k, v = _attn_proj(x, jnp.arange(T), w_qkv)
    nb = T // WINDOW
    qb = q.reshape(B, nb, WINDOW, N_KV_HEADS, GROUP, HEAD_DIM)
    kb = k.reshape(B, nb, WINDOW, N_KV_HEADS, HEAD_DIM)
    vb = v.reshape(B, nb, WINDOW, N_KV_HEADS, HEAD_DIM)

    def with_prev(a):
        prev = jnp.pad(a[:, :-1], ((0, 0), (1, 0), (0, 0), (0, 0), (0, 0)))
        return jnp.concatenate([prev, a], axis=2)

    kc, vc = with_prev(kb), with_prev(vb)
    scores = jnp.einsum('bnqkgd,bnskd->bnkgqs', qb, kc).astype(F32) * ATTN_SCALE
    blk = jnp.arange(nb)[:, None, None]
    qi = jnp.arange(WINDOW)[None, :, None]
    si = jnp.arange(2 * WINDOW)[None, None, :]
    dist = qi + WINDOW - si
    kpos = blk * WINDOW - WINDOW + si
    valid = (dist >= 0) & (dist < WINDOW) & (kpos >= 0)
    p = _sink_softmax(scores, valid[None, :, None, None], sinks)
    o = jnp.einsum('bnkgqs,bnskd->bnqkgd', p.astype(vc.dtype), vc).reshape(B, T, N_HEADS * HEAD_DIM)
    return o @ w_o, k[:, T - WINDOW:], v[:, T - WINDOW:]


def _swa_sample(x, k_buf, v_buf, w_qkv, w_o, sinks):
    B, L, _ = x.shape
    pos = PAST_LEN + jnp.arange(L)
    q, k, v = _attn_proj(x, pos, w_qkv)
    kall = jnp.concatenate([k_buf.astype(k.dtype), k], axis=1)
    vall = jnp.concatenate([v_buf.astype(v.dtype), v], axis=1)
    kpos = jnp.concatenate([PAST_LEN - WINDOW + jnp.arange(WINDOW), pos])
    dist = pos[:, None] - kpos[None, :]
    valid = (dist >= 0) & (dist < WINDOW) & (kpos >= 0)[None, :]
    qg = q.reshape(B, L, N_KV_HEADS, GROUP, HEAD_DIM)
    scores = jnp.einsum('bqkgd,bskd->bkgqs', qg, kall).astype(F32) * ATTN_SCALE
    p = _sink_softmax(scores, valid, sinks)
    o = jnp.einsum('bkgqs,bskd->bqkgd', p.astype(vall.dtype), vall).reshape(B, L, N_HEADS * HEAD_DIM)
    return o @ w_o, kall[:, -WINDOW:], vall[:, -WINDOW:]


def _ret_log_decay():
    return jnp.log(1.0 - 2.0 ** (-5.0 - jnp.arange(RET_HEADS, dtype=F32)))


def _ret_proj(x, pos, w_in):
    B, T, _ = x.shape
    h = x @ w_in
    nq = RET_HEADS * RET_QK_DIM
    nv = RET_HEADS * RET_V_DIM
    q = h[..., :nq].reshape(B, T, RET_HEADS, RET_QK_DIM)
    k = h[..., nq:2 * nq].reshape(B, T, RET_HEADS, RET_QK_DIM)
    v = h[..., 2 * nq:2 * nq + nv].reshape(B, T, RET_HEADS, RET_V_DIM).astype(F32)
    g = h[..., 2 * nq + nv:]
    q = _xpos_rotate(q, pos)
    k = _xpos_rotate(k, pos) * (RET_QK_DIM ** -0.5)
    return q, k, v, g


def _retention_chunk(S, q, k, v, lg):
    L = q.shape[1]
    idx = jnp.arange(L, dtype=F32)
    diff = idx[:, None] - idx[None, :]
    causal = diff >= 0
    dmat = jnp.where(causal[None], jnp.exp(lg[:, None, None] * jnp.where(causal, diff, 0.0)[None]), 0.0)
    att = jnp.einsum('bihd,bjhd->bhij', q, k) * dmat[None]
    o = jnp.einsum('bhij,bjhe->bihe', att, v)
    xi = jnp.exp(lg[None, :] * (idx[:, None] + 1.0))
    o = o + jnp.einsum('bihd,bhde->bihe', q, S) * xi[None, :, :, None]
    zeta = jnp.exp(lg[None, :] * (L - 1.0 - idx[:, None]))
    S_new = jnp.exp(lg * L)[None, :, None, None] * S + jnp.einsum('bjhd,bjhe->bhde', k * zeta[None, :, :, None], v)
    return S_new, o


def _retention_out(o, g, w_o, dtype):
    B, T = o.shape[:2]
    mu = jnp.mean(o, axis=-1, keepdims=True)
    var = jnp.mean(jnp.square(o - mu), axis=-1, keepdims=True)
    on = ((o - mu) * lax.rsqrt(var + LN_EPS)).reshape(B, T, RET_HEADS * RET_V_DIM)
    y = jax.nn.silu(g.astype(F32)) * on
    return y.astype(dtype) @ w_o


def _retention_prompt(x, w_in, w_o, lg):
    B, T, _ = x.shape
    q, k, v, g = _ret_proj(x, jnp.arange(T), w_in)
    nc = T // RET_CHUNK

    def to_chunks(a):
        return a.reshape((B, nc, RET_CHUNK) + a.shape[2:]).swapaxes(0, 1)

    S0 = jnp.zeros((B, RET_HEADS, RET_QK_DIM, RET_V_DIM), F32)
    S_fin, o = lax.scan(lambda S, c: _retention_chunk(S, c[0], c[1], c[2], lg), S0,
                        (to_chunks(q), to_chunks(k), to_chunks(v)))
    o = o.swapaxes(0, 1).reshape(B, T, RET_HEADS, RET_V_DIM)
    return _retention_out(o, g, w_o, x.dtype), S_fin


def _retention_sample(x, S, w_in, w_o, lg):
    B, L, _ = x.shape
    q, k, v, g = _ret_proj(x, PAST_LEN + jnp.arange(L), w_in)
    S_new, o = _retention_chunk(S.astype(F32), q, k, v, lg)
    return _retention_out(o, g, w_o, x.dtype), S_new


def setup_inputs(seed: int = 0) -> dict:
    key = jax.random.key(seed)
    ks = jax.random.split(key, 20)
    n = jax.random.normal
    D, F = D_MODEL, FFN_DIM
    qkv_w = (N_HEADS + 2 * N_KV_HEADS) * HEAD_DIM
    ret_in_w = RET_HEADS * (2 * RET_QK_DIM + 2 * RET_V_DIM)
    return {
        "x_prompt": n(ks[0], (BATCH, SEQ, D), F32),
        "x_sample": n(ks[1], (DEC_BATCH, DEC_SEQ, D), F32),
        "cache_k_win": n(ks[2], (N_ATTN_LAYERS, DEC_BATCH, WINDOW, N_KV_HEADS, HEAD_DIM), F32),
        "cache_v_win": n(ks[3], (N_ATTN_LAYERS, DEC_BATCH, WINDOW, N_KV_HEADS, HEAD_DIM), F32),
        "state_ret": 0.1 * n(ks[4], (N_RET_LAYERS, DEC_BATCH, RET_HEADS, RET_QK_DIM, RET_V_DIM), F32),
        "ffn1_w_gate": n(ks[5], (DEPTH, D, F), F32) * D ** -0.5,
        "ffn1_w_up": n(ks[6], (DEPTH, D, F), F32) * D ** -0.5,
        "ffn1_w_down": n(ks[7], (DEPTH, F, D), F32) * (F ** -0.5 * BETA),
        "ffn2_w_gate": n(ks[8], (DEPTH, D, F), F32) * D ** -0.5,
        "ffn2_w_up": n(ks[9], (DEPTH, D, F), F32) * D ** -0.5,
        "ffn2_w_down": n(ks[10], (DEPTH, F, D), F32) * (F ** -0.5 * BETA),
        "ln_g": 1.0 + 0.02 * n(ks[11], (DEPTH, 3, D), F32),
        "ln_b": 0.02 * n(ks[12], (DEPTH, 3, D), F32),
        "attn_w_qkv": n(ks[13], (N_ATTN_LAYERS, D, qkv_w), F32) * D ** -0.5,
        "attn_w_o": n(ks[14], (N_ATTN_LAYERS, N_HEADS * HEAD_DIM, D), F32) * ((N_HEADS * HEAD_DIM) ** -0.5 * BETA),
        "attn_sinks": n(ks[15], (N_ATTN_LAYERS, N_HEADS), F32),
        "ret_w_in": n(ks[16], (N_RET_LAYERS, D, ret_in_w), F32) * D ** -0.5,
        "ret_w_o": n(ks[17], (N_RET_LAYERS, RET_HEADS * RET_V_DIM, D), F32) * ((RET_HEADS * RET_V_DIM) ** -0.5 * BETA),
    }


def reference(x_prompt, x_sample, cache_k_win, cache_v_win, state_ret,
              ffn1_w_gate, ffn1_w_up, ffn1_w_down, ffn2_w_gate, ffn2_w_up, ffn2_w_down,
              ln_g, ln_b, attn_w_qkv, attn_w_o, attn_sinks, ret_w_in, ret_w_o):
    xp, xs = x_prompt, x_sample
    lg = _ret_log_decay()
    kp_l, vp_l, ks_l, vs_l, sp_l, ss_l = [], [], [], [], [], []
    for i in range(DEPTH):
        xp = _half_ffn(xp, ffn1_w_gate[i], ffn1_w_up[i], ffn1_w_down[i], ln_g[i, 0], ln_b[i, 0])
        xs = _half_ffn(xs, ffn1_w_gate[i], ffn1_w_up[i], ffn1_w_down[i], ln_g[i, 0], ln_b[i, 0])
        j = i // N_MIXERS
        if i % N_MIXERS == 0:
            mp, kp, vp = _swa_prompt(xp, attn_w_qkv[j], attn_w_o[j], attn_sinks[j])
            ms, kS, vS = _swa_sample(xs, cache_k_win[j], cache_v_win[j], attn_w_qkv[j], attn_w_o[j], attn_sinks[j])
            kp_l.append(kp.astype(x_prompt.dtype))
            vp_l.append(vp.astype(x_prompt.dtype))
            ks_l.append(kS.astype(cache_k_win.dtype))
            vs_l.append(vS.astype(cache_v_win.dtype))
        else:
            mp, sp = _retention_prompt(xp, ret_w_in[j], ret_w_o[j], lg)
            ms, sS = _retention_sample(xs, state_ret[j], ret_w_in[j], ret_w_o[j], lg)
            sp_l.append(sp.astype(x_prompt.dtype))
            ss_l.append(sS.astype(state_ret.dtype))
        xp = _layer_norm(ALPHA * xp + mp, ln_g[i, 1], ln_b[i, 1])
        xs = _layer_norm(ALPHA * xs + ms, ln_g[i, 1], ln_b[i, 1])
        xp = _half_ffn(xp, ffn2_w_gate[i], ffn2_w_up[i], ffn2_w_down[i], ln_g[i, 2], ln_b[i, 2])
        xs = _half_ffn(xs, ffn2_w_gate[i], ffn2_w_up[i], ffn2_w_down[i], ln_g[i, 2], ln_b[i, 2])
    return (xp, xs, jnp.stack(kp_l), jnp.stack(vp_l), jnp.stack(ks_l), jnp.stack(vs_l), jnp.stack(sp_l), jnp.stack(ss_l))
```

```python
import math
from contextlib import ExitStack

import numpy as np
import concourse.bass as bass
import concourse.mybir as mybir
from concourse.bass_utils import run_bass_kernel_spmd

F32 = mybir.dt.float32
BF16 = mybir.dt.bfloat16
AF = mybir.ActivationFunctionType
ALU = mybir.AluOpType
AX = mybir.AxisListType

D = 1024
FF = 2816
NFC = FF // 128
DEPTH = 2
ALPHA = (2.0 * DEPTH) ** 0.25
LN_EPS = 1e-5
N_CORES = 8
HD = 64
NH = 16
NKV = 4
WINDOW = 128
PAST = 16384
RH = 4
RQK = 256
RV = 512


class Op:
    __slots__ = ("eng", "fn", "deps", "dma", "token", "needs_inc", "idx", "xset")


class Sched:
    ENGS = ("pe", "act", "dve", "pool", "sp")

    def __init__(self, nc, stack, n_dma_sems=100):
        self.nc = nc
        self.sem = {e: stack.enter_context(nc.semaphore("s_" + e)) for e in self.ENGS}
        self.cnt = {e: 0 for e in self.ENGS}
        self.stack = stack
        self.dma_sem = {}
        self.dma_cnt = {}
        self.waited = {e: {} for e in self.ENGS}
        self.ops = []
        self.last_w = {}
        self.readers = {}
        self.all_dma_tokens = {}

    def _sem_for(self, key):
        if key not in self.dma_sem:
            self.dma_sem[key] = self.stack.enter_context(self.nc.semaphore("dq_" + key))
            self.dma_cnt[key] = 0
        return self.dma_sem[key]

    def add(self, eng, fn, r=(), w=(), dma=None, x=()):
        op = Op()
        op.eng, op.fn, op.dma, op.needs_inc, op.token = eng, fn, dma, False, None
        op.idx = len(self.ops)
        op.xset = frozenset(x)
        deps = set()
        for res in r:
            lw = self.last_w.get(res)
            if lw is not None:
                deps.add(lw)
        for res in w:
            lw = self.last_w.get(res)
            if lw is not None:
                deps.add(lw)
            for rd in self.readers.get(res, ()):
                deps.add(rd)
        for res in x:
            lw = self.last_w.get(res)
            if lw is not None and not (lw.eng == eng and res in lw.xset):
                deps.add(lw)
            for rd in self.readers.get(res, ()):
                deps.add(rd)
        deps.discard(op)
        op.deps = deps
        for res in r:
            self.readers.setdefault(res, []).append(op)
        for res in list(w) + list(x):
            self.last_w[res] = op
            self.readers[res] = []
        if dma is not None:
            self._sem_for(dma)
        self.ops.append(op)
        return op

    def pe(self, fn, r=(), w=(), x=()):
        return self.add("pe", fn, r, w, x=x)

    def act(self, fn, r=(), w=(), x=()):
        return self.add("act", fn, r, w, x=x)

    def dve(self, fn, r=(), w=(), x=()):
        return self.add("dve", fn, r, w, x=x)

    def pool(self, fn, r=(), w=(), x=()):
        return self.add("pool", fn, r, w, x=x)

    def dma(self, q, key, fn, r=(), w=()):
        return self.add(q, fn, r, w, dma=key)

    def flush(self, final_wait_all_dma=True):
        ops = self.ops
        if not ops:
            return
        for op in ops:
            for d in op.deps:
                if d.dma is None:
                    if d.eng == "pe" and op.eng == "pe":
                        continue
                    d.needs_inc = True
        for op in ops:
            if op.dma is not None:
                self.dma_cnt[op.dma] += 16
                op.token = (self.dma_sem[op.dma], self.dma_cnt[op.dma])
                self.all_dma_tokens[op.dma] = op.token
            elif op.needs_inc:
                self.cnt[op.eng] += 1
                op.token = (self.sem[op.eng], self.cnt[op.eng])
        plan = {e: [] for e in self.ENGS}
        for op in ops:
            need = {}
            for d in op.deps:
                if d.dma is None and d.eng == "pe" and op.eng == "pe":
                    continue
                s, v = d.token
                k = id(s)
                if k not in need or need[k][1] < v:
                    need[k] = (s, v)
            waits = []
            wd = self.waited[op.eng]
            for k, (s, v) in need.items():
                if wd.get(k, 0) >= v:
                    continue
                wd[k] = v
                waits.append((s, v))
            plan[op.eng].append((op, waits))
        if final_wait_all_dma:
            fin = []
            wd = self.waited["sp"]
            for key, (s, v) in self.all_dma_tokens.items():
                if wd.get(id(s), 0) >= v:
                    continue
                wd[id(s)] = v
                fin.append((s, v))
        else:
            fin = []

        def emit(engname):
            lst = plan[engname]

            def body(e):
                for op, waits in lst:
                    for s, v in waits:
                        e.wait_ge(s, v)
                    ins = op.fn(e)
                    if op.token is not None:
                        if op.dma is not None:
                            ins.then_inc(op.token[0], 16)
                        else:
                            ins.then_inc(op.token[0], 1)
                if engname == "sp":
                    for s, v in fin:
                        e.wait_ge(s, v)
            return body

        with self.nc.Block() as blk:
            if plan["pe"]:
                blk.tensor(emit("pe"))
            if plan["act"]:
                blk.scalar(emit("act"))
            if plan["dve"]:
                blk.vector(emit("dve"))
            if plan["pool"]:
                blk.gpsimd(emit("pool"))
            if plan["sp"] or fin:
                blk.sync(emit("sp"))
        self.ops = []
        self.last_w = {}
        self.readers = {}


ATTN_SCALE = 1.0 / math.sqrt(HD)
GAM = [1.0 - 2.0 ** (-5.0 - h) for h in range(RH)]
LG = [math.log(g) for g in GAM]


class Cfg:
    def __init__(self, npt=16, n_cores=N_CORES, G=6, stop_after=None):
        self.npt = npt
        self.samp = npt
        self.halo = npt + 1
        self.nt = npt + 2
        self.n_cores = n_cores
        self.G = G
        self.stop_after = stop_after

    def groups(self, with_halo):
        g = [(i, min(4, self.npt - i)) for i in range(0, self.npt, 4)]
        g.append((self.samp, 2 if with_halo else 1))
        return g


class Prog:
    def __init__(self, cfg):
        self.cfg = cfg
        self.nc = bass.Bass("TRN2", target_bir_lowering=False)
        self.stack = ExitStack()
        self.uid = 0

    def dram_in(self, name, shape, dt=F32):
        return self.nc.dram_tensor(name, list(shape), dt, kind="ExternalInput").ap()

    def dram_out(self, name, shape, dt=F32):
        return self.nc.dram_tensor(name, list(shape), dt, kind="ExternalOutput").ap()

    def sb(self, name, shape, dt, stack=None):
        self.uid += 1
        return (stack or self.stack).enter_context(
            self.nc.sbuf_tensor("%s_%d" % (name, self.uid), list(shape), dt))

    def ps(self, name, shape, dt, stack=None):
        self.uid += 1
        return (stack or self.stack).enter_context(
            self.nc.psum_tensor("%s_%d" % (name, self.uid), list(shape), dt))


def TT(out, in0, in1, op):
    return lambda e: e.tensor_tensor(out=out, in0=in0, in1=in1, op=op)


def STT(out, in0, scalar, in1, op0, op1):
    return lambda e: e.scalar_tensor_tensor(out=out, in0=in0, scalar=scalar, in1=in1, op0=op0, op1=op1)


def TS(out, in0, s1, op0):
    return lambda e: e.tensor_scalar(out=out, in0=in0, scalar1=s1, scalar2=None, op0=op0)


def CP(out, in_):
    return lambda e: e.tensor_copy(out=out, in_=in_)


def ACP(out, in_):
    return lambda e: e.copy(out=out, in_=in_)


def AMUL(out, in_, m):
    return lambda e: e.mul(out=out, in_=in_, mul=m)


def ACTF(out, in_, func, **kw):
    return lambda e: e.activation(out=out, in_=in_, func=func, **kw)


def DMA(out, in_):
    return lambda e: e.dma_start(out=out, in_=in_)


def MM8(out, lhs_of_k, rhs_of_k, n=8):
    def f(e):
        ins = None
        for k in range(n):
            ins = e.matmul(out, lhs_of_k(k), rhs_of_k(k), start=(k == 0), stop=(k == n - 1))
        return ins
    return f


def TRS(outs, ins_, ident):
    def f(e):
        ins = None
        for o, i in zip(outs, ins_):
            ins = e.transpose(o, i, ident)
        return ins
    return f


class _GBView:
    def __init__(self, t):
        self.t = t

    def __getitem__(self, k):
        return self.t[k]


class Rec:
    def __init__(self):
        self.ops = []

    def add(self, eng, fn, r=(), w=(), dma=None, x=()):
        self.ops.append((eng, fn, tuple(r), tuple(w), dma, tuple(x)))

    def pe(self, fn, r=(), w=(), x=()):
        self.add("pe", fn, r, w, x=x)

    def act(self, fn, r=(), w=(), x=()):
        self.add("act", fn, r, w, x=x)

    def dve(self, fn, r=(), w=(), x=()):
        self.add("dve", fn, r, w, x=x)

    def pool(self, fn, r=(), w=(), x=()):
        self.add("pool", fn, r, w, x=x)

    def dma(self, q, key, fn, r=(), w=()):
        self.add(q, fn, r, w, dma=key)


def interleave(S, recs):
    recs = [r for r in recs if r is not None and r.ops]
    idx = [0] * len(recs)
    left = sum(len(r.ops) for r in recs)
    while left:
        for i, r in enumerate(recs):
            if idx[i] < len(r.ops):
                eng, fn, rr, ww, dma, xx = r.ops[idx[i]]
                S.add(eng, fn, rr, ww, dma=dma, x=xx)
                idx[i] += 1
                left -= 1


def build_program(cfg, stage):
    P = Prog(cfg)
    nc = P.nc
    NT = cfg.nt
    npt = cfg.npt
    NTOK = NT * 128
    NB = 16
    with P.stack:
        w = {}

        def IN(nm, shp):
            w[nm] = P.dram_in(nm, shp)
            return w[nm]

        IN("identf", [128, 128])
        IN("ln_g", [DEPTH * 3, D])
        IN("ln_b", [DEPTH * 3, D])
        IN("masks", [5, 128, 512])
        if stage == 1:
            xin = IN("xin", [NTOK, D])
            for nm in ("ffn1", "ffn2"):
                IN(nm + "_wg", [DEPTH, D, FF]); IN(nm + "_wu", [DEPTH, D, FF]); IN(nm + "_wd", [DEPTH, FF, D])
            IN("attn_w_qkv", [D, 1536]); IN("attn_w_o", [D, D]); IN("sinks2", [2, 8])
            IN("cache_k", [NB, 128, 256]); IN("cache_v", [NB, 128, 256])
            IN("rope", [NTOK, 128])
            IN("ret_w_in", [D, 6144])
            IN("xpos", [NTOK, 512]); IN("zeta1", [128, npt * 4])
            xmid_o = P.dram_out("xmid", [(npt + 1) * 128, D])
            F_o = P.dram_out("F", [RH, RQK, RV])
            kwp_o = P.dram_out("kwin_p", [128, 256]); vwp_o = P.dram_out("vwin_p", [128, 256])
            kws_o = P.dram_out("kwin_s", [NB, 128, 256]); vws_o = P.dram_out("vwin_s", [NB, 128, 256])
        else:
            xmid_i = IN("xmid", [(npt + 1) * 128, D])
            IN("ffn2_wg", [DEPTH, D, FF]); IN("ffn2_wu", [DEPTH, D, FF]); IN("ffn2_wd", [DEPTH, FF, D])
            IN("ret_w_in", [D, 6144]); IN("ret_w_o", [2048, D])
            IN("xpos", [NTOK, 512]); IN("rscal", [128, 24]); IN("coef", [128, 32])
            IN("xposh", [RH, npt * 128, 1024])
            IN("ind", [128, NB]); IN("indT", [128, NB * 128])
            IN("fall", [N_CORES, RH, RQK, RV])
            IN("state", [NB, RH, RQK, RV])
            yout = P.dram_out("y", [(npt + 1) * 128, D])
            stp_o = P.dram_out("state_p", [RH, RQK, RV])
            sts_o = P.dram_out("state_s", [NB, RH, RQK, RV])

        x = P.sb("x", [128, NT, D], F32)
        xT = P.sb("xT", [128, 8, NTOK], BF16)
        ident = P.sb("ident", [128, 128], BF16)

        S = Sched(nc, P.stack)

        def post_tile(S, t, xb, ti, psTl, pre_scale=None):
            b = ti % xb.shape[1]
            pb = ti % len(psTl)
            pT = psTl[pb]
            if pre_scale is None:
                S.act(ACP(xb[:, b, :], x[:, t, :]), r=[("x", t, 0), ("x", t, 1)], w=[("xb", b)])
            else:
                S.act(AMUL(xb[:, b, :], x[:, t, :], pre_scale), r=[("x", t, 0), ("x", t, 1)], w=[("xb", b)])
            S.pe(TRS([pT[:, k, :] for k in range(8)], [xb[:, b, k * 128:(k + 1) * 128] for k in range(8)], ident[:]),
                 r=[("xb", b), ("ident",)], w=[("psT", pb)])
            S.dve(CP(xT[:, :, t * 128:(t + 1) * 128], pT[:]), w=[("xT", t)], x=[("psT", pb)])
            if pre_scale is None:
                S.act(AMUL(x[:, t, :], x[:, t, :], ALPHA), r=[("xb", b)], w=[("x", t, 0), ("x", t, 1)])

        def norm_ops(S, j, lnt, src_lo, src_hi, r_lo, r_hi):
            st, mv, sc = lnt["stats"], lnt["mv"], lnt["sc"]
            S.dve(lambda e: e.bn_stats(out=st[:, j, 0, :], in_=src_lo), r=r_lo[0], w=[("st", j, 0)], x=r_lo[1])
            if src_hi is not None:
                S.dve(lambda e: e.bn_stats(out=st[:, j, 1, :], in_=src_hi), r=r_hi[0], w=[("st", j, 1)], x=r_hi[1])
                S.dve(lambda e: e.bn_aggr(out=mv[:, j, :], in_=st[:, j, :, :].rearrange("p a b -> p (a b)")),
                      r=[("st", j, 0), ("st", j, 1)], w=[("mv", j)])
            else:
                S.dve(lambda e: e.bn_aggr(out=mv[:, j, :], in_=st[:, j, 0, :]), r=[("st", j, 0)], w=[("mv", j)])
            S.act(ACTF(sc[:, j, 0:1], mv[:, j, 1:2], AF.Sqrt, bias=lnt["eps"][:, 0:1]),
                  r=[("mv", j), ("eps",)], w=[("sc", j, 0)])
            S.dve(lambda e: e.reciprocal(out=sc[:, j, 1:2], in_=sc[:, j, 0:1]), r=[("sc", j, 0)], w=[("sc", j, 1)])
            S.dve(STT(sc[:, j, 2:3], mv[:, j, 0:1], -1.0, sc[:, j, 1:2], ALU.mult, ALU.mult),
                  r=[("mv", j), ("sc", j, 1)], w=[("sc", j, 2)])

        def ln_a1(S, t, ti, lnt):
            j = ti % lnt["ring"]
            st, mv, sc = lnt["stats"], lnt["mv"], lnt["sc"]
            S.dve(lambda e: e.bn_stats(out=st[:, j, 0, :], in_=x[:, t, 0:512]), r=[("x", t, 0)], w=[("st", j, 0)])
            S.dve(lambda e: e.bn_stats(out=st[:, j, 1, :], in_=x[:, t, 512:1024]), r=[("x", t, 1)], w=[("st", j, 1)])
            S.dve(lambda e: e.bn_aggr(out=mv[:, j, :], in_=st[:, j, :, :].rearrange("p a b -> p (a b)")),
                  r=[("st", j, 0), ("st", j, 1)], w=[("mv", j)])
            S.act(ACTF(sc[:, j, 0:1], mv[:, j, 1:2], AF.Sqrt, bias=lnt["eps"][:, 0:1]),
                  r=[("mv", j), ("eps",)], w=[("sc", j, 0)])
            S.dve(lambda e: e.reciprocal(out=sc[:, j, 1:2], in_=sc[:, j, 0:1]), r=[("sc", j, 0)], w=[("sc", j, 1)])

        def ln_a2(S, t, ti, gb, lnt, gbkey="gb"):
            j = ti % lnt["ring"]
            mv, sc = lnt["mv"], lnt["sc"]
            S.dve(lambda e: e.tensor_scalar(out=x[:, t, :], in0=x[:, t, :], scalar1=mv[:, j, 0:1], scalar2=sc[:, j, 1:2],
                                            op0=ALU.subtract, op1=ALU.mult),
                  r=[("mv", j), ("sc", j, 1)], w=[("x", t, 0), ("x", t, 1)])
            S.pool(TT(x[:, t, :], x[:, t, :], gb[:, 0, :], ALU.mult), r=[(gbkey, 0)], w=[("x", t, 0), ("x", t, 1)])
            S.pool(TT(x[:, t, :], x[:, t, :], gb[:, 1, :], ALU.add), r=[(gbkey, 1)], w=[("x", t, 0), ("x", t, 1)])

        def ln_a(S, t, ti, gb, lnt, gbkey="gb"):
            ln_a1(S, t, ti, lnt)
            ln_a2(S, t, ti, gb, lnt, gbkey)

        def ln_b(S, t, ti, xb, psTl, final_out=None):
            if final_out is None:
                post_tile(S, t, xb, ti, psTl)
            else:
                S.dma("sp", "yout%d" % (ti % 4), DMA(final_out[t * 128:(t + 1) * 128, :], x[:, t, :]),
                      r=[("x", t, 0), ("x", t, 1)])

        def ln_tile(S, t, ti, gb, lnt, xb, psTl, final_out=None, gbkey="gb"):
            ln_a(S, t, ti, gb, lnt, gbkey)
            ln_b(S, t, ti, xb, psTl, final_out)

        def load_gb(S, gb, idx):
            S.dma("sp", "gb", DMA(gb[:, 0, :], w["ln_g"][idx:idx + 1, :].partition_broadcast(128)), w=[("gb", 0)])
            S.dma("sp", "gb2", DMA(gb[:, 1, :], w["ln_b"][idx:idx + 1, :].partition_broadcast(128)), w=[("gb", 1)])

        def ln_bufs(ph, ring=3):
            lnt = {
                "stats": P.sb("lnstats", [128, ring, 2, 6], F32, ph),
                "mv": P.sb("lnmv", [128, ring, 2], F32, ph),
                "sc": P.sb("lnsc", [128, ring, 4], F32, ph),
                "eps": P.sb("lneps", [128, 1], F32, ph),
                "ring": ring,
            }
            S.dve(lambda e: e.memset(lnt["eps"][:], LN_EPS), w=[("eps",)])
            return lnt

        def dump_x(name):
            dbg = P.dram_out(name, [NT * 128, D])
            for t in range(NT):
                S.dma("sp", "tile%d" % t, DMA(dbg[t * 128:(t + 1) * 128, :], x[:, t, :]),
                      r=[("x", t, 0), ("x", t, 1)])
            S.flush()

        with ExitStack() as ph:
            psT0 = [P.ps("psT0", [128, 8, 128], BF16, ph) for _ in range(2)]
            xb = P.sb("xb0", [128, 2, D], BF16, ph)
            identf = P.sb("identf_sb", [128, 128], F32, ph)
            S.dma("sp", "identf", DMA(identf[:], w["identf"][:, :]), w=[("identf",)])
            S.act(ACP(ident[:], identf[:]), r=[("identf",)], w=[("ident",)])
            if stage == 1:
                for t in range(NT):
                    S.dma("sp", "tile%d" % t, DMA(x[:, t, :], xin[t * 128:(t + 1) * 128, :]),
                          w=[("x", t, 0), ("x", t, 1)])
                for t in range(NT):
                    post_tile(S, t, xb, t, psT0)
            else:
                for t in range(npt + 1):
                    S.dma("sp", "tile%d" % t, DMA(x[:, t, :], xmid_i[t * 128:(t + 1) * 128, :]),
                          w=[("x", t, 0), ("x", t, 1)])
                for t in range(npt + 1):
                    post_tile(S, t, xb, t, psT0, pre_scale=1.0 / ALPHA)
            S.flush()

        def ffn_phase(S, wg, wu, wd, ln_idx, with_halo, final_out=None, pre_ln=None, pre_tiles=None):
            G = cfg.G
            groups = cfg.groups(with_halo)
            tiles = []
            tgrp = {}
            for gi, (t0, n) in enumerate(groups):
                for t in range(t0, t0 + n):
                    tiles.append(t)
                    tgrp[t] = gi
            ntl_ = len(tiles)
            parts = [list(range(i, min(i + G, NFC))) for i in range(0, NFC, G)]
            RW = 5
            RD = RW + G - 1
            with ExitStack() as ph:
                psA = [P.ps("psA", [128, 512], F32, ph) for _ in range(6)]
                psTl = [P.ps("psT", [128, 8, 128], BF16, ph) for _ in range(2)]
                hT = P.sb("hT", [128, G, NTOK], BF16, ph)
                wgs = P.sb("wgs", [128, RW, 8, 128], BF16, ph)
                wus = P.sb("wus", [128, RW, 8, 128], BF16, ph)
                wds = P.sb("wds", [128, RD, D], BF16, ph)
                sg = P.sb("sg", [128, 2, 512], F32, ph)
                gb = P.sb("gb", [128, 2, D], F32, ph)
                xb = P.sb("xb", [128, 2, D], BF16, ph)
                lnt = ln_bufs(ph, ring=8)
                if pre_ln is not None:
                    gbp = P.sb("gbp", [128, 2, D], F32, ph)
                    S.dma("sp", "gbp", DMA(gbp[:, 0, :], w["ln_g"][pre_ln:pre_ln + 1, :].partition_broadcast(128)), w=[("gbp", 0)])
                    S.dma("sp", "gbp2", DMA(gbp[:, 1, :], w["ln_b"][pre_ln:pre_ln + 1, :].partition_broadcast(128)), w=[("gbp", 1)])
                load_gb(S, gb, ln_idx)

                def issue_w(S, fc):
                    sl = fc % RW
                    sd = fc % RD
                    S.dma("pool", "wg%d" % sl,
                          DMA(wgs[:, sl, :, :], wg[:, fc * 128:(fc + 1) * 128].rearrange("(k p) f -> p k f", p=128)),
                          w=[("wg", sl)])
                    S.dma("pool", "wu%d" % sl,
                          DMA(wus[:, sl, :, :], wu[:, fc * 128:(fc + 1) * 128].rearrange("(k p) f -> p k f", p=128)),
                          w=[("wu", sl)])
                    S.dma("pool", "wd%d" % sd, DMA(wds[:, sd, :], wd[fc * 128:(fc + 1) * 128, :]), w=[("wd", sd)])

                cnt = {"it": 0, "yi": 0}

                def gu(S, fc, fl, gi):
                    t0, n = groups[gi]
                    it = cnt["it"]
                    cnt["it"] += 1
                    sl = fc % RW
                    N = n * 128
                    c0 = t0 * 128
                    bg = psA[it % 2]
                    bu = psA[2 + it % 2]
                    sb_ = it % 2
                    xr = [("xT", t) for t in range(t0, t0 + n)]
                    S.pe(MM8(bg[:, 0:N], (lambda k: wgs[:, sl, k, :]), (lambda k: xT[:, k, c0:c0 + N])),
                         r=[("wg", sl)] + xr, w=[("ps", it % 2)])
                    S.pe(MM8(bu[:, 0:N], (lambda k: wus[:, sl, k, :]), (lambda k: xT[:, k, c0:c0 + N])),
                         r=[("wu", sl)] + xr, w=[("ps", 2 + it % 2)])
                    S.act(ACTF(sg[:, sb_, 0:N], bg[:, 0:N], AF.Silu), w=[("sg", sb_)], x=[("ps", it % 2)])
                    S.dve(TT(hT[:, fl, c0:c0 + N], bu[:, 0:N], sg[:, sb_, 0:N], ALU.mult),
                          r=[("sg", sb_)], w=[("hT", fl, gi)], x=[("ps", 2 + it % 2)])

                def down(S, part, t, bank=None):
                    for dh in range(2):
                        yi = cnt["yi"]
                        cnt["yi"] += 1
                        bi = (4 + yi % 2) if bank is None else bank
                        by = psA[bi]

                        def mmd(e, by=by, dh=dh):
                            ins = None
                            for fl, fc in enumerate(part):
                                ins = e.matmul(by[:, :], hT[:, fl, t * 128:(t + 1) * 128],
                                               wds[:, fc % RD, dh * 512:(dh + 1) * 512],
                                               start=(fl == 0), stop=(fl == len(part) - 1))
                            return ins
                        S.pe(mmd, r=[("hT", fl, tgrp[t]) for fl in range(len(part))] +
                             [("wd", fc % RD) for fc in part], w=[("ps", bi)])
                        S.dve(STT(x[:, t, dh * 512:(dh + 1) * 512], by[:, :], 0.5,
                                  x[:, t, dh * 512:(dh + 1) * 512], ALU.mult, ALU.add),
                              w=[("x", t, dh)], x=[("ps", bi)])

                for fc in range(min(RW - 1, NFC)):
                    issue_w(S, fc)

                first_done = set()
                if pre_ln is not None:
                    pre_set = set(tiles if pre_tiles is None else pre_tiles)
                    L = [ti for ti, t in enumerate(tiles) if t in pre_set]
                    Ra, Ra2, Rb = {}, {}, {}
                    for ti in L:
                        t = tiles[ti]
                        Ra[ti] = Rec(); ln_a1(Ra[ti], t, ti, lnt)
                        Ra2[ti] = Rec(); ln_a2(Ra2[ti], t, ti, gbp, lnt, "gbp")
                        Rb[ti] = Rec(); ln_b(Rb[ti], t, ti, xb, psTl)
                    nL = len(L)
                    done_at = {}
                    for k, ti in enumerate(L):
                        done_at[ti] = (k - k % 2) + 4
                    for i in range(0, nL + 6, 2):
                        grp = [Ra[L[k]] for k in (i, i + 1) if k < nL]
                        grp += [Ra2[L[k]] for k in (i - 2, i - 1) if 0 <= k < nL]
                        grp += [Rb[L[k]] for k in (i - 4, i - 3) if 0 <= k < nL]
                        for gi, (t0, n) in enumerate(groups):
                            need = [tiles.index(t) for t in range(t0, t0 + n) if t in pre_set]
                            ready = all(done_at[ti] < i for ti in need)
                            if gi not in first_done and ready:
                                R = Rec()
                                if not first_done and RW - 1 < NFC:
                                    issue_w(R, RW - 1)
                                gu(R, 0, 0, gi)
                                grp.append(R)
                                first_done.add(gi)
                                break
                        interleave(S, grp)
                    for gi in range(len(groups)):
                        if gi not in first_done:
                            if not first_done and RW - 1 < NFC:
                                issue_w(S, RW - 1)
                            gu(S, 0, 0, gi)
                            first_done.add(gi)

                for pi, part in enumerate(parts):
                    last = (pi == len(parts) - 1)
                    if not last:
                        for fl, fc in enumerate(part):
                            if fc + RW - 1 < NFC and not (fc == 0 and first_done):
                                issue_w(S, fc + RW - 1)
                            for gi in range(len(groups)):
                                if fc == 0 and gi in first_done:
                                    continue
                                gu(S, fc, fl, gi)
                        for t in tiles:
                            down(S, part, t)
                        continue
                    assert len(part) <= RW - 1 and part[-1] == NFC - 1
                    Rd, Ra, Ra2, Rb, Rgu = {}, {}, {}, {}, {}
                    for gi in range(len(groups)):
                        Rgu[gi] = Rec()
                        for fl, fc in enumerate(part):
                            gu(Rgu[gi], fc, fl, gi)
                    for ti, t in enumerate(tiles):
                        Rd[ti] = Rec(); down(Rd[ti], part, t, bank=4 + ti % 2)
                        Ra[ti] = Rec(); ln_a1(Ra[ti], t, ti, lnt)
                        Ra2[ti] = Rec(); ln_a2(Ra2[ti], t, ti, gb, lnt)
                        Rb[ti] = Rec(); ln_b(Rb[ti], t, ti, xb, psTl, final_out)
                    interleave(S, [Rgu[0]])
                    first_ti = [tiles.index(t0) for (t0, n) in groups] + [ntl_]
                    inject = {}
                    for gi in range(1, len(groups)):
                        steps = list(range(first_ti[gi - 1] - first_ti[gi - 1] % 2, first_ti[gi], 2))
                        ops = Rgu[gi].ops
                        per = (len(ops) + len(steps) - 1) // len(steps)
                        for si, st_ in enumerate(steps):
                            R = Rec()
                            R.ops = ops[si * per:(si + 1) * per]
                            inject.setdefault(st_, []).append(R)
                    for i in range(0, ntl_ + 8, 2):
                        grp = inject.get(i, [])
                        grp += [Rd[k] for k in (i, i + 1) if k < ntl_]
                        grp += [Ra[k] for k in (i - 2, i - 1) if 0 <= k < ntl_]
                        grp += [Ra2[k] for k in (i - 4, i - 3) if 0 <= k < ntl_]
                        grp += [Rb[k] for k in (i - 6, i - 5) if 0 <= k < ntl_]
                        interleave(S, grp)
                S.flush()

        def attn_phases(S):
            with ExitStack() as po:
                wo = P.sb("wo", [128, 8, D], BF16, po)
                es2 = P.sb("es2", [128, 8], F32, po)
                ones64 = P.sb("ones64", [128, 64], BF16, po)
                QTs = P.sb("QTs", [64, 16, 128], BF16, po)
                KTs = P.sb("KTs", [64, 4, 128], BF16, po)
                Vbs = P.sb("Vbs", [128, 256], BF16, po)
                krs = P.sb("krs", [128, 256], F32, po)
                vfs = P.sb("vfs", [128, 256], F32, po)
                e_ = P.sb("e", [128, 2, 512], BF16, po)
                rd = P.sb("rd", [128, 2, 256], F32, po)
                oT = P.sb("oT", [128, 2, 8, 128], BF16, po)

                def common_loads(S, masks, m0, nm):
                    S.dma("pool", "masks", DMA(masks[:], w["masks"][m0:m0 + nm].rearrange("m p n -> p m n")), w=[("masks",)])

                def normalize(S, g, ob, os_, psOD):
                    rdv = rd[:, ob, :].rearrange("p (c q) -> p c q", c=2)
                    S.dve(TT(rdv, psOD[ob][:, 256:512].rearrange("p (c q) -> p c q", c=2),
                             es2[:, 2 * g:2 * g + 2].unsqueeze(2).broadcast_to([128, 2, 128]), ALU.add),
                          r=[("es2",)], w=[("rd", ob)], x=[("psod", ob)])
                    S.dve(lambda e: e.reciprocal(out=rd[:, ob, :], in_=rd[:, ob, :]), w=[("rd", ob)])
                    S.dve(TT(oT[:, os_, 2 * g:2 * g + 2, :], psOD[ob][:, 0:256].rearrange("p (c q) -> p c q", c=2),
                             rdv, ALU.mult), r=[("rd", ob)], w=[("oT", os_, g)], x=[("psod", ob)])

                def outproj_add(S, t, os_, psY, ytag):
                    for dh in range(2):
                        def f(e, dh=dh):
                            ins = None
                            for c in range(8):
                                ins = e.matmul(psY[dh][:, :], oT[:, os_, c, :], wo[:, c, dh * 512:(dh + 1) * 512],
                                               start=(c == 0), stop=(c == 7))
                            return ins
                        S.pe(f, r=[("oT", os_, g) for g in range(4)] + [("wo",)], w=[(ytag, dh)])
                        S.dve(TT(x[:, t, dh * 512:(dh + 1) * 512], psY[dh][:, :], x[:, t, dh * 512:(dh + 1) * 512], ALU.add),
                              w=[("x", t, dh)], x=[(ytag, dh)])

                S.dma("pool", "wo", DMA(wo[:], w["attn_w_o"].rearrange("(c p) n -> p c n", p=128)), w=[("wo",)])
                S.dma("sp", "es2a", DMA(es2[0:64, :], w["sinks2"][0:1, :].partition_broadcast(64)), w=[("es2", 0)])
                S.dma("sp", "es2b", DMA(es2[64:128, :], w["sinks2"][1:2, :].partition_broadcast(64)), w=[("es2", 1)])
                S.act(ACTF(es2[:], es2[:], AF.Exp), r=[("es2", 0), ("es2", 1)], w=[("es2",)])
                S.dve(lambda e: e.memset(ones64[:], 1.0), w=[("ones",)])

                with ExitStack() as ph:
                    psQ = [P.ps("psQ", [128, 512], F32, ph) for _ in range(2)]
                    psKV = P.ps("psKV", [128, 512], F32, ph)
                    psT = P.ps("psT", [128, 8, 128], BF16, ph)
                    psS = [P.ps("psS", [128, 512], F32, ph) for _ in range(2)]
                    psOD = [P.ps("psOD", [128, 512], F32, ph) for _ in range(2)]
                    masks = P.sb("masks", [128, 3, 512], BF16, ph)
                    wqkv = P.sb("wqkv", [128, 8, 1536], BF16, ph)
                    rope_sb = P.sb("rope", [128, 2, 128], F32, ph)
                    ropeA = P.sb("ropeA", [128, 8, 64], F32, ph)
                    ropeB = P.sb("ropeB", [128, 8, 64], F32, ph)
                    kA = P.sb("kA", [128, 4, 64], F32, ph)
                    kB = P.sb("kB", [128, 4, 64], F32, ph)
                    qrot = P.sb("qrot", [128, 2, 1024], BF16, ph)
                    QT = P.sb("QT", [64, 2, 16, 128], BF16, ph)
                    krf = P.sb("krf", [128, 2, 256], F32, ph)
                    kbf = P.sb("kbf", [128, 2, 256], BF16, ph)
                    KT = P.sb("KT", [64, 5, 4, 128], BF16, ph)
                    vf = P.sb("vf", [128, 2, 256], F32, ph)
                    Vb = P.sb("Vb", [128, 5, 256], BF16, ph)
                    PT = P.sb("PT", [128, 8, 512], BF16, ph)
                    common_loads(S, masks, 0, 3)
                    S.dma("pool", "wqkv", DMA(wqkv[:], w["attn_w_qkv"].rearrange("(k p) n -> p k n", p=128)),
                          w=[("wqkv",)])
                    order = [cfg.halo] + list(range(npt)) + [cfg.samp]
                    NO = len(order)

                    def dst(ti):
                        t = order[ti]
                        b2, slot = ti % 2, ti % 5
                        if t == cfg.samp:
                            return dict(krf=krs[:, :], vf=vfs[:, :], Vb=Vbs[:, :], KT=KTs, QT=QTs,
                                        rk=("krs",), rv=("vfs",), rVb=("Vbs",), rKT=("KTs",), rQT=("QTs",))
                        return dict(krf=krf[:, b2, :], vf=vf[:, b2, :], Vb=Vb[:, slot, :], KT=KT[:, slot], QT=QT[:, b2],
                                    rk=("krf", b2), rv=("vf", b2), rVb=("Vb", slot), rKT=("KT", slot), rQT=("QT", b2))

                    def P1(R, ti):
                        t = order[ti]
                        need_q = (t != cfg.halo)
                        cs = slice(t * 128, (t + 1) * 128)
                        b2 = ti % 2
                        d_ = dst(ti)
                        R.dma("sp", "rope%d" % b2, DMA(rope_sb[:, b2, :], w["rope"][cs, :]), w=[("rope", b2)])
                        c2 = rope_sb[:, b2, 0:64]
                        nlo = rope_sb[:, b2, 64:96]
                        nhi = rope_sb[:, b2, 96:128]
                        R.pe(MM8(psKV[:, :], (lambda k: xT[:, k, cs]), (lambda k: wqkv[:, k, 1024:1536])),
                             r=[("xT", t), ("wqkv",)], w=[("pskv",)])
                        kv = psKV[:, 0:256].rearrange("p (h d) -> p h d", d=64)
                        R.dve(TT(kA[:], kv, c2.unsqueeze(1).broadcast_to([128, 4, 64]), ALU.mult),
                              r=[("rope", b2)], w=[("kA",)], x=[("pskv",)])
                        R.dve(TT(kB[:, :, 0:32], kv[:, :, 32:64], nlo.unsqueeze(1).broadcast_to([128, 4, 32]), ALU.mult),
                              r=[("rope", b2)], w=[("kB", 0)], x=[("pskv",)])
                        R.dve(TT(kB[:, :, 32:64], kv[:, :, 0:32], nhi.unsqueeze(1).broadcast_to([128, 4, 32]), ALU.mult),
                              r=[("rope", b2)], w=[("kB", 1)], x=[("pskv",)])
                        R.act(ACP(d_["vf"], psKV[:, 256:512]), w=[d_["rv"]], x=[("pskv",)])
                        R.pool(TT(d_["krf"].rearrange("p (h d) -> p h d", d=64), kA[:], kB[:], ALU.add),
                               r=[("kA",), ("kB", 0), ("kB", 1)], w=[d_["rk"]])
                        R.act(ACP(d_["Vb"], d_["vf"]), r=[d_["rv"]], w=[d_["rVb"]])
                        R.pool(CP(kbf[:, b2, :], d_["krf"]), r=[d_["rk"]], w=[("kbf", b2)])
                        if need_q:
                            for half in range(2):
                                R.pe(MM8(psQ[half][:, :], (lambda k: xT[:, k, cs]),
                                         (lambda k, half=half: wqkv[:, k, half * 512:(half + 1) * 512])),
                                     r=[("xT", t), ("wqkv",)], w=[("psq", half)])
                                qv = psQ[half][:, :].rearrange("p (h d) -> p h d", d=64)
                                R.dve(TT(ropeA[:], qv, c2.unsqueeze(1).broadcast_to([128, 8, 64]), ALU.mult),
                                      r=[("rope", b2)], w=[("ropeA",)], x=[("psq", half)])
                                R.dve(TT(ropeB[:, :, 0:32], qv[:, :, 32:64], nlo.unsqueeze(1).broadcast_to([128, 8, 32]),
                                         ALU.mult), r=[("rope", b2)], w=[("ropeB", 0)], x=[("psq", half)])
                                R.dve(TT(ropeB[:, :, 32:64], qv[:, :, 0:32], nhi.unsqueeze(1).broadcast_to([128, 8, 32]),
                                         ALU.mult), r=[("rope", b2)], w=[("ropeB", 1)], x=[("psq", half)])
                                R.pool(TT(qrot[:, b2, half * 512:(half + 1) * 512].rearrange("p (h d) -> p h d", d=64),
                                          ropeA[:], ropeB[:], ALU.add),
                                       r=[("ropeA",), ("ropeB", 0), ("ropeB", 1)], w=[("qrot", b2, half)])

                    def P2(R, ti):
                        t = order[ti]
                        need_q = (t != cfg.halo)
                        b2 = ti % 2
                        d_ = dst(ti)
                        R.pe(TRS([psT[0:64, g, :] for g in range(4)],
                                 [kbf[:, b2, g * 64:(g + 1) * 64] for g in range(4)], ident[:]),
                             r=[("kbf", b2), ("ident",)], w=[("psT", 0)])
                        R.act(ACP(d_["KT"][:, :, :], psT[0:64, 0:4, :]), w=[d_["rKT"]], x=[("psT", 0)])
                        if need_q:
                            for half in range(2):
                                R.pe(TRS([psT[0:64, h8, :] for h8 in range(8)],
                                         [qrot[:, b2, (half * 8 + h8) * 64:(half * 8 + h8 + 1) * 64] for h8 in range(8)],
                                         ident[:]), r=[("qrot", b2, half), ("ident",)], w=[("psT", 0)])
                                R.act(ACP(d_["QT"][:, half * 8:(half + 1) * 8, :], psT[0:64, :, :]),
                                      w=[d_["rQT"] + (half,)], x=[("psT", 0)])
                        if t == npt - 1:
                            R.dma("sp", "kwp", DMA(kwp_o[:, :], d_["krf"]), r=[d_["rk"]])
                            R.dma("sp", "vwp", DMA(vwp_o[:, :], d_["vf"]), r=[d_["rv"]])

                    def P3(R, ti):
                        t = order[ti]
                        b2, slot, prev = ti % 2, ti % 5, (ti - 1) % 5
                        u = 0
                        for g in range(4):
                            for kb in range(2):
                                sl_ = prev if kb == 0 else slot
                                midx = (2 if t == 0 else 0) if kb == 0 else 1
                                pb = g * 2 + kb
                                sb_ = u % 2
                                R.pe((lambda sl_=sl_, g=g, sb_=sb_: lambda e: e.matmul(
                                    psS[sb_][:, :], KT[:, sl_, g, :],
                                    QT[:, b2, 4 * g:4 * g + 4, :].rearrange("p h q -> p (h q)"), start=True, stop=True))(),
                                    r=[("KT", sl_), ("QT", b2, 0), ("QT", b2, 1)], w=[("pss", sb_)])
                                R.act(ACTF(e_[:, sb_, :], psS[sb_][:, :], AF.Exp, scale=ATTN_SCALE),
                                      w=[("e", sb_)], x=[("pss", sb_)])
                                R.pool(TT(PT[:, pb, :], e_[:, sb_, :], masks[:, midx, :], ALU.mult),
                                       r=[("e", sb_), ("masks",)], w=[("PT", pb)])
                                u += 1
                        for g in range(4):
                            ob = g % 2

                            def pv(e, g=g, ob=ob):
                                ins = None
                                for j in range(4):
                                    half, cc = j % 2, j // 2
                                    for kind in range(2):
                                        for kb in range(2):
                                            sl_ = prev if kb == 0 else slot
                                            lhs = Vb[:, sl_, g * 64:(g + 1) * 64] if kind == 0 else ones64[:, :]
                                            ins = e.matmul(
                                                psOD[ob][half * 64:(half + 1) * 64,
                                                         kind * 256 + cc * 128:kind * 256 + (cc + 1) * 128],
                                                lhs, PT[:, g * 2 + kb, j * 128:(j + 1) * 128],
                                                start=(kb == 0), stop=(kb == 1))
                                return ins
                            R.pe(pv, r=[("Vb", prev), ("Vb", slot), ("PT", g * 2), ("PT", g * 2 + 1), ("ones",)],
                                 w=[("psod", ob)])
                            normalize(R, g, ob, b2, psOD)

                    def P4(R, ti):
                        outproj_add(R, order[ti], ti % 2, psQ, "psq")

                    for i in range(-2, NO + 2):
                        recs = []
                        for fn_, off in ((P3, 0), (P2, 1), (P4, -1), (P1, 2)):
                            ti = i + off
                            if not (0 <= ti < NO):
                                continue
                            t = order[ti]
                            if fn_ in (P3, P4) and (t == cfg.halo or t == cfg.samp):
                                continue
                            R = Rec()
                            fn_(R, ti)
                            recs.append(R)
                        interleave(S, recs)
                    S.flush()

                with ExitStack() as ph:
                    psY = [P.ps("psY", [128, 512], F32, ph) for _ in range(2)]
                    psT = P.ps("psT", [128, 8, 128], BF16, ph)
                    psSn = P.ps("psSn", [128, 512], F32, ph)
                    psS2 = [P.ps("psS2", [128, 512], F32, ph) for _ in range(2)]
                    psOD = [P.ps("psOD", [128, 512], F32, ph) for _ in range(2)]
                    masks = P.sb("masks", [128, 2, 512], BF16, ph)
                    KcT = P.sb("KcT", [64, NB, 4, 128], BF16, ph)
                    Vc = P.sb("Vc", [128, NB, 256], BF16, ph)
                    kcb = P.sb("kcb", [128, 2, 256], BF16, ph)
                    PT = P.sb("PT", [128, 2, 512], BF16, ph)
                    PTc = P.sb("PTc", [128, 2, 512], BF16, ph)
                    common_loads(S, masks, 3, 2)
                    t = cfg.samp
                    S.dma("pool", "vc", DMA(Vc[:], w["cache_v"].rearrange("b p c -> p b c")), w=[("Vc",)])
                    for b in range(NB):
                        S.dma("pool", "kc%d" % (b % 2), DMA(kcb[:, b % 2, :], w["cache_k"][b]), w=[("kcb", b % 2)])
                        S.pe(TRS([psT[0:64, g, :] for g in range(4)],
                                 [kcb[:, b % 2, g * 64:(g + 1) * 64] for g in range(4)], ident[:]),
                             r=[("kcb", b % 2), ("ident",)], w=[("psT", 0)])
                        S.act(ACP(KcT[:, b, :, :], psT[0:64, 0:4, :]), w=[("KcT", b)], x=[("psT", 0)])
                    S.dma("sp", "kwo", DMA(kws_o[:, 0:120, :], w["cache_k"][:, 8:128, :]))
                    S.dma("sp", "vwo", DMA(vws_o[:, 0:120, :], w["cache_v"][:, 8:128, :]))
                    for b in range(NB):
                        S.dma("sp", "kwn", DMA(kws_o[b, 120:128, :], krs[b * 8:(b + 1) * 8, :]))
                        S.dma("sp", "vwn", DMA(vws_o[b, 120:128, :], vfs[b * 8:(b + 1) * 8, :]))
                    for g in range(4):
                        S.pe((lambda g=g: lambda e: e.matmul(psSn[:, :], KTs[:, g, :],
                                                             QTs[:, 4 * g:4 * g + 4, :].rearrange("p h q -> p (h q)"),
                                                             start=True, stop=True))(),
                             w=[("psn",)])
                        S.act(ACTF(e_[:, 0, :], psSn[:, :], AF.Exp, scale=ATTN_SCALE), w=[("e", 0)], x=[("psn",)])
                        S.pool(TT(PT[:, g % 2, :], e_[:, 0, :], masks[:, 0, :], ALU.mult),
                               r=[("e", 0), ("masks",)], w=[("PT", g % 2)])

                        def sc(e, g=g):
                            ins = None
                            for b in range(NB):
                                ins = e.matmul(psS2[g % 2][:, b * 32:(b + 1) * 32].rearrange("p (h i) -> p h i", h=4),
                                               KcT[:, b, g, :], QTs[:, 4 * g:4 * g + 4, b * 8:(b + 1) * 8],
                                               start=True, stop=True)
                            return ins
                        S.pe(sc, r=[("KcT", b) for b in range(NB)], w=[("ps2", g % 2)])
                        S.act(ACTF(e_[:, 1, :], psS2[g % 2][:, :], AF.Exp, scale=ATTN_SCALE),
                              w=[("e", 1)], x=[("ps2", g % 2)])
                        S.pool(TT(PTc[:, g % 2, :], e_[:, 1, :], masks[:, 1, :], ALU.mult),
                               r=[("e", 1), ("masks",)], w=[("PTc", g % 2)])
                        ob = g % 2

                        def pv2(e, g=g, ob=ob):
                            ins = None
                            for j in range(4):
                                half, cc = j % 2, j // 2
                                for kind in range(2):
                                    reg = psOD[ob][half * 64:(half + 1) * 64,
                                                   kind * 256 + cc * 128:kind * 256 + (cc + 1) * 128]
                                    lhs = Vbs[:, g * 64:(g + 1) * 64] if kind == 0 else ones64[:, :]
                                    ins = e.matmul(reg, lhs, PT[:, g % 2, j * 128:(j + 1) * 128], start=True, stop=False)
                                    for b in range(NB):
                                        lhs = Vc[:, b, g * 64:(g + 1) * 64] if kind == 0 else ones64[:, :]
                                        ins = e.matmul(reg[:, b * 8:(b + 1) * 8], lhs,
                                                       PTc[:, g % 2, b * 32 + j * 8:b * 32 + j * 8 + 8],
                                                       start=False, stop=(b == NB - 1))
                            return ins
                        S.pe(pv2, r=[("Vc",), ("PT", g % 2), ("PTc", g % 2), ("ones",)], w=[("psod", ob)])
                        normalize(S, g, ob, 0, psOD)
                    outproj_add(S, t, 0, psY, "psy")
                    S.flush()

        def xpos_rot(S, src, nqk, xp_t, rA, rB, rR, rtag, srcx):
            v3 = src.rearrange("p (a c) -> p a c", a=nqk)
            v4 = src.rearrange("p (a i two) -> p a i two", a=nqk, two=2)
            b4 = rB[:, 0:nqk, :].rearrange("p a (i two) -> p a i two", two=2)
            S.dve(TT(rA[:, 0:nqk, :], v3, xp_t[:, 0:256].unsqueeze(1).broadcast_to([128, nqk, 256]), ALU.mult),
                  r=[rtag], w=[("rA",)], x=srcx)
            S.dve(TT(b4[:, :, :, 0], v4[:, :, :, 1], xp_t[:, 256:384].unsqueeze(1).broadcast_to([128, nqk, 128]),
                     ALU.mult), r=[rtag], w=[("rB", 0)], x=srcx)
            S.dve(TT(b4[:, :, :, 1], v4[:, :, :, 0], xp_t[:, 384:512].unsqueeze(1).broadcast_to([128, nqk, 128]),
                     ALU.mult), r=[rtag], w=[("rB", 1)], x=srcx)
            S.pool(TT(rR[:, 0:nqk, :], rA[:, 0:nqk, :], rB[:, 0:nqk, :], ALU.add),
                   r=[("rA",), ("rB", 0), ("rB", 1)], w=[("rR",)])

        def sweep1(S):
            with ExitStack() as ph:
                psK = [P.ps("psK", [128, 512], F32, ph) for _ in range(2)]
                psV = [P.ps("psV", [128, 512], F32, ph) for _ in range(2)]
                psF = P.ps("psF", [128, 2, 512], F32, ph)
                wkv = P.sb("wkv", [128, 8, 3072], BF16, ph)
                xp = P.sb("xp", [128, 2, 512], F32, ph)
                z1 = P.sb("z1", [128, npt * 4], F32, ph)
                rA = P.sb("rA", [128, 1, 256], F32, ph)
                rB = P.sb("rB", [128, 1, 256], F32, ph)
                rR = P.sb("rR", [128, 1, 256], F32, ph)
                kz = P.sb("kz", [128, 3, 256], BF16, ph)
                vb = P.sb("vb", [128, 3, 512], BF16, ph)
                Fsb = P.sb("Fsb", [128, 2, 2, 512], F32, ph)
                S.dma("pool", "wk", DMA(wkv[:, :, 0:1024], w["ret_w_in"][:, 1024:2048].rearrange("(k p) n -> p k n", p=128)),
                      w=[("wkv", 0)])
                S.dma("pool", "wv", DMA(wkv[:, :, 1024:3072], w["ret_w_in"][:, 2048:4096].rearrange("(k p) n -> p k n", p=128)),
                      w=[("wkv", 1)])
                S.dma("sp", "z1", DMA(z1[:], w["zeta1"][:, :]), w=[("z1",)])
                units = [(h, t) for h in range(RH) for t in range(npt)]

                def X(R, i):
                    h, t = units[i]
                    cs = slice(t * 128, (t + 1) * 128)
                    b, b3 = i % 2, i % 3
                    R.pe(MM8(psK[b][:, 0:256], (lambda k: xT[:, k, cs]), (lambda k: wkv[:, k, h * 256:(h + 1) * 256])),
                         r=[("xT", t), ("wkv", 0)], w=[("psk", b)])
                    R.dma("sp", "xp%d" % b, DMA(xp[:, b, :], w["xpos"][cs, :]), w=[("xp", b)])
                    R.pe(MM8(psV[b][:, :], (lambda k: xT[:, k, cs]),
                             (lambda k: wkv[:, k, 1024 + h * 512:1024 + (h + 1) * 512])),
                         r=[("xT", t), ("wkv", 1)], w=[("psv", b)])
                    xpos_rot(R, psK[b][:, 0:256], 1, xp[:, b, :], rA, rB, rR, ("xp", b), [("psk", b)])
                    R.act(ACP(vb[:, b3, :], psV[b][:, :]), w=[("vb", b3)], x=[("psv", b)])
                    R.act(AMUL(kz[:, b3, :], rR[:, 0, :], z1[:, t * 4 + h:t * 4 + h + 1]),
                          r=[("rR",), ("z1",)], w=[("kz", b3)])

                def Y(R, i):
                    h, t = units[i]
                    b3 = i % 3

                    def df(e):
                        ins = None
                        for dc in range(2):
                            ins = e.matmul(psF[:, dc, :], kz[:, b3, dc * 128:(dc + 1) * 128], vb[:, b3, :],
                                           start=(t == 0), stop=(t == npt - 1))
                        return ins
                    R.pe(df, r=[("kz", b3), ("vb", b3)], w=[("psf",)])
                    if t == npt - 1:
                        R.act(ACP(Fsb[:, h % 2, :, :], psF[:, :, :]), w=[("Fsb", h % 2)], x=[("psf",)])
                        R.dma("sp", "fo%d" % (h % 2), DMA(F_o[h].rearrange("(dc p) e -> p dc e", p=128), Fsb[:, h % 2, :, :]),
                              r=[("Fsb", h % 2)])

                for i in range(-1, len(units)):
                    recs = []
                    if i >= 0:
                        R = Rec(); Y(R, i); recs.append(R)
                    if i + 1 < len(units):
                        R = Rec(); X(R, i + 1); recs.append(R)
                    interleave(S, recs)
                S.flush()

        def ret_pass(S, h, sample, host_ln=False):
            g128 = math.exp(LG[h] * 128.0)
            g8 = math.exp(LG[h] * 8.0)
            last_head = (h == RH - 1)
            with ExitStack() as ph:
                psA = P.ps("psA", [128, 512], F32, ph)
                psB = P.ps("psB", [128, 512], F32, ph)
                psC = P.ps("psC", [128, 512], F32, ph)
                psT = P.ps("psT", [128, 8, 128], BF16, ph)
                psAT = P.ps("psAT", [128, 512], F32, ph)
                psO = P.ps("psO", [128, 512], F32, ph)
                psdS = P.ps("psdS", [128, 2, 512], F32, ph)
                wih = P.sb("wih", [128, 8, 1536], BF16, ph)
                woh = P.sb("woh", [128, 4, D], BF16, ph)
                xp = P.sb("xp", [128, 2, 512], F32, ph)
                rsc = P.sb("rsc", [128, 24], F32, ph)
                rmask = P.sb("rmask", [128, 128], BF16, ph)
                rA = P.sb("rA", [128, 2, 256], F32, ph)
                rB = P.sb("rB", [128, 2, 256], F32, ph)
                rR = P.sb("rR", [128, 2, 256], F32, ph)
                qk3 = P.sb("qk3", [128, 2, 3, 256], BF16, ph)
                qkT = P.sb("qkT", [128, 2, 4, 128], BF16, ph)
                vb = P.sb("vb", [128, 2, 512], BF16, ph)
                attT = P.sb("attT", [128, 2, 128], BF16, ph)
                on = P.sb("on", [128, 2, 512], F32, ph)
                sgt = P.sb("sgt", [128, 2, 512], F32, ph)
                yb = P.sb("yb", [128, 2, 512], BF16, ph)
                yT = P.sb("yT", [128, 2, 4, 128], BF16, ph)
                lnt = ln_bufs(ph)
                wi = w["ret_w_in"]
                for ci, (c0, n, d0) in enumerate(((h * 256, 256, 0), (1024 + h * 256, 256, 256),
                                                  (2048 + h * 512, 512, 512), (4096 + h * 512, 512, 1024))):
                    S.dma("pool", "wi%d" % ci, DMA(wih[:, :, d0:d0 + n],
                                                  wi[:, c0:c0 + n].rearrange("(k p) n -> p k n", p=128)),
                          w=[("wih", ci)])
                S.dma("pool", "woh", DMA(woh[:], w["ret_w_o"][h * 512:(h + 1) * 512, :].rearrange("(c p) n -> p c n", p=128)),
                      w=[("woh",)])
                S.dma("sp", "rsc", DMA(rsc[:], w["rscal"][:, :]), w=[("rsc",)])
                S.dma("pool", "rmask", DMA(rmask[:], w["masks"][3 if sample else 1, :, 0:128]), w=[("rmask",)])
                rbase = 12 if sample else 0
                xi_ap = rsc[:, rbase + 0 * 4 + h:rbase + 0 * 4 + h + 1]
                kt_ap = rsc[:, rbase + 1 * 4 + h:rbase + 1 * 4 + h + 1]
                kz_ap = rsc[:, rbase + 2 * 4 + h:rbase + 2 * 4 + h + 1]
                if not sample:
                    Sf = P.sb("Sf", [128, 2, 512], F32, ph)
                    Sbf = P.sb("Sbf", [128, 2, 512], BF16, ph)
                    Fin = P.sb("Fin", [128, 2, 2, 512], F32, ph)
                    coef = P.sb("coef", [128, 32], F32, ph)
                    S.dma("sp", "coef", DMA(coef[:], w["coef"][:, :]), w=[("coef",)])
                    for c2 in range(N_CORES):
                        fb = c2 % 2
                        S.dma("sp", "fin%d" % fb, DMA(Fin[:, fb, :, :], w["fall"][c2, h].rearrange("(dc p) e -> p dc e", p=128)),
                              w=[("Fin", fb)])
                        cf = coef[:, c2 * 4 + h:c2 * 4 + h + 1]
                        if c2 == 0:
                            S.dve(TS(Sf[:], Fin[:, fb, :, :], cf, ALU.mult), r=[("Fin", fb), ("coef",)], w=[("Sf",)])
                        else:
                            S.dve(STT(Sf[:], Fin[:, fb, :, :], cf, Sf[:], ALU.mult, ALU.add),
                                  r=[("Fin", fb), ("coef",)], w=[("Sf",)])
                    S.pool(CP(Sbf[:], Sf[:]), r=[("Sf",)], w=[("Sbf",)])
                    tiles = list(range(npt))
                else:
                    SR = 3 if host_ln else 5
                    Sb = P.sb("Sb", [128, SR, 2, 512], F32, ph)
                    if host_ln:
                        gbp = P.sb("gbp", [128, 2, D], F32, ph)
                        xbp = P.sb("xbp", [128, 2, D], BF16, ph)
                        lntp = ln_bufs(ph, ring=4)
                        S.dma("sp", "gbp", DMA(gbp[:, 0, :], w["ln_g"][4:5, :].partition_broadcast(128)), w=[("gbp", 0)])
                        S.dma("sp", "gbp2", DMA(gbp[:, 1, :], w["ln_b"][4:5, :].partition_broadcast(128)), w=[("gbp", 1)])
                    Sbb = P.sb("Sbb", [128, 2, 2, 512], BF16, ph)
                    qTm = P.sb("qTm", [128, 2, 2, 128], BF16, ph)
                    kzm = P.sb("kzm", [128, 2, 256], BF16, ph)
                    ind = P.sb("ind", [128, NB], F32, ph)
                    indT = P.sb("indT", [128, NB, 128], BF16, ph)
                    S.dma("sp", "ind", DMA(ind[:], w["ind"][:, :]), w=[("ind",)])
                    S.dma("pool", "indT", DMA(indT[:], w["indT"].rearrange("p (b q) -> p b q", b=NB)), w=[("indT",)])
                    tiles = [cfg.samp]

                for ti, t in enumerate(tiles):
                    cs = slice(t * 128, (t + 1) * 128)
                    b = ti % 2
                    S.pe(MM8(psA[:, :], (lambda k, cs=cs: xT[:, k, cs]), (lambda k: wih[:, k, 0:512])),
                         r=[("xT", t), ("wih", 0), ("wih", 1)], w=[("psa",)])
                    S.pe(MM8(psB[:, :], (lambda k, cs=cs: xT[:, k, cs]), (lambda k: wih[:, k, 512:1024])),
                         r=[("xT", t), ("wih", 2)], w=[("psb",)])
                    S.pe(MM8(psC[:, :], (lambda k, cs=cs: xT[:, k, cs]), (lambda k: wih[:, k, 1024:1536])),
                         r=[("xT", t), ("wih", 3)], w=[("psc",)])
                    S.dma("sp", "xp%d" % b, DMA(xp[:, b, :], w["xpos"][cs, :]), w=[("xp", b)])
                    xpos_rot(S, psA[:, :], 2, xp[:, b, :], rA, rB, rR, ("xp", b), [("psa",)])
                    S.act(AMUL(qk3[:, b, 0, :], rR[:, 0, :], xi_ap), r=[("rR",), ("rsc",)], w=[("qk3", b, 0)])
                    S.act(AMUL(qk3[:, b, 1, :], rR[:, 1, :], kt_ap), r=[("rR",), ("rsc",)], w=[("qk3", b, 1)])
                    S.act(AMUL(qk3[:, b, 2, :], rR[:, 1, :], kz_ap), r=[("rR",), ("rsc",)], w=[("qk3", b, 2)])
                    S.act(ACP(vb[:, b, :], psB[:, :]), w=[("vb", b)], x=[("psb",)])
                    S.pe(TRS([psT[:, a, :] for a in range(4)],
                             [qk3[:, b, 0, 0:128], qk3[:, b, 0, 128:256], qk3[:, b, 1, 0:128], qk3[:, b, 1, 128:256]],
                             ident[:]), r=[("qk3", b, 0), ("qk3", b, 1), ("ident",)], w=[("psT", 0)])
                    S.dve(CP(qkT[:, b, :, :], psT[:, 0:4, :]), w=[("qkT", b)], x=[("psT", 0)])

                    def at(e, b=b):
                        ins = None
                        for dc in range(2):
                            ins = e.matmul(psAT[:, 0:128], qkT[:, b, 2 + dc, :], qkT[:, b, dc, :],
                                           start=(dc == 0), stop=(dc == 1))
                        return ins
                    S.pe(at, r=[("qkT", b)], w=[("psat",)])
                    S.dve(TT(attT[:, b, :], psAT[:, 0:128], rmask[:, :], ALU.mult), r=[("rmask",)],
                          w=[("attT", b)], x=[("psat",)])
                    if not sample:
                        def om(e, b=b):
                            e.matmul(psO[:, :], attT[:, b, :], vb[:, b, :], start=True, stop=False)
                            ins = None
                            for dc in range(2):
                                ins = e.matmul(psO[:, :], qkT[:, b, dc, :], Sbf[:, dc, :], start=False, stop=(dc == 1))
                            return ins
                        S.pe(om, r=[("attT", b), ("vb", b), ("qkT", b), ("Sbf",)], w=[("pso",)])

                        def ds(e, b=b):
                            ins = None
                            for dc in range(2):
                                ins = e.matmul(psdS[:, dc, :], qk3[:, b, 2, dc * 128:(dc + 1) * 128], vb[:, b, :],
                                               start=True, stop=True)
                            return ins
                        S.pe(ds, r=[("qk3", b, 2), ("vb", b)], w=[("psds",)])
                        S.dve(STT(Sf[:], Sf[:], g128, psdS[:, :, :], ALU.mult, ALU.add), w=[("Sf",)], x=[("psds",)])
                        S.pool(CP(Sbf[:], Sf[:]), r=[("Sf",)], w=[("Sbf",)])
                        if t == npt - 1:
                            S.dma("sp", "stp", DMA(stp_o[h].rearrange("(dc p) e -> p dc e", p=128), Sf[:]), r=[("Sf",)])
                    else:
                        S.pe((lambda b=b: lambda e: e.matmul(psO[:, :], attT[:, b, :], vb[:, b, :],
                                                             start=True, stop=False))(),
                             r=[("attT", b), ("vb", b)], w=[("pso",)])
                        def U(R, sq):
                            sb_, s3 = sq % 2, sq % SR
                            R.dma("sp", "sin%d" % s3, DMA(Sb[:, s3, :, :],
                                                          w["state"][sq, h].rearrange("(dc p) e -> p dc e", p=128)),
                                  w=[("Sb", s3)])
                            R.act(ACP(Sbb[:, sb_, :, :], Sb[:, s3, :, :]), r=[("Sb", s3)], w=[("Sbb", sb_)])
                            R.dve(TT(qTm[:, sb_, :, :], qkT[:, b, 0:2, :],
                                     indT[:, sq, :].unsqueeze(1).broadcast_to([128, 2, 128]), ALU.mult),
                                  r=[("qkT", b), ("indT",)], w=[("qTm", sb_)])
                            R.act(AMUL(kzm[:, sb_, :], qk3[:, b, 2, :], ind[:, sq:sq + 1]),
                                  r=[("qk3", b, 2), ("ind",)], w=[("kzm", sb_)])

                        def V(R, sq):
                            sb_, s3 = sq % 2, sq % SR

                            def om2(e):
                                ins = None
                                for dc in range(2):
                                    ins = e.matmul(psO[:, :], qTm[:, sb_, dc, :], Sbb[:, sb_, dc, :], start=False,
                                                   stop=(sq == NB - 1 and dc == 1))
                                return ins
                            R.pe(om2, r=[("qTm", sb_), ("Sbb", sb_)], w=[("pso",)])
                            if sq % 2 == 0:
                                def ds2(e):
                                    ins = None
                                    for dc in range(2):
                                        ins = e.matmul(psdS[:, dc, :], kzm[:, sb_, dc * 128:(dc + 1) * 128], vb[:, b, :],
                                                       start=True, stop=True)
                                    return ins
                                R.pe(ds2, r=[("kzm", sb_), ("vb", b)], w=[("psds",)])
                                R.dve(STT(Sb[:, s3, :, :], Sb[:, s3, :, :], g8, psdS[:, :, :], ALU.mult, ALU.add),
                                      w=[("Sb", s3)], x=[("psds",)])
                            else:
                                for dc, (bank, tag) in enumerate(((psA, ("psa",)), (psB, ("psb",)))):
                                    R.pe((lambda dc=dc, bank=bank: lambda e: e.matmul(
                                        bank[:, :], kzm[:, sb_, dc * 128:(dc + 1) * 128], vb[:, b, :],
                                        start=True, stop=True))(), r=[("kzm", sb_), ("vb", b)], w=[tag])
                                    R.dve(STT(Sb[:, s3, dc, :], Sb[:, s3, dc, :], g8, bank[:, :], ALU.mult, ALU.add),
                                          w=[("Sb", s3)], x=[tag])
                            R.dma("pool", "sout%d" % s3, DMA(sts_o[sq, h].rearrange("(dc p) e -> p dc e", p=128),
                                                             Sb[:, s3, :, :]), r=[("Sb", s3)])

                        def ln_steps(s):
                            out = []
                            if not host_ln:
                                return out
                            if 0 <= s < npt:
                                R = Rec(); ln_a1(R, s, s, lntp); out.append(R)
                            if 0 <= s - 1 < npt:
                                R = Rec(); ln_a2(R, s - 1, s - 1, gbp, lntp, "gbp"); out.append(R)
                            if 0 <= s - 2 < npt:
                                R = Rec(); ln_b(R, s - 2, s - 2, xbp, [psT]); out.append(R)
                            return out

                        for i in range(-1, max(NB, npt + 2)):
                            recs = []
                            if 0 <= i < NB:
                                R = Rec(); V(R, i); recs.append(R)
                            if i + 1 < NB:
                                R = Rec(); U(R, i + 1); recs.append(R)
                            recs += ln_steps(i + 1)
                            interleave(S, recs)
                    j = ti % 3
                    sc = lnt["sc"]
                    norm_ops(S, j, lnt, psO[:, :], None, ([], [("pso",)]), None)
                    S.act(ACTF(on[:, b, :], psO[:, :], AF.Identity, bias=sc[:, j, 2:3], scale=sc[:, j, 1:2]),
                          r=[("sc", j, 1), ("sc", j, 2)], w=[("on", b)], x=[("pso",)])
                    S.act(ACTF(sgt[:, b, :], psC[:, :], AF.Silu), w=[("sgt", b)], x=[("psc",)])
                    S.pool(TT(yb[:, b, :], on[:, b, :], sgt[:, b, :], ALU.mult), r=[("on", b), ("sgt", b)], w=[("yb", b)])
                    S.pe(TRS([psT[:, 4 + c, :] for c in range(4)], [yb[:, b, c * 128:(c + 1) * 128] for c in range(4)],
                             ident[:]), r=[("yb", b), ("ident",)], w=[("psT", 0)])
                    S.dve(CP(yT[:, b, :, :], psT[:, 4:8, :]), w=[("yT", b)], x=[("psT", 0)])

                    def op(e, b=b):
                        ins = None
                        for dh, bank in enumerate((psA, psB)):
                            for c in range(4):
                                ins = e.matmul(bank[:, :], yT[:, b, c, :], woh[:, c, dh * 512:(dh + 1) * 512],
                                               start=(c == 0), stop=(c == 3))
                        return ins
                    S.pe(op, r=[("yT", b), ("woh",)], w=[("psa",), ("psb",)])
                    S.dve(TT(x[:, t, 0:512], psA[:, :], x[:, t, 0:512], ALU.add), w=[("x", t, 0)], x=[("psa",)])
                    S.dve(TT(x[:, t, 512:1024], psB[:, :], x[:, t, 512:1024], ALU.add), w=[("x", t, 1)], x=[("psb",)])
                S.flush()

        def ret_prompt_all(S):
            with ExitStack() as ph:
                psQK = P.ps("psQK", [128, 512], F32, ph)
                psVG = P.ps("psVG", [128, 512], F32, ph)
                psT = P.ps("psT", [128, 8, 128], BF16, ph)
                psAT = P.ps("psAT", [128, 512], F32, ph)
                psO = P.ps("psO", [128, 512], F32, ph)
                psdS = P.ps("psdS", [128, 2, 512], F32, ph)
                psY = P.ps("psY", [128, 512], F32, ph)
                wih = P.sb("wih", [128, 8, 1536], BF16, ph)
                woh = P.sb("woh", [128, 4, D], BF16, ph)
                xp = P.sb("xp", [128, 2, 1024], F32, ph)
                rmask = P.sb("rmask", [128, 128], BF16, ph)
                rA = P.sb("rA", [128, 2, 256], F32, ph)
                rB = P.sb("rB", [128, 2, 256], F32, ph)
                qk2 = P.sb("qk2", [128, 4, 2, 256], BF16, ph)
                qkT = P.sb("qkT", [128, 3, 4, 128], BF16, ph)
                vb = P.sb("vb", [128, 4, 512], BF16, ph)
                attT = P.sb("attT", [128, 3, 128], BF16, ph)
                sgt = P.sb("sgt", [128, 4, 512], F32, ph)
                osb = P.sb("osb", [128, 3, 512], F32, ph)
                yb = P.sb("yb", [128, 2, 512], BF16, ph)
                yT = P.sb("yT", [128, 2, 4, 128], BF16, ph)
                Uf = P.sb("Uf", [128, 2, 2, 512], F32, ph)
                Sbf = P.sb("Sbf", [128, 2, 2, 512], BF16, ph)
                Sout = P.sb("Sout", [128, 2, 512], F32, ph)
                Fin = P.sb("Fin", [128, 2, 2, 512], F32, ph)
                coef = P.sb("coef", [128, 32], F32, ph)
                lnt = ln_bufs(ph)
                wi = w["ret_w_in"]
                G128 = [math.exp(LG[h] * 128.0) for h in range(RH)]

                def load_wih(S, h):
                    for ci, (c0, n, d0) in enumerate(((h * 256, 256, 0), (1024 + h * 256, 256, 256),
                                                      (2048 + h * 512, 512, 512), (4096 + h * 512, 512, 1024))):
                        S.dma("pool", "wi%d" % ci, DMA(wih[:, :, d0:d0 + n],
                                                      wi[:, c0:c0 + n].rearrange("(k p) n -> p k n", p=128)),
                              w=[("wih", ci)])

                def load_woh(S, h):
                    S.dma("pool", "woh", DMA(woh[:], w["ret_w_o"][h * 512:(h + 1) * 512, :].rearrange("(c p) n -> p c n", p=128)),
                          w=[("woh",)])

                def s_init_parts(h):
                    hp = h % 2
                    parts = []

                    def ld(R, c2):
                        fb = c2 % 2
                        R.dma("sp", "fin%d" % fb, DMA(Fin[:, fb], w["fall"][c2, h].rearrange("(dc p) e -> p dc e", p=128)),
                              w=[("Fin", fb)])
                    R = Rec(); ld(R, 0); ld(R, 1); parts.append(R)
                    for c2 in range(N_CORES):
                        R = Rec()
                        fb = c2 % 2
                        cf = coef[:, c2 * 4 + h:c2 * 4 + h + 1]
                        if c2 == 0:
                            R.dve(TS(Uf[:, hp], Fin[:, fb], cf, ALU.mult), r=[("Fin", fb), ("coef",)], w=[("Uf", hp)])
                        else:
                            R.dve(STT(Uf[:, hp], Fin[:, fb], cf, Uf[:, hp], ALU.mult, ALU.add),
                                  r=[("Fin", fb), ("coef",)], w=[("Uf", hp)])
                        if c2 + 2 < N_CORES:
                            ld(R, c2 + 2)
                        parts.append(R)
                    R = Rec()
                    R.act(ACP(Sbf[:, hp], Uf[:, hp]), r=[("Uf", hp)], w=[("Sbf", hp)])
                    R.act(AMUL(Uf[:, hp], Uf[:, hp], 1.0 / G128[h]), r=[("Sbf", hp)], w=[("Uf", hp)])
                    parts.append(R)
                    return parts

                S.dma("pool", "rmask", DMA(rmask[:], w["masks"][1, :, 0:128]), w=[("rmask",)])
                S.dma("sp", "coef", DMA(coef[:], w["coef"][:, :]), w=[("coef",)])
                load_wih(S, 0)
                load_woh(S, 0)
                interleave(S, s_init_parts(0)[0:1])
                for R_ in s_init_parts(0)[1:]:
                    interleave(S, [R_])

                def stA1(R, u):
                    h, t = divmod(u, npt)
                    cs = slice(t * 128, (t + 1) * 128)
                    b2, b4 = u % 2, u % 4
                    R.pe(MM8(psQK[:, :], (lambda k: xT[:, k, cs]), (lambda k: wih[:, k, 0:512])),
                         r=[("xT", t), ("wih", 0), ("wih", 1)], w=[("psqk",)])
                    R.dma("sp", "xp%d" % b2, DMA(xp[:, b2, :], w["xposh"][h, cs, :]), w=[("xp", b2)])
                    tab = xp[:, b2, :]
                    v3 = psQK[:, :].rearrange("p (a c) -> p a c", a=2)
                    v4 = psQK[:, :].rearrange("p (a i two) -> p a i two", a=2, two=2)
                    bb4 = rB[:, :, :].rearrange("p a (i two) -> p a i two", two=2)
                    R.dve(TT(rA[:, :, :], v3, tab[:, 0:512].rearrange("p (a c) -> p a c", a=2), ALU.mult),
                          r=[("xp", b2)], w=[("rA",)], x=[("psqk",)])
                    R.pe(MM8(psVG[:, :], (lambda k: xT[:, k, cs]), (lambda k: wih[:, k, 512:1024])),
                         r=[("xT", t), ("wih", 2)], w=[("psvg",)])
                    R.dve(TT(bb4[:, :, :, 0], v4[:, :, :, 1], tab[:, 512:768].rearrange("p (a c) -> p a c", a=2), ALU.mult),
                          r=[("xp", b2)], w=[("rB", 0)], x=[("psqk",)])
                    R.act(ACP(vb[:, b4, :], psVG[:, :]), w=[("vb", b4)], x=[("psvg",)])
                    R.dve(TT(bb4[:, :, :, 1], v4[:, :, :, 0], tab[:, 768:1024].rearrange("p (a c) -> p a c", a=2), ALU.mult),
                          r=[("xp", b2)], w=[("rB", 1)], x=[("psqk",)])
                    R.pe(MM8(psVG[:, :], (lambda k: xT[:, k, cs]), (lambda k: wih[:, k, 1024:1536])),
                         r=[("xT", t), ("wih", 3)], w=[("psvg",)])
                    R.pool(TT(qk2[:, b4, :, :], rA[:, :, :], rB[:, :, :], ALU.add),
                           r=[("rA",), ("rB", 0), ("rB", 1)], w=[("qk2", b4)])
                    R.act(ACTF(sgt[:, b4, :], psVG[:, :], AF.Silu), w=[("sgt", b4)], x=[("psvg",)])

                def stA2(R, u):
                    b3, b4 = u % 3, u % 4
                    R.pe(TRS([psT[:, a, :] for a in range(4)],
                             [qk2[:, b4, 0, 0:128], qk2[:, b4, 0, 128:256], qk2[:, b4, 1, 0:128], qk2[:, b4, 1, 128:256]],
                             ident[:]), r=[("qk2", b4), ("ident",)], w=[("psT", 0)])
                    R.dve(CP(qkT[:, b3, :, :], psT[:, 0:4, :]), w=[("qkT", b3)], x=[("psT", 0)])

                    def at(e):
                        ins = None
                        for dc in range(2):
                            ins = e.matmul(psAT[:, 0:128], qkT[:, b3, 2 + dc, :], qkT[:, b3, dc, :],
                                           start=(dc == 0), stop=(dc == 1))
                        return ins
                    R.pe(at, r=[("qkT", b3)], w=[("psat",)])
                    R.dve(TT(attT[:, b3, :], psAT[:, 0:128], rmask[:, :], ALU.mult), r=[("rmask",)],
                          w=[("attT", b3)], x=[("psat",)])

                def stB(R, u):
                    h, t = divmod(u, npt)
                    hp = h % 2
                    g128 = G128[h]
                    b3, b4 = u % 3, u % 4

                    def om(e):
                        e.matmul(psO[:, :], attT[:, b3, :], vb[:, b4, :], start=True, stop=False)
                        ins = None
                        for dc in range(2):
                            ins = e.matmul(psO[:, :], qkT[:, b3, dc, :], Sbf[:, hp, dc, :], start=False, stop=(dc == 1))
                        return ins
                    R.pe(om, r=[("attT", b3), ("vb", b4), ("qkT", b3), ("Sbf", hp)], w=[("pso",)])

                    def ds(e):
                        ins = None
                        for dc in range(2):
                            ins = e.matmul(psdS[:, dc, :], qk2[:, b4, 1, dc * 128:(dc + 1) * 128], vb[:, b4, :],
                                           start=True, stop=True)
                        return ins
                    R.pe(ds, r=[("qk2", b4), ("vb", b4)], w=[("psds",)])
                    R.dve(STT(Uf[:, hp], Uf[:, hp], g128, psdS[:, :, :], ALU.mult, ALU.add), w=[("Uf", hp)], x=[("psds",)])
                    R.act(ACP(osb[:, b3, :], psO[:, :]), w=[("osb", b3)], x=[("pso",)])
                    R.act(AMUL(Sbf[:, hp], Uf[:, hp], g128), r=[("Uf", hp)], w=[("Sbf", hp)])
                    if t == npt - 1:
                        R.act(AMUL(Sout[:], Uf[:, hp], g128), r=[("Uf", hp)], w=[("Sout",)])
                        R.dma("sp", "stp", DMA(stp_o[h].rearrange("(dc p) e -> p dc e", p=128), Sout[:]), r=[("Sout",)])

                def stC1(R, u):
                    b2, b3, b4 = u % 2, u % 3, u % 4
                    j = u % 3
                    st, mv, sc = lnt["stats"], lnt["mv"], lnt["sc"]
                    R.dve(lambda e: e.bn_stats(out=st[:, j, 0, :], in_=osb[:, b3, :]), r=[("osb", b3)], w=[("st", j, 0)])
                    R.dve(lambda e: e.bn_aggr(out=mv[:, j, :], in_=st[:, j, 0, :]), r=[("st", j, 0)], w=[("mv", j)])
                    R.act(ACTF(sc[:, j, 0:1], mv[:, j, 1:2], AF.Sqrt, bias=lnt["eps"][:, 0:1]),
                          r=[("mv", j), ("eps",)], w=[("sc", j, 0)])
                    R.dve(lambda e: e.reciprocal(out=sc[:, j, 1:2], in_=sc[:, j, 0:1]), r=[("sc", j, 0)], w=[("sc", j, 1)])
                    R.dve(lambda e: e.tensor_scalar(out=osb[:, b3, :], in0=osb[:, b3, :], scalar1=mv[:, j, 0:1],
                                                    scalar2=sc[:, j, 1:2], op0=ALU.subtract, op1=ALU.mult),
                          r=[("mv", j), ("sc", j, 1)], w=[("osb", b3)])
                    R.pool(TT(yb[:, b2, :], osb[:, b3, :], sgt[:, b4, :], ALU.mult),
                           r=[("osb", b3), ("sgt", b4)], w=[("yb", b2)])

                def stC2(R, u):
                    h, t = divmod(u, npt)
                    b2 = u % 2
                    R.pe(TRS([psT[:, 4 + c, :] for c in range(4)], [yb[:, b2, c * 128:(c + 1) * 128] for c in range(4)],
                             ident[:]), r=[("yb", b2), ("ident",)], w=[("psT", 0)])
                    R.dve(CP(yT[:, b2, :, :], psT[:, 4:8, :]), w=[("yT", b2)], x=[("psT", 0)])
                    for dh in range(2):
                        def op(e, dh=dh):
                            ins = None
                            for c in range(4):
                                ins = e.matmul(psY[:, :], yT[:, b2, c, :], woh[:, c, dh * 512:(dh + 1) * 512],
                                               start=(c == 0), stop=(c == 3))
                            return ins
                        R.pe(op, r=[("yT", b2), ("woh",)], w=[("psy",)])
                        R.dve(TT(x[:, t, dh * 512:(dh + 1) * 512], psY[:, :], x[:, t, dh * 512:(dh + 1) * 512], ALU.add),
                              w=[("x", t, dh)], x=[("psy",)])

                NU = RH * npt
                stages = [(stB, 0), (stA2, 1), (stC1, -1), (stA1, 2), (stC2, -2)]
                init_at = {}
                for h in range(1, RH):
                    for k, R_ in enumerate(s_init_parts(h)):
                        init_at[h * npt - 12 + k] = R_
                for i in range(-2, NU + 3):
                    recs = []
                    if i in init_at:
                        recs.append(init_at[i])
                    for fn_, off in stages:
                        u = i + off
                        if not (0 <= u < NU):
                            continue
                        h, t = divmod(u, npt)
                        if t == 0 and h > 0:
                            if fn_ is stA1:
                                load_wih(S, h)
                            if fn_ is stC2:
                                load_woh(S, h)
                        R = Rec()
                        fn_(R, u)
                        recs.append(R)
                    interleave(S, recs)
                S.flush()

        stop = cfg.stop_after
        if stage == 1:
            ffn_phase(S, w["ffn1_wg"][0], w["ffn1_wu"][0], w["ffn1_wd"][0], 0, True)
            if stop == "ffn1":
                dump_x("dbg_x")
                return nc
            attn_phases(S)
            if stop == "attn":
                dump_x("dbg_x")
                return nc
            ffn_phase(S, w["ffn2_wg"][0], w["ffn2_wu"][0], w["ffn2_wd"][0], 2, False, pre_ln=1)
            ffn_phase(S, w["ffn1_wg"][1], w["ffn1_wu"][1], w["ffn1_wd"][1], 3, False)
            sweep1(S)
            for t in range(npt + 1):
                S.dma("sp", "tile%d" % t, DMA(xmid_o[t * 128:(t + 1) * 128, :], x[:, t, :]),
                      r=[("x", t, 0), ("x", t, 1)])
            S.flush()
        else:
            ret_prompt_all(S)
            for h in range(RH):
                ret_pass(S, h, True, host_ln=(h == 0))
            if stop == "ret":
                dump_x("dbg_x")
                return nc
            ffn_phase(S, w["ffn2_wg"][1], w["ffn2_wu"][1], w["ffn2_wd"][1], 5, False, final_out=yout, pre_ln=4,
                      pre_tiles=[cfg.samp])
    return nc


def _tables(c, cfg):
    npt, NT = cfg.npt, cfg.nt
    ntl = npt * 128
    r = np.arange(128)
    pos = np.zeros(NT * 128, np.float32)
    pos[0:ntl] = c * ntl + np.arange(ntl)
    pos[cfg.samp * 128:(cfg.samp + 1) * 128] = PAST + (r % 8)
    pos[cfg.halo * 128:(cfg.halo + 1) * 128] = (c * ntl - 128 + r) if c > 0 else r
    pos = pos.astype(np.float32)
    inv = (np.float32(10000.0) ** (-np.arange(0, HD, 2, dtype=np.float32) / np.float32(HD))).astype(np.float32)
    ang = (pos[:, None] * inv[None, :]).astype(np.float32)
    cs, sn = np.cos(ang).astype(np.float32), np.sin(ang).astype(np.float32)
    rope = np.concatenate([cs, cs, -sn, sn], axis=1).astype(np.float32)
    inv2 = (np.float32(1.0) / (np.float32(10000.0) ** np.linspace(0.0, 1.0, RQK // 2, dtype=np.float32))).astype(np.float32)
    ang2 = (pos[:, None] * inv2[None, :]).astype(np.float32)
    c2, s2 = np.cos(ang2).astype(np.float32), np.sin(ang2).astype(np.float32)
    xpos = np.concatenate([np.repeat(c2, 2, axis=1), -s2, s2], axis=1).astype(np.float32)
    k = r[:, None]
    q = r[None, :]
    m0 = (k > q).astype(np.float32)
    m1 = (k <= q).astype(np.float32)
    m2 = np.zeros_like(m0) if c == 0 else m0
    m3 = ((k // 8 == q // 8) & (k % 8 <= q % 8)).astype(np.float32)
    col = np.arange(512)[None, :]
    m4 = (k > (col % 8)).astype(np.float32)
    masks = np.stack([np.tile(m0, (1, 4)), np.tile(m1, (1, 4)), np.tile(m2, (1, 4)), np.tile(m3, (1, 4)), m4]).astype(np.float32)
    gam = np.array(GAM, np.float64)
    sc = RQK ** -0.5
    zeta1 = np.zeros((128, npt * 4), np.float64)
    for t in range(npt):
        for h in range(RH):
            zeta1[:, t * 4 + h] = sc * gam[h] ** (ntl - 1 - (t * 128 + r))
    rscal = np.zeros((128, 24), np.float64)
    for h in range(RH):
        rscal[:, 0 + h] = gam[h] ** (r + 1.0)
        rscal[:, 4 + h] = sc * gam[h] ** (-(r + 1.0))
        rscal[:, 8 + h] = sc * gam[h] ** (127.0 - r)
        i8 = (r % 8).astype(np.float64)
        rscal[:, 12 + h] = gam[h] ** (i8 + 1.0)
        rscal[:, 16 + h] = sc * gam[h] ** (-(i8 + 1.0))
        rscal[:, 20 + h] = sc * gam[h] ** (7.0 - i8)
    coef = np.zeros((128, 32), np.float64)
    for c2_ in range(N_CORES):
        for h in range(RH):
            if c2_ < c:
                coef[:, c2_ * 4 + h] = gam[h] ** (float(ntl) * (c - 1 - c2_))
    c2d = np.repeat(c2[0:ntl].astype(np.float64), 2, axis=1)
    s2d = s2[0:ntl].astype(np.float64)
    pl = (np.arange(ntl) % 128).astype(np.float64)
    xposh = np.zeros((RH, ntl, 1024), np.float32)
    for h in range(RH):
        xi = (gam[h] ** (pl + 1.0))[:, None]
        kt = (sc * gam[h] ** (-(pl + 1.0)))[:, None]
        xposh[h] = np.concatenate([c2d * xi, c2d * kt, -s2d * xi, -s2d * kt, s2d * xi, s2d * kt], axis=1)
    ind = (r[:, None] // 8 == np.arange(16)[None, :]).astype(np.float32)
    indT = np.tile((np.arange(16)[:, None] == (r[None, :] // 8)).astype(np.float32).reshape(1, 16 * 128), (128, 1))
    return dict(rope=rope, xpos=xpos, masks=masks, zeta1=zeta1.astype(np.float32), rscal=rscal.astype(np.float32),
                coef=coef.astype(np.float32), ind=ind, indT=indT.astype(np.float32), xposh=xposh,
                identf=np.eye(128, dtype=np.float32))


_PROGS = {}


def _prog(stage):
    if stage not in _PROGS:
        _PROGS[stage] = build_program(Cfg(), stage)
    return _PROGS[stage]


def kernel(x_prompt, x_sample, cache_k_win, cache_v_win, state_ret,
           ffn1_w_gate, ffn1_w_up, ffn1_w_down, ffn2_w_gate, ffn2_w_up, ffn2_w_down,
           ln_g, ln_b, attn_w_qkv, attn_w_o, attn_sinks, ret_w_in, ret_w_o):
    cfg = Cfg()
    f32 = lambda a: np.ascontiguousarray(np.asarray(a, dtype=np.float32))
    xp = f32(x_prompt)[0]
    xs = f32(x_sample)
    ck = f32(cache_k_win)[0].reshape(128, 128, 256)
    cv = f32(cache_v_win)[0].reshape(128, 128, 256)
    st = f32(state_ret)[0]
    lng = f32(ln_g).reshape(6, D)
    lnb = f32(ln_b).reshape(6, D)
    wts1 = {"ffn1_wg": f32(ffn1_w_gate), "ffn1_wu": f32(ffn1_w_up), "ffn1_wd": f32(ffn1_w_down),
            "ffn2_wg": f32(ffn2_w_gate), "ffn2_wu": f32(ffn2_w_up), "ffn2_wd": f32(ffn2_w_down),
            "attn_w_qkv": f32(attn_w_qkv)[0], "attn_w_o": f32(attn_w_o)[0],
            "sinks2": np.ascontiguousarray(f32(attn_sinks)[0].reshape(8, 2).T),
            "ret_w_in": f32(ret_w_in)[0], "ln_g": lng, "ln_b": lnb}
    tabs = [_tables(c, cfg) for c in range(N_CORES)]
    ntl = cfg.npt * 128
    in1 = []
    for c in range(N_CORES):
        halo = xp[c * ntl - 128:c * ntl] if c > 0 else xp[0:128]
        xin = np.concatenate([xp[c * ntl:(c + 1) * ntl], xs[c * 16:(c + 1) * 16].reshape(128, D), halo], axis=0)
        m = dict(wts1)
        m.update(xin=np.ascontiguousarray(xin), cache_k=ck[c * 16:(c + 1) * 16], cache_v=cv[c * 16:(c + 1) * 16])
        for k_ in ("identf", "masks", "rope", "xpos", "zeta1"):
            m[k_] = tabs[c][k_]
        in1.append(m)
    r1 = run_bass_kernel_spmd(_prog(1), in1, core_ids=list(range(N_CORES))).results
    fall = np.ascontiguousarray(np.stack([r1[c]["F"] for c in range(N_CORES)], axis=0))
    wts2 = {"ffn2_wg": wts1["ffn2_wg"], "ffn2_wu": wts1["ffn2_wu"], "ffn2_wd": wts1["ffn2_wd"],
            "ret_w_in": wts1["ret_w_in"], "ret_w_o": f32(ret_w_o)[0], "ln_g": lng, "ln_b": lnb, "fall": fall}
    in2 = []
    for c in range(N_CORES):
        m = dict(wts2)
        m.update(xmid=r1[c]["xmid"], state=st[c * 16:(c + 1) * 16])
        for k_ in ("identf", "masks", "xpos", "rscal", "coef", "ind", "indT", "xposh"):
            m[k_] = tabs[c][k_]
        in2.append(m)
    r2 = run_bass_kernel_spmd(_prog(2), in2, core_ids=list(range(N_CORES))).results
    y_prompt = np.concatenate([r2[c]["y"][0:ntl] for c in range(N_CORES)], axis=0)[None]
    y_sample = np.concatenate([r2[c]["y"][ntl:ntl + 128].reshape(16, 8, D) for c in range(N_CORES)], axis=0)
    L = N_CORES - 1
    kwp = r1[L]["kwin_p"].reshape(1, 1, 128, NKV, HD)
    vwp = r1[L]["vwin_p"].reshape(1, 1, 128, NKV, HD)
    kws = np.concatenate([r1[c]["kwin_s"] for c in range(N_CORES)], axis=0).reshape(1, 128, 128, NKV, HD)
    vws = np.concatenate([r1[c]["vwin_s"] for c in range(N_CORES)], axis=0).reshape(1, 128, 128, NKV, HD)
    stp = r2[L]["state_p"].reshape(1, 1, RH, RQK, RV)
    sts = np.concatenate([r2[c]["state_s"] for c in range(N_CORES)], axis=0).reshape(1, 128, RH, RQK, RV)
    return (y_prompt.astype(np.float32), y_sample.astype(np.float32), kwp, vwp, kws, vws, stp, sts)
```

```python
import math
from contextlib import ExitStack

import numpy as np
import concourse.bass as bass
import concourse.mybir as mybir
from concourse.bass_utils import run_bass_kernel_spmd

F32 = mybir.dt.float32
BF16 = mybir.dt.bfloat16
AF = mybir.ActivationFunctionType
ALU = mybir.AluOpType
AX = mybir.AxisListType

D = 1024
FF = 2816
NFC = FF // 128
DEPTH = 2
ALPHA = (2.0 * DEPTH) ** 0.25
LN_EPS = 1e-5
N_CORES = 8
HD = 64
NH = 16
NKV = 4
WINDOW = 128
PAST = 16384
RH = 4
RQK = 256
RV = 512


class Op:
    __slots__ = ("eng", "fn", "deps", "dma", "token", "needs_inc", "idx", "xset")


class Sched:
    ENGS = ("pe", "act", "dve", "pool", "sp")

    def __init__(self, nc, stack, n_dma_sems=100):
        self.nc = nc
        self.sem = {e: stack.enter_context(nc.semaphore("s_" + e)) for e in self.ENGS}
        self.cnt = {e: 0 for e in self.ENGS}
        self.stack = stack
        self.dma_sem = {}
        self.dma_cnt = {}
        self.waited = {e: {} for e in self.ENGS}
        self.ops = []
        self.last_w = {}
        self.readers = {}
        self.all_dma_tokens = {}

    def _sem_for(self, key):
        if key not in self.dma_sem:
            self.dma_sem[key] = self.stack.enter_context(self.nc.semaphore("dq_" + key))
            self.dma_cnt[key] = 0
        return self.dma_sem[key]

    def add(self, eng, fn, r=(), w=(), dma=None, x=()):
        op = Op()
        op.eng, op.fn, op.dma, op.needs_inc, op.token = eng, fn, dma, False, None
        op.idx = len(self.ops)
        op.xset = frozenset(x)
        deps = set()
        for res in r:
            lw = self.last_w.get(res)
            if lw is not None:
                deps.add(lw)
        for res in w:
            lw = self.last_w.get(res)
            if lw is not None:
                deps.add(lw)
            for rd in self.readers.get(res, ()):
                deps.add(rd)
        for res in x:
            lw = self.last_w.get(res)
            if lw is not None and not (lw.eng == eng and res in lw.xset):
                deps.add(lw)
            for rd in self.readers.get(res, ()):
                deps.add(rd)
        deps.discard(op)
        op.deps = deps
        for res in r:
            self.readers.setdefault(res, []).append(op)
        for res in list(w) + list(x):
            self.last_w[res] = op
            self.readers[res] = []
        if dma is not None:
            self._sem_for(dma)
        self.ops.append(op)
        return op

    def pe(self, fn, r=(), w=(), x=()):
        return self.add("pe", fn, r, w, x=x)

    def act(self, fn, r=(), w=(), x=()):
        return self.add("act", fn, r, w, x=x)

    def dve(self, fn, r=(), w=(), x=()):
        return self.add("dve", fn, r, w, x=x)

    def pool(self, fn, r=(), w=(), x=()):
        return self.add("pool", fn, r, w, x=x)

    def dma(self, q, key, fn, r=(), w=()):
        return self.add(q, fn, r, w, dma=key)

    def flush(self, final_wait_all_dma=True):
        ops = self.ops
        if not ops:
            return
        for op in ops:
            for d in op.deps:
                if d.dma is None:
                    if d.eng == "pe" and op.eng == "pe":
                        continue
                    d.needs_inc = True
        for op in ops:
            if op.dma is not None:
                self.dma_cnt[op.dma] += 16
                op.token = (self.dma_sem[op.dma], self.dma_cnt[op.dma])
                self.all_dma_tokens[op.dma] = op.token
            elif op.needs_inc:
                self.cnt[op.eng] += 1
                op.token = (self.sem[op.eng], self.cnt[op.eng])
        plan = {e: [] for e in self.ENGS}
        for op in ops:
            need = {}
            for d in op.deps:
                if d.dma is None and d.eng == "pe" and op.eng == "pe":
                    continue
                s, v = d.token
                k = id(s)
                if k not in need or need[k][1] < v:
                    need[k] = (s, v)
            waits = []
            wd = self.waited[op.eng]
            for k, (s, v) in need.items():
                if wd.get(k, 0) >= v:
                    continue
                wd[k] = v
                waits.append((s, v))
            plan[op.eng].append((op, waits))
        if final_wait_all_dma:
            fin = []
            wd = self.waited["sp"]
            for key, (s, v) in self.all_dma_tokens.items():
                if wd.get(id(s), 0) >= v:
                    continue
                wd[id(s)] = v
                fin.append((s, v))
        else:
            fin = []

        def emit(engname):
            lst = plan[engname]

            def body(e):
                for op, waits in lst:
                    for s, v in waits:
                        e.wait_ge(s, v)
                    ins = op.fn(e)
                    if op.token is not None:
                        if op.dma is not None:
                            ins.then_inc(op.token[0], 16)
                        else:
                            ins.then_inc(op.token[0], 1)
                if engname == "sp":
                    for s, v in fin:
                        e.wait_ge(s, v)
            return body

        with self.nc.Block() as blk:
            if plan["pe"]:
                blk.tensor(emit("pe"))
            if plan["act"]:
                blk.scalar(emit("act"))
            if plan["dve"]:
                blk.vector(emit("dve"))
            if plan["pool"]:
                blk.gpsimd(emit("pool"))
            if plan["sp"] or fin:
                blk.sync(emit("sp"))
        self.ops = []
        self.last_w = {}
        self.readers = {}


ATTN_SCALE = 1.0 / math.sqrt(HD)
GAM = [1.0 - 2.0 ** (-5.0 - h) for h in range(RH)]
LG = [math.log(g) for g in GAM]


class Cfg:
    def __init__(self, npt=16, n_cores=N_CORES, G=6, stop_after=None):
        self.npt = npt
        self.samp = npt
        self.halo = npt + 1
        self.nt = npt + 2
        self.n_cores = n_cores
        self.G = G
        self.stop_after = stop_after

    def groups(self, with_halo):
        g = [(i, min(4, self.npt - i)) for i in range(0, self.npt, 4)]
        g.append((self.samp, 2 if with_halo else 1))
        return g


class Prog:
    def __init__(self, cfg):
        self.cfg = cfg
        self.nc = bass.Bass("TRN2", target_bir_lowering=False)
        self.stack = ExitStack()
        self.uid = 0

    def dram_in(self, name, shape, dt=F32):
        return self.nc.dram_tensor(name, list(shape), dt, kind="ExternalInput").ap()

    def dram_out(self, name, shape, dt=F32):
        return self.nc.dram_tensor(name, list(shape), dt, kind="ExternalOutput").ap()

    def sb(self, name, shape, dt, stack=None):
        self.uid += 1
        return (stack or self.stack).enter_context(
            self.nc.sbuf_tensor("%s_%d" % (name, self.uid), list(shape), dt))

    def ps(self, name, shape, dt, stack=None):
        self.uid += 1
        return (stack or self.stack).enter_context(
            self.nc.psum_tensor("%s_%d" % (name, self.uid), list(shape), dt))


def TT(out, in0, in1, op):
    return lambda e: e.tensor_tensor(out=out, in0=in0, in1=in1, op=op)


def STT(out, in0, scalar, in1, op0, op1):
    return lambda e: e.scalar_tensor_tensor(out=out, in0=in0, scalar=scalar, in1=in1, op0=op0, op1=op1)


def TS(out, in0, s1, op0):
    return lambda e: e.tensor_scalar(out=out, in0=in0, scalar1=s1, scalar2=None, op0=op0)


def CP(out, in_):
    return lambda e: e.tensor_copy(out=out, in_=in_)


def ACP(out, in_):
    return lambda e: e.copy(out=out, in_=in_)


def AMUL(out, in_, m):
    return lambda e: e.mul(out=out, in_=in_, mul=m)


def ACTF(out, in_, func, **kw):
    return lambda e: e.activation(out=out, in_=in_, func=func, **kw)


def DMA(out, in_):
    return lambda e: e.dma_start(out=out, in_=in_)


def MM8(out, lhs_of_k, rhs_of_k, n=8):
    def f(e):
        ins = None
        for k in range(n):
            ins = e.matmul(out, lhs_of_k(k), rhs_of_k(k), start=(k == 0), stop=(k == n - 1))
        return ins
    return f


def TRS(outs, ins_, ident):
    def f(e):
        ins = None
        for o, i in zip(outs, ins_):
            ins = e.transpose(o, i, ident)
        return ins
    return f


class _GBView:
    def __init__(self, t):
        self.t = t

    def __getitem__(self, k):
        return self.t[k]


class Rec:
    def __init__(self):
        self.ops = []

    def add(self, eng, fn, r=(), w=(), dma=None, x=()):
        self.ops.append((eng, fn, tuple(r), tuple(w), dma, tuple(x)))

    def pe(self, fn, r=(), w=(), x=()):
        self.add("pe", fn, r, w, x=x)

    def act(self, fn, r=(), w=(), x=()):
        self.add("act", fn, r, w, x=x)

    def dve(self, fn, r=(), w=(), x=()):
        self.add("dve", fn, r, w, x=x)

    def pool(self, fn, r=(), w=(), x=()):
        self.add("pool", fn, r, w, x=x)

    def dma(self, q, key, fn, r=(), w=()):
        self.add(q, fn, r, w, dma=key)


def interleave(S, recs):
    recs = [r for r in recs if r is not None and r.ops]
    idx = [0] * len(recs)
    left = sum(len(r.ops) for r in recs)
    while left:
        for i, r in enumerate(recs):
            if idx[i] < len(r.ops):
                eng, fn, rr, ww, dma, xx = r.ops[idx[i]]
                S.add(eng, fn, rr, ww, dma=dma, x=xx)
                idx[i] += 1
                left -= 1


def build_program(cfg, stage):
    P = Prog(cfg)
    nc = P.nc
    NT = cfg.nt
    npt = cfg.npt
    NTOK = NT * 128
    NB = 16
    with P.stack:
        w = {}

        def IN(nm, shp):
            w[nm] = P.dram_in(nm, shp)
            return w[nm]

        IN("identf", [128, 128])
        IN("ln_g", [DEPTH * 3, D])
        IN("ln_b", [DEPTH * 3, D])
        IN("masks", [5, 128, 512])
        if stage == 1:
            xin = IN("xin", [NTOK, D])
            for nm in ("ffn1", "ffn2"):
                IN(nm + "_wg", [DEPTH, D, FF]); IN(nm + "_wu", [DEPTH, D, FF]); IN(nm + "_wd", [DEPTH, FF, D])
            IN("attn_w_qkv", [D, 1536]); IN("attn_w_o", [D, D]); IN("sinks2", [2, 8])
            IN("cache_k", [NB, 128, 256]); IN("cache_v", [NB, 128, 256])
            IN("rope", [NTOK, 128])
            IN("ret_w_in", [D, 6144])
            IN("xpos", [NTOK, 512]); IN("zeta1", [128, npt * 4])
            xmid_o = P.dram_out("xmid", [(npt + 1) * 128, D])
            F_o = P.dram_out("F", [RH, RQK, RV])
            kwp_o = P.dram_out("kwin_p", [128, 256]); vwp_o = P.dram_out("vwin_p", [128, 256])
            kws_o = P.dram_out("kwin_s", [NB, 128, 256]); vws_o = P.dram_out("vwin_s", [NB, 128, 256])
        else:
            xmid_i = IN("xmid", [(npt + 1) * 128, D])
            IN("ffn2_wg", [DEPTH, D, FF]); IN("ffn2_wu", [DEPTH, D, FF]); IN("ffn2_wd", [DEPTH, FF, D])
            IN("ret_w_in", [D, 6144]); IN("ret_w_o", [2048, D])
            IN("xpos", [NTOK, 512]); IN("rscal", [128, 24]); IN("coef", [128, 32])
            IN("xposh", [RH, npt * 128, 1024])
            IN("ind", [128, NB]); IN("indT", [128, NB * 128])
            IN("fall", [N_CORES, RH, RQK, RV])
            IN("state", [NB, RH, RQK, RV])
            yout = P.dram_out("y", [(npt + 1) * 128, D])
            stp_o = P.dram_out("state_p", [RH, RQK, RV])
            sts_o = P.dram_out("state_s", [NB, RH, RQK, RV])

        x = P.sb("x", [128, NT, D], F32)
        xT = P.sb("xT", [128, 8, NTOK], BF16)
        ident = P.sb("ident", [128, 128], BF16)

        S = Sched(nc, P.stack)

        def post_tile(S, t, xb, ti, psTl, pre_scale=None):
            b = ti % xb.shape[1]
            pb = ti % len(psTl)
            pT = psTl[pb]
            if pre_scale is None:
                S.act(ACP(xb[:, b, :], x[:, t, :]), r=[("x", t, 0), ("x", t, 1)], w=[("xb", b)])
            else:
                S.act(AMUL(xb[:, b, :], x[:, t, :], pre_scale), r=[("x", t, 0), ("x", t, 1)], w=[("xb", b)])
            S.pe(TRS([pT[:, k, :] for k in range(8)], [xb[:, b, k * 128:(k + 1) * 128] for k in range(8)], ident[:]),
                 r=[("xb", b), ("ident",)], w=[("psT", pb)])
            S.dve(CP(xT[:, :, t * 128:(t + 1) * 128], pT[:]), w=[("xT", t)], x=[("psT", pb)])
            if pre_scale is None:
                S.act(AMUL(x[:, t, :], x[:, t, :], ALPHA), r=[("xb", b)], w=[("x", t, 0), ("x", t, 1)])

        def norm_ops(S, j, lnt, src_lo, src_hi, r_lo, r_hi):
            st, mv, sc = lnt["stats"], lnt["mv"], lnt["sc"]
            S.dve(lambda e: e.bn_stats(out=st[:, j, 0, :], in_=src_lo), r=r_lo[0], w=[("st", j, 0)], x=r_lo[1])
            if src_hi is not None:
                S.dve(lambda e: e.bn_stats(out=st[:, j, 1, :], in_=src_hi), r=r_hi[0], w=[("st", j, 1)], x=r_hi[1])
                S.dve(lambda e: e.bn_aggr(out=mv[:, j, :], in_=st[:, j, :, :].rearrange("p a b -> p (a b)")),
                      r=[("st", j, 0), ("st", j, 1)], w=[("mv", j)])
            else:
                S.dve(lambda e: e.bn_aggr(out=mv[:, j, :], in_=st[:, j, 0, :]), r=[("st", j, 0)], w=[("mv", j)])
            S.act(ACTF(sc[:, j, 0:1], mv[:, j, 1:2], AF.Sqrt, bias=lnt["eps"][:, 0:1]),
                  r=[("mv", j), ("eps",)], w=[("sc", j, 0)])
            S.dve(lambda e: e.reciprocal(out=sc[:, j, 1:2], in_=sc[:, j, 0:1]), r=[("sc", j, 0)], w=[("sc", j, 1)])
            S.dve(STT(sc[:, j, 2:3], mv[:, j, 0:1], -1.0, sc[:, j, 1:2], ALU.mult, ALU.mult),
                  r=[("mv", j), ("sc", j, 1)], w=[("sc", j, 2)])

        def ln_a1(S, t, ti, lnt):
            j = ti % lnt["ring"]
            st, mv, sc = lnt["stats"], lnt["mv"], lnt["sc"]
            S.dve(lambda e: e.bn_stats(out=st[:, j, 0, :], in_=x[:, t, 0:512]), r=[("x", t, 0)], w=[("st", j, 0)])
            S.dve(lambda e: e.bn_stats(out=st[:, j, 1, :], in_=x[:, t, 512:1024]), r=[("x", t, 1)], w=[("st", j, 1)])
            S.dve(lambda e: e.bn_aggr(out=mv[:, j, :], in_=st[:, j, :, :].rearrange("p a b -> p (a b)")),
                  r=[("st", j, 0), ("st", j, 1)], w=[("mv", j)])
            S.act(ACTF(sc[:, j, 0:1], mv[:, j, 1:2], AF.Sqrt, bias=lnt["eps"][:, 0:1]),
                  r=[("mv", j), ("eps",)], w=[("sc", j, 0)])
            S.dve(lambda e: e.reciprocal(out=sc[:, j, 1:2], in_=sc[:, j, 0:1]), r=[("sc", j, 0)], w=[("sc", j, 1)])

        def ln_a2(S, t, ti, gb, lnt, gbkey="gb"):
            j = ti % lnt["ring"]
            mv, sc = lnt["mv"], lnt["sc"]
            S.dve(lambda e: e.tensor_scalar(out=x[:, t, :], in0=x[:, t, :], scalar1=mv[:, j, 0:1], scalar2=sc[:, j, 1:2],
                                            op0=ALU.subtract, op1=ALU.mult),
                  r=[("mv", j), ("sc", j, 1)], w=[("x", t, 0), ("x", t, 1)])
            S.pool(TT(x[:, t, :], x[:, t, :], gb[:, 0, :], ALU.mult), r=[(gbkey, 0)], w=[("x", t, 0), ("x", t, 1)])
            S.pool(TT(x[:, t, :], x[:, t, :], gb[:, 1, :], ALU.add), r=[(gbkey, 1)], w=[("x", t, 0), ("x", t, 1)])

        def ln_a(S, t, ti, gb, lnt, gbkey="gb"):
            ln_a1(S, t, ti, lnt)
            ln_a2(S, t, ti, gb, lnt, gbkey)

        def ln_b(S, t, ti, xb, psTl, final_out=None):
            if final_out is None:
                post_tile(S, t, xb, ti, psTl)
            else:
                S.dma("sp", "yout%d" % (ti % 4), DMA(final_out[t * 128:(t + 1) * 128, :], x[:, t, :]),
                      r=[("x", t, 0), ("x", t, 1)])

        def ln_tile(S, t, ti, gb, lnt, xb, psTl, final_out=None, gbkey="gb"):
            ln_a(S, t, ti, gb, lnt, gbkey)
            ln_b(S, t, ti, xb, psTl, final_out)

        def load_gb(S, gb, idx):
            S.dma("sp", "gb", DMA(gb[:, 0, :], w["ln_g"][idx:idx + 1, :].partition_broadcast(128)), w=[("gb", 0)])
            S.dma("sp", "gb2", DMA(gb[:, 1, :], w["ln_b"][idx:idx + 1, :].partition_broadcast(128)), w=[("gb", 1)])

        def ln_bufs(ph, ring=3):
            lnt = {
                "stats": P.sb("lnstats", [128, ring, 2, 6], F32, ph),
                "mv": P.sb("lnmv", [128, ring, 2], F32, ph),
                "sc": P.sb("lnsc", [128, ring, 4], F32, ph),
                "eps": P.sb("lneps", [128, 1], F32, ph),
                "ring": ring,
            }
            S.dve(lambda e: e.memset(lnt["eps"][:], LN_EPS), w=[("eps",)])
            return lnt

        def dump_x(name):
            dbg = P.dram_out(name, [NT * 128, D])
            for t in range(NT):
                S.dma("sp", "tile%d" % t, DMA(dbg[t * 128:(t + 1) * 128, :], x[:, t, :]),
                      r=[("x", t, 0), ("x", t, 1)])
            S.flush()

        with ExitStack() as ph:
            psT0 = [P.ps("psT0", [128, 8, 128], BF16, ph) for _ in range(4)]
            xb = P.sb("xb0", [128, 4, D], BF16, ph)
            identf = P.sb("identf_sb", [128, 128], F32, ph)
            S.dma("sp", "identf", DMA(identf[:], w["identf"][:, :]), w=[("identf",)])
            S.act(ACP(ident[:], identf[:]), r=[("identf",)], w=[("ident",)])
            if stage == 1:
                for t in range(NT):
                    S.dma("sp", "tile%d" % t, DMA(x[:, t, :], xin[t * 128:(t + 1) * 128, :]),
                          w=[("x", t, 0), ("x", t, 1)])
                for t in range(NT):
                    post_tile(S, t, xb, t, psT0)
            else:
                for t in range(npt + 1):
                    S.dma("sp", "tile%d" % t, DMA(x[:, t, :], xmid_i[t * 128:(t + 1) * 128, :]),
                          w=[("x", t, 0), ("x", t, 1)])
                for t in range(npt + 1):
                    post_tile(S, t, xb, t, psT0, pre_scale=1.0 / ALPHA)
            S.flush()

        def ffn_phase(S, wg, wu, wd, ln_idx, with_halo, final_out=None, pre_ln=None, pre_tiles=None):
            G = cfg.G
            groups = cfg.groups(with_halo)
            tiles = []
            tgrp = {}
            for gi, (t0, n) in enumerate(groups):
                for t in range(t0, t0 + n):
                    tiles.append(t)
                    tgrp[t] = gi
            ntl_ = len(tiles)
            parts = [list(range(i, min(i + G, NFC))) for i in range(0, NFC, G)]
            RW = 5
            RD = RW + G - 1
            with ExitStack() as ph:
                psA = [P.ps("psA", [128, 512], F32, ph) for _ in range(6)]
                psTl = [P.ps("psT", [128, 8, 128], BF16, ph) for _ in range(2)]
                hT = P.sb("hT", [128, G, NTOK], BF16, ph)
                wgs = P.sb("wgs", [128, RW, 8, 128], BF16, ph)
                wus = P.sb("wus", [128, RW, 8, 128], BF16, ph)
                wds = P.sb("wds", [128, RD, D], BF16, ph)
                sg = P.sb("sg", [128, 2, 512], F32, ph)
                gb = P.sb("gb", [128, 2, D], F32, ph)
                xb = P.sb("xb", [128, 2, D], BF16, ph)
                lnt = ln_bufs(ph, ring=8)
                if pre_ln is not None:
                    gbp = P.sb("gbp", [128, 2, D], F32, ph)
                    S.dma("sp", "gbp", DMA(gbp[:, 0, :], w["ln_g"][pre_ln:pre_ln + 1, :].partition_broadcast(128)), w=[("gbp", 0)])
                    S.dma("sp", "gbp2", DMA(gbp[:, 1, :], w["ln_b"][pre_ln:pre_ln + 1, :].partition_broadcast(128)), w=[("gbp", 1)])
                load_gb(S, gb, ln_idx)

                def issue_w(S, fc):
                    sl = fc % RW
                    sd = fc % RD
                    S.dma("pool", "wg%d" % sl,
                          DMA(wgs[:, sl, :, :], wg[:, fc * 128:(fc + 1) * 128].rearrange("(k p) f -> p k f", p=128)),
                          w=[("wg", sl)])
                    S.dma("pool", "wu%d" % sl,
                          DMA(wus[:, sl, :, :], wu[:, fc * 128:(fc + 1) * 128].rearrange("(k p) f -> p k f", p=128)),
                          w=[("wu", sl)])
                    S.dma("pool", "wd%d" % sd, DMA(wds[:, sd, :], wd[fc * 128:(fc + 1) * 128, :]), w=[("wd", sd)])

                cnt = {"it": 0, "yi": 0}

                def gu(S, fc, fl, gi):
                    t0, n = groups[gi]
                    it = cnt["it"]
                    cnt["it"] += 1
                    sl = fc % RW
                    N = n * 128
                    c0 = t0 * 128
                    bg = psA[it % 2]
                    bu = psA[2 + it % 2]
                    sb_ = it % 2
                    xr = [("xT", t) for t in range(t0, t0 + n)]
                    S.pe(MM8(bg[:, 0:N], (lambda k: wgs[:, sl, k, :]), (lambda k: xT[:, k, c0:c0 + N])),
                         r=[("wg", sl)] + xr, w=[("ps", it % 2)])
                    S.pe(MM8(bu[:, 0:N], (lambda k: wus[:, sl, k, :]), (lambda k: xT[:, k, c0:c0 + N])),
                         r=[("wu", sl)] + xr, w=[("ps", 2 + it % 2)])
                    S.act(ACTF(sg[:, sb_, 0:N], bg[:, 0:N], AF.Silu), w=[("sg", sb_)], x=[("ps", it % 2)])
                    S.dve(TT(hT[:, fl, c0:c0 + N], bu[:, 0:N], sg[:, sb_, 0:N], ALU.mult),
                          r=[("sg", sb_)], w=[("hT", fl, gi)], x=[("ps", 2 + it % 2)])

                def down(S, part, t, bank=None):
                    for dh in range(2):
                        yi = cnt["yi"]
                        cnt["yi"] += 1
                        bi = (4 + yi % 2) if bank is None else bank
                        by = psA[bi]

                        def mmd(e, by=by, dh=dh):
                            ins = None
                            for fl, fc in enumerate(part):
                                ins = e.matmul(by[:, :], hT[:, fl, t * 128:(t + 1) * 128],
                                               wds[:, fc % RD, dh * 512:(dh + 1) * 512],
                                               start=(fl == 0), stop=(fl == len(part) - 1))
                            return ins
                        S.pe(mmd, r=[("hT", fl, tgrp[t]) for fl in range(len(part))] +
                             [("wd", fc % RD) for fc in part], w=[("ps", bi)])
                        S.dve(STT(x[:, t, dh * 512:(dh + 1) * 512], by[:, :], 0.5,
                                  x[:, t, dh * 512:(dh + 1) * 512], ALU.mult, ALU.add),
                              w=[("x", t, dh)], x=[("ps", bi)])

                for fc in range(min(RW - 1, NFC)):
                    issue_w(S, fc)

                first_done = set()
                if pre_ln is not None:
                    pre_set = set(tiles if pre_tiles is None else pre_tiles)
                    L = [ti for ti, t in enumerate(tiles) if t in pre_set]
                    Ra, Ra2, Rb = {}, {}, {}
                    for ti in L:
                        t = tiles[ti]
                        Ra[ti] = Rec(); ln_a1(Ra[ti], t, ti, lnt)
                        Ra2[ti] = Rec(); ln_a2(Ra2[ti], t, ti, gbp, lnt, "gbp")
                        Rb[ti] = Rec(); ln_b(Rb[ti], t, ti, xb, psTl)
                    nL = len(L)
                    done_at = {}
                    for k, ti in enumerate(L):
                        done_at[ti] = (k - k % 2) + 4
                    for i in range(0, nL + 6, 2):
                        grp = [Ra[L[k]] for k in (i, i + 1) if k < nL]
                        grp += [Ra2[L[k]] for k in (i - 2, i - 1) if 0 <= k < nL]
                        grp += [Rb[L[k]] for k in (i - 4, i - 3) if 0 <= k < nL]
                        for gi, (t0, n) in enumerate(groups):
                            need = [tiles.index(t) for t in range(t0, t0 + n) if t in pre_set]
                            ready = all(done_at[ti] < i for ti in need)
                            if gi not in first_done and ready:
                                R = Rec()
                                if not first_done and RW - 1 < NFC:
                                    issue_w(R, RW - 1)
                                gu(R, 0, 0, gi)
                                grp.append(R)
                                first_done.add(gi)
                                break
                        interleave(S, grp)
                    for gi in range(len(groups)):
                        if gi not in first_done:
                            if not first_done and RW - 1 < NFC:
                                issue_w(S, RW - 1)
                            gu(S, 0, 0, gi)
                            first_done.add(gi)

                for pi, part in enumerate(parts):
                    last = (pi == len(parts) - 1)
                    if not last:
                        for fl, fc in enumerate(part):
                            if fc + RW - 1 < NFC and not (fc == 0 and first_done):
                                issue_w(S, fc + RW - 1)
                            for gi in range(len(groups)):
                                if fc == 0 and gi in first_done:
                                    continue
                                gu(S, fc, fl, gi)
                        for t in tiles:
                            down(S, part, t)
                        continue
                    assert len(part) <= RW - 1 and part[-1] == NFC - 1
                    Rd, Ra, Ra2, Rb, Rgu = {}, {}, {}, {}, {}
                    for gi in range(len(groups)):
                        Rgu[gi] = Rec()
                        for fl, fc in enumerate(part):
                            gu(Rgu[gi], fc, fl, gi)
                    for ti, t in enumerate(tiles):
                        Rd[ti] = Rec(); down(Rd[ti], part, t, bank=4 + ti % 2)
                        Ra[ti] = Rec(); ln_a1(Ra[ti], t, ti, lnt)
                        Ra2[ti] = Rec(); ln_a2(Ra2[ti], t, ti, gb, lnt)
                        Rb[ti] = Rec(); ln_b(Rb[ti], t, ti, xb, psTl, final_out)
                    interleave(S, [Rgu[0]])
                    first_ti = [tiles.index(t0) for (t0, n) in groups] + [ntl_]
                    inject = {}
                    for gi in range(1, len(groups)):
                        steps = list(range(first_ti[gi - 1] - first_ti[gi - 1] % 2, first_ti[gi], 2))
                        ops = Rgu[gi].ops
                        per = (len(ops) + len(steps) - 1) // len(steps)
                        for si, st_ in enumerate(steps):
                            R = Rec()
                            R.ops = ops[si * per:(si + 1) * per]
                            inject.setdefault(st_, []).append(R)
                    for i in range(0, ntl_ + 8, 2):
                        grp = inject.get(i, [])
                        grp += [Rd[k] for k in (i, i + 1) if k < ntl_]
                        grp += [Ra[k] for k in (i - 2, i - 1) if 0 <= k < ntl_]
                        grp += [Ra2[k] for k in (i - 4, i - 3) if 0 <= k < ntl_]
                        grp += [Rb[k] for k in (i - 6, i - 5) if 0 <= k < ntl_]
                        interleave(S, grp)
                S.flush()

        def attn_phases(S):
            with ExitStack() as po:
                wo = P.sb("wo", [128, 8, D], BF16, po)
                es2 = P.sb("es2", [128, 8], F32, po)
                ones64 = P.sb("ones64", [128, 64], BF16, po)
                QTs = P.sb("QTs", [64, 16, 128], BF16, po)
                KTs = P.sb("KTs", [64, 4, 128], BF16, po)
                Vbs = P.sb("Vbs", [128, 256], BF16, po)
                krs = P.sb("krs", [128, 256], F32, po)
                vfs = P.sb("vfs", [128, 256], F32, po)
                e_ = P.sb("e", [128, 2, 512], BF16, po)
                rd = P.sb("rd", [128, 2, 256], F32, po)
                oT = P.sb("oT", [128, 2, 8, 128], BF16, po)

                def common_loads(S, masks, m0, nm):
                    S.dma("pool", "masks", DMA(masks[:], w["masks"][m0:m0 + nm].rearrange("m p n -> p m n")), w=[("masks",)])

                def normalize(S, g, ob, os_, psOD):
                    rdv = rd[:, ob, :].rearrange("p (c q) -> p c q", c=2)
                    S.dve(TT(rdv, psOD[ob][:, 256:512].rearrange("p (c q) -> p c q", c=2),
                             es2[:, 2 * g:2 * g + 2].unsqueeze(2).broadcast_to([128, 2, 128]), ALU.add),
                          r=[("es2",)], w=[("rd", ob)], x=[("psod", ob)])
                    S.dve(lambda e: e.reciprocal(out=rd[:, ob, :], in_=rd[:, ob, :]), w=[("rd", ob)])
                    S.dve(TT(oT[:, os_, 2 * g:2 * g + 2, :], psOD[ob][:, 0:256].rearrange("p (c q) -> p c q", c=2),
                             rdv, ALU.mult), r=[("rd", ob)], w=[("oT", os_, g)], x=[("psod", ob)])

                def outproj_add(S, t, os_, psY, ytag):
                    for dh in range(2):
                        def f(e, dh=dh):
                            ins = None
                            for c in range(8):
                                ins = e.matmul(psY[dh][:, :], oT[:, os_, c, :], wo[:, c, dh * 512:(dh + 1) * 512],
                                               start=(c == 0), stop=(c == 7))
                            return ins
                        S.pe(f, r=[("oT", os_, g) for g in range(4)] + [("wo",)], w=[(ytag, dh)])
                        S.dve(TT(x[:, t, dh * 512:(dh + 1) * 512], psY[dh][:, :], x[:, t, dh * 512:(dh + 1) * 512], ALU.add),
                              w=[("x", t, dh)], x=[(ytag, dh)])

                S.dma("pool", "wo", DMA(wo[:], w["attn_w_o"].rearrange("(c p) n -> p c n", p=128)), w=[("wo",)])
                S.dma("sp", "es2a", DMA(es2[0:64, :], w["sinks2"][0:1, :].partition_broadcast(64)), w=[("es2", 0)])
                S.dma("sp", "es2b", DMA(es2[64:128, :], w["sinks2"][1:2, :].partition_broadcast(64)), w=[("es2", 1)])
                S.act(ACTF(es2[:], es2[:], AF.Exp), r=[("es2", 0), ("es2", 1)], w=[("es2",)])
                S.dve(lambda e: e.memset(ones64[:], 1.0), w=[("ones",)])

                with ExitStack() as ph:
                    psQ = [P.ps("psQ", [128, 512], F32, ph) for _ in range(2)]
                    psKV = P.ps("psKV", [128, 512], F32, ph)
                    psT = P.ps("psT", [128, 8, 128], BF16, ph)
                    psS = [P.ps("psS", [128, 512], F32, ph) for _ in range(2)]
                    psOD = [P.ps("psOD", [128, 512], F32, ph) for _ in range(2)]
                    masks = P.sb("masks", [128, 3, 512], BF16, ph)
                    wqkv = P.sb("wqkv", [128, 8, 1536], BF16, ph)
                    rope_sb = P.sb("rope", [128, 2, 128], F32, ph)
                    ropeA = P.sb("ropeA", [128, 8, 64], F32, ph)
                    ropeB = P.sb("ropeB", [128, 8, 64], F32, ph)
                    kA = P.sb("kA", [128, 4, 64], F32, ph)
                    kB = P.sb("kB", [128, 4, 64], F32, ph)
                    qrot = P.sb("qrot", [128, 2, 1024], BF16, ph)
                    QT = P.sb("QT", [64, 2, 16, 128], BF16, ph)
                    krf = P.sb("krf", [128, 2, 256], F32, ph)
                    kbf = P.sb("kbf", [128, 2, 256], BF16, ph)
                    KT = P.sb("KT", [64, 5, 4, 128], BF16, ph)
                    vf = P.sb("vf", [128, 2, 256], F32, ph)
                    Vb = P.sb("Vb", [128, 5, 256], BF16, ph)
                    PT = P.sb("PT", [128, 8, 512], BF16, ph)
                    common_loads(S, masks, 0, 3)
                    S.dma("pool", "wqkv", DMA(wqkv[:], w["attn_w_qkv"].rearrange("(k p) n -> p k n", p=128)),
                          w=[("wqkv",)])
                    order = [cfg.halo] + list(range(npt)) + [cfg.samp]
                    NO = len(order)

                    def dst(ti):
                        t = order[ti]
                        b2, slot = ti % 2, ti % 5
                        if t == cfg.samp:
                            return dict(krf=krs[:, :], vf=vfs[:, :], Vb=Vbs[:, :], KT=KTs, QT=QTs,
                                        rk=("krs",), rv=("vfs",), rVb=("Vbs",), rKT=("KTs",), rQT=("QTs",))
                        return dict(krf=krf[:, b2, :], vf=vf[:, b2, :], Vb=Vb[:, slot, :], KT=KT[:, slot], QT=QT[:, b2],
                                    rk=("krf", b2), rv=("vf", b2), rVb=("Vb", slot), rKT=("KT", slot), rQT=("QT", b2))

                    def P1(R, ti):
                        t = order[ti]
                        need_q = (t != cfg.halo)
                        cs = slice(t * 128, (t + 1) * 128)
                        b2 = ti % 2
                        d_ = dst(ti)
                        R.dma("sp", "rope%d" % b2, DMA(rope_sb[:, b2, :], w["rope"][cs, :]), w=[("rope", b2)])
                        c2 = rope_sb[:, b2, 0:64]
                        nlo = rope_sb[:, b2, 64:96]
                        nhi = rope_sb[:, b2, 96:128]
                        R.pe(MM8(psKV[:, :], (lambda k: xT[:, k, cs]), (lambda k: wqkv[:, k, 1024:1536])),
                             r=[("xT", t), ("wqkv",)], w=[("pskv",)])
                        kv = psKV[:, 0:256].rearrange("p (h d) -> p h d", d=64)
                        R.dve(TT(kA[:], kv, c2.unsqueeze(1).broadcast_to([128, 4, 64]), ALU.mult),
                              r=[("rope", b2)], w=[("kA",)], x=[("pskv",)])
                        R.dve(TT(kB[:, :, 0:32], kv[:, :, 32:64], nlo.unsqueeze(1).broadcast_to([128, 4, 32]), ALU.mult),
                              r=[("rope", b2)], w=[("kB", 0)], x=[("pskv",)])
                        R.dve(TT(kB[:, :, 32:64], kv[:, :, 0:32], nhi.unsqueeze(1).broadcast_to([128, 4, 32]), ALU.mult),
                              r=[("rope", b2)], w=[("kB", 1)], x=[("pskv",)])
                        R.act(ACP(d_["vf"], psKV[:, 256:512]), w=[d_["rv"]], x=[("pskv",)])
                        R.pool(TT(d_["krf"].rearrange("p (h d) -> p h d", d=64), kA[:], kB[:], ALU.add),
                               r=[("kA",), ("kB", 0), ("kB", 1)], w=[d_["rk"]])
                        R.act(ACP(d_["Vb"], d_["vf"]), r=[d_["rv"]], w=[d_["rVb"]])
                        R.pool(CP(kbf[:, b2, :], d_["krf"]), r=[d_["rk"]], w=[("kbf", b2)])
                        if need_q:
                            for half in range(2):
                                R.pe(MM8(psQ[half][:, :], (lambda k: xT[:, k, cs]),
                                         (lambda k, half=half: wqkv[:, k, half * 512:(half + 1) * 512])),
                                     r=[("xT", t), ("wqkv",)], w=[("psq", half)])
                                qv = psQ[half][:, :].rearrange("p (h d) -> p h d", d=64)
                                R.dve(TT(ropeA[:], qv, c2.unsqueeze(1).broadcast_to([128, 8, 64]), ALU.mult),
                                      r=[("rope", b2)], w=[("ropeA",)], x=[("psq", half)])
                                R.dve(TT(ropeB[:, :, 0:32], qv[:, :, 32:64], nlo.unsqueeze(1).broadcast_to([128, 8, 32]),
                                         ALU.mult), r=[("rope", b2)], w=[("ropeB", 0)], x=[("psq", half)])
                                R.dve(TT(ropeB[:, :, 32:64], qv[:, :, 0:32], nhi.unsqueeze(1).broadcast_to([128, 8, 32]),
                                         ALU.mult), r=[("rope", b2)], w=[("ropeB", 1)], x=[("psq", half)])
                                R.pool(TT(qrot[:, b2, half * 512:(half + 1) * 512].rearrange("p (h d) -> p h d", d=64),
                                          ropeA[:], ropeB[:], ALU.add),
                                       r=[("ropeA",), ("ropeB", 0), ("ropeB", 1)], w=[("qrot", b2, half)])

                    def P2(R, ti):
                        t = order[ti]
                        need_q = (t != cfg.halo)
                        b2 = ti % 2
                        d_ = dst(ti)
                        R.pe(TRS([psT[0:64, g, :] for g in range(4)],
                                 [kbf[:, b2, g * 64:(g + 1) * 64] for g in range(4)], ident[:]),
                             r=[("kbf", b2), ("ident",)], w=[("psT", 0)])
                        R.act(ACP(d_["KT"][:, :, :], psT[0:64, 0:4, :]), w=[d_["rKT"]], x=[("psT", 0)])
                        if need_q:
                            for half in range(2):
                                R.pe(TRS([psT[0:64, h8, :] for h8 in range(8)],
                                         [qrot[:, b2, (half * 8 + h8) * 64:(half * 8 + h8 + 1) * 64] for h8 in range(8)],
                                         ident[:]), r=[("qrot", b2, half), ("ident",)], w=[("psT", 0)])
                                R.act(ACP(d_["QT"][:, half * 8:(half + 1) * 8, :], psT[0:64, :, :]),
                                      w=[d_["rQT"] + (half,)], x=[("psT", 0)])
                        if t == npt - 1:
                            R.dma("sp", "kwp", DMA(kwp_o[:, :], d_["krf"]), r=[d_["rk"]])
                            R.dma("sp", "vwp", DMA(vwp_o[:, :], d_["vf"]), r=[d_["rv"]])

                    def P3(R, ti):
                        t = order[ti]
                        b2, slot, prev = ti % 2, ti % 5, (ti - 1) % 5
                        u = 0
                        for g in range(4):
                            for kb in range(2):
                                sl_ = prev if kb == 0 else slot
                                midx = (2 if t == 0 else 0) if kb == 0 else 1
                                pb = g * 2 + kb
                                sb_ = u % 2
                                R.pe((lambda sl_=sl_, g=g, sb_=sb_: lambda e: e.matmul(
                                    psS[sb_][:, :], KT[:, sl_, g, :],
                                    QT[:, b2, 4 * g:4 * g + 4, :].rearrange("p h q -> p (h q)"), start=True, stop=True))(),
                                    r=[("KT", sl_), ("QT", b2, 0), ("QT", b2, 1)], w=[("pss", sb_)])
                                R.act(ACTF(e_[:, sb_, :], psS[sb_][:, :], AF.Exp, scale=ATTN_SCALE),
                                      w=[("e", sb_)], x=[("pss", sb_)])
                                R.pool(TT(PT[:, pb, :], e_[:, sb_, :], masks[:, midx, :], ALU.mult),
                                       r=[("e", sb_), ("masks",)], w=[("PT", pb)])
                                u += 1
                        for g in range(4):
                            ob = g % 2

                            def pv(e, g=g, ob=ob):
                                ins = None
                                for half in range(2):
                                    for kind in range(2):
                                        for kb in range(2):
                                            sl_ = prev if kb == 0 else slot
                                            lhs = Vb[:, sl_, g * 64:(g + 1) * 64] if kind == 0 else ones64[:, :]
                                            rhs = PT[:, g * 2 + kb, :].rearrange("p (c j q) -> p c j q", c=2, j=2)[:, :, half, :]
                                            ins = e.matmul(
                                                psOD[ob][half * 64:(half + 1) * 64,
                                                         kind * 256:(kind + 1) * 256].rearrange("p (c q) -> p c q", c=2),
                                                lhs, rhs, start=(kb == 0), stop=(kb == 1))
                                return ins
                            R.pe(pv, r=[("Vb", prev), ("Vb", slot), ("PT", g * 2), ("PT", g * 2 + 1), ("ones",)],
                                 w=[("psod", ob)])
                            normalize(R, g, ob, b2, psOD)

                    def P4(R, ti):
                        outproj_add(R, order[ti], ti % 2, psQ, "psq")

                    for i in range(-2, NO + 2):
                        recs = []
                        for fn_, off in ((P3, 0), (P2, 1), (P4, -1), (P1, 2)):
                            ti = i + off
                            if not (0 <= ti < NO):
                                continue
                            t = order[ti]
                            if fn_ in (P3, P4) and (t == cfg.halo or t == cfg.samp):
                                continue
                            R = Rec()
                            fn_(R, ti)
                            recs.append(R)
                        interleave(S, recs)
                    S.flush()

                with ExitStack() as ph:
                    psY = [P.ps("psY", [128, 512], F32, ph) for _ in range(2)]
                    psT = P.ps("psT", [128, 8, 128], BF16, ph)
                    psSn = P.ps("psSn", [128, 512], F32, ph)
                    psS2 = [P.ps("psS2", [128, 512], F32, ph) for _ in range(2)]
                    psOD = [P.ps("psOD", [128, 512], F32, ph) for _ in range(2)]
                    masks = P.sb("masks", [128, 2, 512], BF16, ph)
                    KcT = P.sb("KcT", [64, NB, 4, 128], BF16, ph)
                    Vc = P.sb("Vc", [128, NB, 256], BF16, ph)
                    kcb = P.sb("kcb", [128, 2, 256], BF16, ph)
                    PT = P.sb("PT", [128, 2, 512], BF16, ph)
                    PTc = P.sb("PTc", [128, 2, 512], BF16, ph)
                    common_loads(S, masks, 3, 2)
                    t = cfg.samp
                    S.dma("pool", "vc", DMA(Vc[:], w["cache_v"].rearrange("b p c -> p b c")), w=[("Vc",)])
                    for b in range(NB):
                        S.dma("pool", "kc%d" % (b % 2), DMA(kcb[:, b % 2, :], w["cache_k"][b]), w=[("kcb", b % 2)])
                        S.pe(TRS([psT[0:64, g, :] for g in range(4)],
                                 [kcb[:, b % 2, g * 64:(g + 1) * 64] for g in range(4)], ident[:]),
                             r=[("kcb", b % 2), ("ident",)], w=[("psT", 0)])
                        S.act(ACP(KcT[:, b, :, :], psT[0:64, 0:4, :]), w=[("KcT", b)], x=[("psT", 0)])
                    S.dma("sp", "kwo", DMA(kws_o[:, 0:120, :], w["cache_k"][:, 8:128, :]))
                    S.dma("sp", "vwo", DMA(vws_o[:, 0:120, :], w["cache_v"][:, 8:128, :]))
                    for b in range(NB):
                        S.dma("sp", "kwn", DMA(kws_o[b, 120:128, :], krs[b * 8:(b + 1) * 8, :]))
                        S.dma("sp", "vwn", DMA(vws_o[b, 120:128, :], vfs[b * 8:(b + 1) * 8, :]))
                    for g in range(4):
                        S.pe((lambda g=g: lambda e: e.matmul(psSn[:, :], KTs[:, g, :],
                                                             QTs[:, 4 * g:4 * g + 4, :].rearrange("p h q -> p (h q)"),
                                                             start=True, stop=True))(),
                             w=[("psn",)])
                        S.act(ACTF(e_[:, 0, :], psSn[:, :], AF.Exp, scale=ATTN_SCALE), w=[("e", 0)], x=[("psn",)])
                        S.pool(TT(PT[:, g % 2, :], e_[:, 0, :], masks[:, 0, :], ALU.mult),
                               r=[("e", 0), ("masks",)], w=[("PT", g % 2)])

                        def sc(e, g=g):
                            ins = None
                            for b in range(NB):
                                ins = e.matmul(psS2[g % 2][:, b * 32:(b + 1) * 32].rearrange("p (h i) -> p h i", h=4),
                                               KcT[:, b, g, :], QTs[:, 4 * g:4 * g + 4, b * 8:(b + 1) * 8],
                                               start=True, stop=True)
                            return ins
                        S.pe(sc, r=[("KcT", b) for b in range(NB)], w=[("ps2", g % 2)])
                        S.act(ACTF(e_[:, 1, :], psS2[g % 2][:, :], AF.Exp, scale=ATTN_SCALE),
                              w=[("e", 1)], x=[("ps2", g % 2)])
                        S.pool(TT(PTc[:, g % 2, :], e_[:, 1, :], masks[:, 1, :], ALU.mult),
                               r=[("e", 1), ("masks",)], w=[("PTc", g % 2)])
                        ob = g % 2

                        def pv2(e, g=g, ob=ob):
                            ins = None
                            for j in range(4):
                                half, cc = j % 2, j // 2
                                for kind in range(2):
                                    reg = psOD[ob][half * 64:(half + 1) * 64,
                                                   kind * 256 + cc * 128:kind * 256 + (cc + 1) * 128]
                                    lhs = Vbs[:, g * 64:(g + 1) * 64] if kind == 0 else ones64[:, :]
                                    ins = e.matmul(reg, lhs, PT[:, g % 2, j * 128:(j + 1) * 128], start=True, stop=False)
                                    for b in range(NB):
                                        lhs = Vc[:, b, g * 64:(g + 1) * 64] if kind == 0 else ones64[:, :]
                                        ins = e.matmul(reg[:, b * 8:(b + 1) * 8], lhs,
                                                       PTc[:, g % 2, b * 32 + j * 8:b * 32 + j * 8 + 8],
                                                       start=False, stop=(b == NB - 1))
                            return ins
                        S.pe(pv2, r=[("Vc",), ("PT", g % 2), ("PTc", g % 2), ("ones",)], w=[("psod", ob)])
                        normalize(S, g, ob, 0, psOD)
                    outproj_add(S, t, 0, psY, "psy")
                    S.flush()

        def xpos_rot(S, src, nqk, xp_t, rA, rB, rR, rtag, srcx):
            v3 = src.rearrange("p (a c) -> p a c", a=nqk)
            v4 = src.rearrange("p (a i two) -> p a i two", a=nqk, two=2)
            b4 = rB[:, 0:nqk, :].rearrange("p a (i two) -> p a i two", two=2)
            S.dve(TT(rA[:, 0:nqk, :], v3, xp_t[:, 0:256].unsqueeze(1).broadcast_to([128, nqk, 256]), ALU.mult),
                  r=[rtag], w=[("rA",)], x=srcx)
            S.dve(TT(b4[:, :, :, 0], v4[:, :, :, 1], xp_t[:, 256:384].unsqueeze(1).broadcast_to([128, nqk, 128]),
                     ALU.mult), r=[rtag], w=[("rB", 0)], x=srcx)
            S.dve(TT(b4[:, :, :, 1], v4[:, :, :, 0], xp_t[:, 384:512].unsqueeze(1).broadcast_to([128, nqk, 128]),
                     ALU.mult), r=[rtag], w=[("rB", 1)], x=srcx)
            S.pool(TT(rR[:, 0:nqk, :], rA[:, 0:nqk, :], rB[:, 0:nqk, :], ALU.add),
                   r=[("rA",), ("rB", 0), ("rB", 1)], w=[("rR",)])

        def sweep1(S):
            with ExitStack() as ph:
                psK = [P.ps("psK", [128, 512], F32, ph) for _ in range(2)]
                psV = [P.ps("psV", [128, 512], F32, ph) for _ in range(2)]
                psF = P.ps("psF", [128, 2, 512], F32, ph)
                wkv = P.sb("wkv", [128, 8, 3072], BF16, ph)
                xp = P.sb("xp", [128, 4, 512], F32, ph)
                z1 = P.sb("z1", [128, npt * 4], F32, ph)
                rA = P.sb("rA", [128, 2, 256], F32, ph)
                rB = P.sb("rB", [128, 2, 256], F32, ph)
                rR = P.sb("rR", [128, 2, 256], F32, ph)
                kz = P.sb("kz", [128, 6, 256], BF16, ph)
                vb = P.sb("vb", [128, 6, 512], BF16, ph)
                Fsb = P.sb("Fsb", [128, 2, 2, 512], F32, ph)
                S.dma("pool", "wk", DMA(wkv[:, :, 0:1024], w["ret_w_in"][:, 1024:2048].rearrange("(k p) n -> p k n", p=128)),
                      w=[("wkv", 0)])
                S.dma("pool", "wv", DMA(wkv[:, :, 1024:3072], w["ret_w_in"][:, 2048:4096].rearrange("(k p) n -> p k n", p=128)),
                      w=[("wkv", 1)])
                S.dma("sp", "z1", DMA(z1[:], w["zeta1"][:, :]), w=[("z1",)])
                units = [(h, t) for h in range(RH) for t in range(npt)]
                NU = len(units)

                def X(R, i):
                    h, t = units[i]
                    cs = slice(t * 128, (t + 1) * 128)
                    b, b4, b6 = i % 2, i % 4, i % 6
                    R.pe(MM8(psK[b][:, 0:256], (lambda k: xT[:, k, cs]), (lambda k: wkv[:, k, h * 256:(h + 1) * 256])),
                         r=[("xT", t), ("wkv", 0)], w=[("psk", b)])
                    R.dma("sp", "xp%d" % b4, DMA(xp[:, b4, :], w["xpos"][cs, :]), w=[("xp", b4)])
                    R.pe(MM8(psV[b][:, :], (lambda k: xT[:, k, cs]),
                             (lambda k: wkv[:, k, 1024 + h * 512:1024 + (h + 1) * 512])),
                         r=[("xT", t), ("wkv", 1)], w=[("psv", b)])
                    src = psK[b][:, 0:256]
                    tab = xp[:, b4, :]
                    v4 = src.rearrange("p (i two) -> p i two", two=2)
                    bb = rB[:, b, :].rearrange("p (i two) -> p i two", two=2)
                    R.dve(TT(rA[:, b, :], src, tab[:, 0:256], ALU.mult), r=[("xp", b4)], w=[("rA", b)], x=[("psk", b)])
                    R.dve(TT(bb[:, :, 0], v4[:, :, 1], tab[:, 256:384], ALU.mult), r=[("xp", b4)], w=[("rB", b, 0)],
                          x=[("psk", b)])
                    R.dve(TT(bb[:, :, 1], v4[:, :, 0], tab[:, 384:512], ALU.mult), r=[("xp", b4)], w=[("rB", b, 1)],
                          x=[("psk", b)])
                    R.act(ACP(vb[:, b6, :], psV[b][:, :]), w=[("vb", b6)], x=[("psv", b)])
                    R.pool(TT(rR[:, b, :], rA[:, b, :], rB[:, b, :], ALU.add),
                           r=[("rA", b), ("rB", b, 0), ("rB", b, 1)], w=[("rR", b)])
                    R.act(AMUL(kz[:, b6, :], rR[:, b, :], z1[:, t * 4 + h:t * 4 + h + 1]),
                          r=[("rR", b), ("z1",)], w=[("kz", b6)])

                def Y(R, i):
                    h, t = units[i]
                    b6 = i % 6

                    def df(e):
                        ins = None
                        for dc in range(2):
                            ins = e.matmul(psF[:, dc, :], kz[:, b6, dc * 128:(dc + 1) * 128], vb[:, b6, :],
                                           start=(t == 0), stop=(t == npt - 1))
                        return ins
                    R.pe(df, r=[("kz", b6), ("vb", b6)], w=[("psf",)])
                    if t == npt - 1:
                        R.act(ACP(Fsb[:, h % 2, :, :], psF[:, :, :]), w=[("Fsb", h % 2)], x=[("psf",)])
                        R.dma("sp", "fo%d" % (h % 2), DMA(F_o[h].rearrange("(dc p) e -> p dc e", p=128), Fsb[:, h % 2, :, :]),
                              r=[("Fsb", h % 2)])

                for i in range(-2, NU, 2):
                    recs = []
                    Ry = Rec()
                    for k in (i, i + 1):
                        if 0 <= k < NU:
                            Y(Ry, k)
                    recs.append(Ry)
                    for k in (i + 2, i + 3):
                        if 0 <= k < NU:
                            R = Rec(); X(R, k); recs.append(R)
                    interleave(S, recs)
                S.flush()

        def ret_pass(S, h, sample, host_ln=False):
            g128 = math.exp(LG[h] * 128.0)
            g8 = math.exp(LG[h] * 8.0)
            last_head = (h == RH - 1)
            with ExitStack() as ph:
                psA = P.ps("psA", [128, 512], F32, ph)
                psB = P.ps("psB", [128, 512], F32, ph)
                psC = P.ps("psC", [128, 512], F32, ph)
                psT = P.ps("psT", [128, 8, 128], BF16, ph)
                psAT = P.ps("psAT", [128, 512], F32, ph)
                psO = P.ps("psO", [128, 512], F32, ph)
                psdS = P.ps("psdS", [128, 2, 512], F32, ph)
                wih = P.sb("wih", [128, 8, 1536], BF16, ph)
                woh = P.sb("woh", [128, 4, D], BF16, ph)
                xp = P.sb("xp", [128, 2, 512], F32, ph)
                rsc = P.sb("rsc", [128, 24], F32, ph)
                rmask = P.sb("rmask", [128, 128], BF16, ph)
                rA = P.sb("rA", [128, 2, 256], F32, ph)
                rB = P.sb("rB", [128, 2, 256], F32, ph)
                rR = P.sb("rR", [128, 2, 256], F32, ph)
                qk3 = P.sb("qk3", [128, 2, 3, 256], BF16, ph)
                qkT = P.sb("qkT", [128, 2, 4, 128], BF16, ph)
                vb = P.sb("vb", [128, 2, 512], BF16, ph)
                attT = P.sb("attT", [128, 2, 128], BF16, ph)
                on = P.sb("on", [128, 2, 512], F32, ph)
                sgt = P.sb("sgt", [128, 2, 512], F32, ph)
                yb = P.sb("yb", [128, 2, 512], BF16, ph)
                yT = P.sb("yT", [128, 2, 4, 128], BF16, ph)
                lnt = ln_bufs(ph)
                wi = w["ret_w_in"]
                for ci, (c0, n, d0) in enumerate(((h * 256, 256, 0), (1024 + h * 256, 256, 256),
                                                  (2048 + h * 512, 512, 512), (4096 + h * 512, 512, 1024))):
                    S.dma("pool", "wi%d" % ci, DMA(wih[:, :, d0:d0 + n],
                                                  wi[:, c0:c0 + n].rearrange("(k p) n -> p k n", p=128)),
                          w=[("wih", ci)])
                S.dma("pool", "woh", DMA(woh[:], w["ret_w_o"][h * 512:(h + 1) * 512, :].rearrange("(c p) n -> p c n", p=128)),
                      w=[("woh",)])
                S.dma("sp", "rsc", DMA(rsc[:], w["rscal"][:, :]), w=[("rsc",)])
                S.dma("pool", "rmask", DMA(rmask[:], w["masks"][3 if sample else 1, :, 0:128]), w=[("rmask",)])
                rbase = 12 if sample else 0
                xi_ap = rsc[:, rbase + 0 * 4 + h:rbase + 0 * 4 + h + 1]
                kt_ap = rsc[:, rbase + 1 * 4 + h:rbase + 1 * 4 + h + 1]
                kz_ap = rsc[:, rbase + 2 * 4 + h:rbase + 2 * 4 + h + 1]
                if not sample:
                    Sf = P.sb("Sf", [128, 2, 512], F32, ph)
                    Sbf = P.sb("Sbf", [128, 2, 512], BF16, ph)
                    Fin = P.sb("Fin", [128, 2, 2, 512], F32, ph)
                    coef = P.sb("coef", [128, 32], F32, ph)
                    S.dma("sp", "coef", DMA(coef[:], w["coef"][:, :]), w=[("coef",)])
                    for c2 in range(N_CORES):
                        fb = c2 % 2
                        S.dma("sp", "fin%d" % fb, DMA(Fin[:, fb, :, :], w["fall"][c2, h].rearrange("(dc p) e -> p dc e", p=128)),
                              w=[("Fin", fb)])
                        cf = coef[:, c2 * 4 + h:c2 * 4 + h + 1]
                        if c2 == 0:
                            S.dve(TS(Sf[:], Fin[:, fb, :, :], cf, ALU.mult), r=[("Fin", fb), ("coef",)], w=[("Sf",)])
                        else:
                            S.dve(STT(Sf[:], Fin[:, fb, :, :], cf, Sf[:], ALU.mult, ALU.add),
                                  r=[("Fin", fb), ("coef",)], w=[("Sf",)])
                    S.pool(CP(Sbf[:], Sf[:]), r=[("Sf",)], w=[("Sbf",)])
                    tiles = list(range(npt))
                else:
                    SR = 3 if host_ln else 5
                    Sb = P.sb("Sb", [128, SR, 2, 512], F32, ph)
                    if host_ln:
                        gbp = P.sb("gbp", [128, 2, D], F32, ph)
                        xbp = P.sb("xbp", [128, 2, D], BF16, ph)
                        lntp = ln_bufs(ph, ring=4)
                        S.dma("sp", "gbp", DMA(gbp[:, 0, :], w["ln_g"][4:5, :].partition_broadcast(128)), w=[("gbp", 0)])
                        S.dma("sp", "gbp2", DMA(gbp[:, 1, :], w["ln_b"][4:5, :].partition_broadcast(128)), w=[("gbp", 1)])
                    Sbb = P.sb("Sbb", [128, 2, 2, 512], BF16, ph)
                    qTm = P.sb("qTm", [128, 2, 2, 128], BF16, ph)
                    kzm = P.sb("kzm", [128, 2, 256], BF16, ph)
                    ind = P.sb("ind", [128, NB], F32, ph)
                    indT = P.sb("indT", [128, NB, 128], BF16, ph)
                    S.dma("sp", "ind", DMA(ind[:], w["ind"][:, :]), w=[("ind",)])
                    S.dma("pool", "indT", DMA(indT[:], w["indT"].rearrange("p (b q) -> p b q", b=NB)), w=[("indT",)])
                    tiles = [cfg.samp]

                for ti, t in enumerate(tiles):
                    cs = slice(t * 128, (t + 1) * 128)
                    b = ti % 2
                    S.pe(MM8(psA[:, :], (lambda k, cs=cs: xT[:, k, cs]), (lambda k: wih[:, k, 0:512])),
                         r=[("xT", t), ("wih", 0), ("wih", 1)], w=[("psa",)])
                    S.pe(MM8(psB[:, :], (lambda k, cs=cs: xT[:, k, cs]), (lambda k: wih[:, k, 512:1024])),
                         r=[("xT", t), ("wih", 2)], w=[("psb",)])
                    S.pe(MM8(psC[:, :], (lambda k, cs=cs: xT[:, k, cs]), (lambda k: wih[:, k, 1024:1536])),
                         r=[("xT", t), ("wih", 3)], w=[("psc",)])
                    S.dma("sp", "xp%d" % b, DMA(xp[:, b, :], w["xpos"][cs, :]), w=[("xp", b)])
                    xpos_rot(S, psA[:, :], 2, xp[:, b, :], rA, rB, rR, ("xp", b), [("psa",)])
                    S.act(AMUL(qk3[:, b, 0, :], rR[:, 0, :], xi_ap), r=[("rR",), ("rsc",)], w=[("qk3", b, 0)])
                    S.act(AMUL(qk3[:, b, 1, :], rR[:, 1, :], kt_ap), r=[("rR",), ("rsc",)], w=[("qk3", b, 1)])
                    S.act(AMUL(qk3[:, b, 2, :], rR[:, 1, :], kz_ap), r=[("rR",), ("rsc",)], w=[("qk3", b, 2)])
                    S.act(ACP(vb[:, b, :], psB[:, :]), w=[("vb", b)], x=[("psb",)])
                    S.pe(TRS([psT[:, a, :] for a in range(4)],
                             [qk3[:, b, 0, 0:128], qk3[:, b, 0, 128:256], qk3[:, b, 1, 0:128], qk3[:, b, 1, 128:256]],
                             ident[:]), r=[("qk3", b, 0), ("qk3", b, 1), ("ident",)], w=[("psT", 0)])
                    S.dve(CP(qkT[:, b, :, :], psT[:, 0:4, :]), w=[("qkT", b)], x=[("psT", 0)])

                    def at(e, b=b):
                        ins = None
                        for dc in range(2):
                            ins = e.matmul(psAT[:, 0:128], qkT[:, b, 2 + dc, :], qkT[:, b, dc, :],
                                           start=(dc == 0), stop=(dc == 1))
                        return ins
                    S.pe(at, r=[("qkT", b)], w=[("psat",)])
                    S.dve(TT(attT[:, b, :], psAT[:, 0:128], rmask[:, :], ALU.mult), r=[("rmask",)],
                          w=[("attT", b)], x=[("psat",)])
                    if not sample:
                        def om(e, b=b):
                            e.matmul(psO[:, :], attT[:, b, :], vb[:, b, :], start=True, stop=False)
                            ins = None
                            for dc in range(2):
                                ins = e.matmul(psO[:, :], qkT[:, b, dc, :], Sbf[:, dc, :], start=False, stop=(dc == 1))
                            return ins
                        S.pe(om, r=[("attT", b), ("vb", b), ("qkT", b), ("Sbf",)], w=[("pso",)])

                        def ds(e, b=b):
                            ins = None
                            for dc in range(2):
                                ins = e.matmul(psdS[:, dc, :], qk3[:, b, 2, dc * 128:(dc + 1) * 128], vb[:, b, :],
                                               start=True, stop=True)
                            return ins
                        S.pe(ds, r=[("qk3", b, 2), ("vb", b)], w=[("psds",)])
                        S.dve(STT(Sf[:], Sf[:], g128, psdS[:, :, :], ALU.mult, ALU.add), w=[("Sf",)], x=[("psds",)])
                        S.pool(CP(Sbf[:], Sf[:]), r=[("Sf",)], w=[("Sbf",)])
                        if t == npt - 1:
                            S.dma("sp", "stp", DMA(stp_o[h].rearrange("(dc p) e -> p dc e", p=128), Sf[:]), r=[("Sf",)])
                    else:
                        S.pe((lambda b=b: lambda e: e.matmul(psO[:, :], attT[:, b, :], vb[:, b, :],
                                                             start=True, stop=False))(),
                             r=[("attT", b), ("vb", b)], w=[("pso",)])
                        def U(R, sq):
                            sb_, s3 = sq % 2, sq % SR
                            R.dma("sp", "sin%d" % s3, DMA(Sb[:, s3, :, :],
                                                          w["state"][sq, h].rearrange("(dc p) e -> p dc e", p=128)),
                                  w=[("Sb", s3)])
                            R.act(ACP(Sbb[:, sb_, :, :], Sb[:, s3, :, :]), r=[("Sb", s3)], w=[("Sbb", sb_)])
                            R.dve(TT(qTm[:, sb_, :, :], qkT[:, b, 0:2, :],
                                     indT[:, sq, :].unsqueeze(1).broadcast_to([128, 2, 128]), ALU.mult),
                                  r=[("qkT", b), ("indT",)], w=[("qTm", sb_)])
                            R.act(AMUL(kzm[:, sb_, :], qk3[:, b, 2, :], ind[:, sq:sq + 1]),
                                  r=[("qk3", b, 2), ("ind",)], w=[("kzm", sb_)])

                        def V(R, sq):
                            sb_, s3 = sq % 2, sq % SR

                            def om2(e):
                                ins = None
                                for dc in range(2):
                                    ins = e.matmul(psO[:, :], qTm[:, sb_, dc, :], Sbb[:, sb_, dc, :], start=False,
                                                   stop=(sq == NB - 1 and dc == 1))
                                return ins
                            R.pe(om2, r=[("qTm", sb_), ("Sbb", sb_)], w=[("pso",)])
                            if sq % 2 == 0:
                                def ds2(e):
                                    ins = None
                                    for dc in range(2):
                                        ins = e.matmul(psdS[:, dc, :], kzm[:, sb_, dc * 128:(dc + 1) * 128], vb[:, b, :],
                                                       start=True, stop=True)
                                    return ins
                                R.pe(ds2, r=[("kzm", sb_), ("vb", b)], w=[("psds",)])
                                R.dve(STT(Sb[:, s3, :, :], Sb[:, s3, :, :], g8, psdS[:, :, :], ALU.mult, ALU.add),
                                      w=[("Sb", s3)], x=[("psds",)])
                            else:
                                for dc, (bank, tag) in enumerate(((psA, ("psa",)), (psB, ("psb",)))):
                                    R.pe((lambda dc=dc, bank=bank: lambda e: e.matmul(
                                        bank[:, :], kzm[:, sb_, dc * 128:(dc + 1) * 128], vb[:, b, :],
                                        start=True, stop=True))(), r=[("kzm", sb_), ("vb", b)], w=[tag])
                                    R.dve(STT(Sb[:, s3, dc, :], Sb[:, s3, dc, :], g8, bank[:, :], ALU.mult, ALU.add),
                                          w=[("Sb", s3)], x=[tag])
                            R.dma("pool", "sout%d" % s3, DMA(sts_o[sq, h].rearrange("(dc p) e -> p dc e", p=128),
                                                             Sb[:, s3, :, :]), r=[("Sb", s3)])

                        def ln_steps(s):
                            out = []
                            if not host_ln:
                                return out
                            if 0 <= s < npt:
                                R = Rec(); ln_a1(R, s, s, lntp); out.append(R)
                            if 0 <= s - 1 < npt:
                                R = Rec(); ln_a2(R, s - 1, s - 1, gbp, lntp, "gbp"); out.append(R)
                            if 0 <= s - 2 < npt:
                                R = Rec(); ln_b(R, s - 2, s - 2, xbp, [psT]); out.append(R)
                            return out

                        for i in range(-1, max(NB, npt + 2)):
                            recs = []
                            if 0 <= i < NB:
                                R = Rec(); V(R, i); recs.append(R)
                            if i + 1 < NB:
                                R = Rec(); U(R, i + 1); recs.append(R)
                            recs += ln_steps(i + 1)
                            interleave(S, recs)
                    j = ti % 3
                    sc = lnt["sc"]
                    norm_ops(S, j, lnt, psO[:, :], None, ([], [("pso",)]), None)
                    S.act(ACTF(on[:, b, :], psO[:, :], AF.Identity, bias=sc[:, j, 2:3], scale=sc[:, j, 1:2]),
                          r=[("sc", j, 1), ("sc", j, 2)], w=[("on", b)], x=[("pso",)])
                    S.act(ACTF(sgt[:, b, :], psC[:, :], AF.Silu), w=[("sgt", b)], x=[("psc",)])
                    S.pool(TT(yb[:, b, :], on[:, b, :], sgt[:, b, :], ALU.mult), r=[("on", b), ("sgt", b)], w=[("yb", b)])
                    S.pe(TRS([psT[:, 4 + c, :] for c in range(4)], [yb[:, b, c * 128:(c + 1) * 128] for c in range(4)],
                             ident[:]), r=[("yb", b), ("ident",)], w=[("psT", 0)])
                    S.dve(CP(yT[:, b, :, :], psT[:, 4:8, :]), w=[("yT", b)], x=[("psT", 0)])

                    def op(e, b=b):
                        ins = None
                        for dh, bank in enumerate((psA, psB)):
                            for c in range(4):
                                ins = e.matmul(bank[:, :], yT[:, b, c, :], woh[:, c, dh * 512:(dh + 1) * 512],
                                               start=(c == 0), stop=(c == 3))
                        return ins
                    S.pe(op, r=[("yT", b), ("woh",)], w=[("psa",), ("psb",)])
                    S.dve(TT(x[:, t, 0:512], psA[:, :], x[:, t, 0:512], ALU.add), w=[("x", t, 0)], x=[("psa",)])
                    S.dve(TT(x[:, t, 512:1024], psB[:, :], x[:, t, 512:1024], ALU.add), w=[("x", t, 1)], x=[("psb",)])
                S.flush()

        def ret_prompt_all(S):
            with ExitStack() as ph:
                psQK = P.ps("psQK", [128, 512], F32, ph)
                psVG = P.ps("psVG", [128, 512], F32, ph)
                psT = P.ps("psT", [128, 8, 128], BF16, ph)
                psAT = P.ps("psAT", [128, 512], F32, ph)
                psO = P.ps("psO", [128, 512], F32, ph)
                psdS = P.ps("psdS", [128, 2, 512], F32, ph)
                psY = P.ps("psY", [128, 512], F32, ph)
                wih = P.sb("wih", [128, 8, 1536], BF16, ph)
                woh = P.sb("woh", [128, 4, D], BF16, ph)
                xp = P.sb("xp", [128, 2, 1024], F32, ph)
                rmask = P.sb("rmask", [128, 128], BF16, ph)
                rA = P.sb("rA", [128, 2, 256], F32, ph)
                rB = P.sb("rB", [128, 2, 256], F32, ph)
                qk2 = P.sb("qk2", [128, 4, 2, 256], BF16, ph)
                qkT = P.sb("qkT", [128, 3, 4, 128], BF16, ph)
                vb = P.sb("vb", [128, 4, 512], BF16, ph)
                attT = P.sb("attT", [128, 3, 128], BF16, ph)
                sgt = P.sb("sgt", [128, 4, 512], F32, ph)
                osb = P.sb("osb", [128, 3, 512], F32, ph)
                yb = P.sb("yb", [128, 2, 512], BF16, ph)
                yT = P.sb("yT", [128, 2, 4, 128], BF16, ph)
                Uf = P.sb("Uf", [128, 2, 2, 512], F32, ph)
                Sbf = P.sb("Sbf", [128, 2, 2, 512], BF16, ph)
                Sout = P.sb("Sout", [128, 2, 512], F32, ph)
                Fin = P.sb("Fin", [128, 2, 2, 512], F32, ph)
                coef = P.sb("coef", [128, 32], F32, ph)
                lnt = ln_bufs(ph)
                wi = w["ret_w_in"]
                G128 = [math.exp(LG[h] * 128.0) for h in range(RH)]

                def load_wih(S, h):
                    for ci, (c0, n, d0) in enumerate(((h * 256, 256, 0), (1024 + h * 256, 256, 256),
                                                      (2048 + h * 512, 512, 512), (4096 + h * 512, 512, 1024))):
                        S.dma("pool", "wi%d" % ci, DMA(wih[:, :, d0:d0 + n],
                                                      wi[:, c0:c0 + n].rearrange("(k p) n -> p k n", p=128)),
                              w=[("wih", ci)])

                def load_woh(S, h):
                    S.dma("pool", "woh", DMA(woh[:], w["ret_w_o"][h * 512:(h + 1) * 512, :].rearrange("(c p) n -> p c n", p=128)),
                          w=[("woh",)])

                def s_init_parts(h):
                    hp = h % 2
                    parts = []

                    def ld(R, c2):
                        fb = c2 % 2
                        R.dma("sp", "fin%d" % fb, DMA(Fin[:, fb], w["fall"][c2, h].rearrange("(dc p) e -> p dc e", p=128)),
                              w=[("Fin", fb)])
                    R = Rec(); ld(R, 0); ld(R, 1); parts.append(R)
                    for c2 in range(N_CORES):
                        R = Rec()
                        fb = c2 % 2
                        cf = coef[:, c2 * 4 + h:c2 * 4 + h + 1]
                        if c2 == 0:
                            R.dve(TS(Uf[:, hp], Fin[:, fb], cf, ALU.mult), r=[("Fin", fb), ("coef",)], w=[("Uf", hp)])
                        else:
                            R.dve(STT(Uf[:, hp], Fin[:, fb], cf, Uf[:, hp], ALU.mult, ALU.add),
                                  r=[("Fin", fb), ("coef",)], w=[("Uf", hp)])
                        if c2 + 2 < N_CORES:
                            ld(R, c2 + 2)
                        parts.append(R)
                    R = Rec()
                    R.act(ACP(Sbf[:, hp], Uf[:, hp]), r=[("Uf", hp)], w=[("Sbf", hp)])
                    R.act(AMUL(Uf[:, hp], Uf[:, hp], 1.0 / G128[h]), r=[("Sbf", hp)], w=[("Uf", hp)])
                    parts.append(R)
                    return parts

                S.dma("pool", "rmask", DMA(rmask[:], w["masks"][1, :, 0:128]), w=[("rmask",)])
                S.dma("sp", "coef", DMA(coef[:], w["coef"][:, :]), w=[("coef",)])
                load_wih(S, 0)
                load_woh(S, 0)
                interleave(S, s_init_parts(0)[0:1])
                for R_ in s_init_parts(0)[1:]:
                    interleave(S, [R_])

                def stA1(R, u):
                    h, t = divmod(u, npt)
                    cs = slice(t * 128, (t + 1) * 128)
                    b2, b4 = u % 2, u % 4
                    R.pe(MM8(psQK[:, :], (lambda k: xT[:, k, cs]), (lambda k: wih[:, k, 0:512])),
                         r=[("xT", t), ("wih", 0), ("wih", 1)], w=[("psqk",)])
                    R.dma("sp", "xp%d" % b2, DMA(xp[:, b2, :], w["xposh"][h, cs, :]), w=[("xp", b2)])
                    tab = xp[:, b2, :]
                    v3 = psQK[:, :].rearrange("p (a c) -> p a c", a=2)
                    v4 = psQK[:, :].rearrange("p (a i two) -> p a i two", a=2, two=2)
                    bb4 = rB[:, :, :].rearrange("p a (i two) -> p a i two", two=2)
                    R.dve(TT(rA[:, :, :], v3, tab[:, 0:512].rearrange("p (a c) -> p a c", a=2), ALU.mult),
                          r=[("xp", b2)], w=[("rA",)], x=[("psqk",)])
                    R.pe(MM8(psVG[:, :], (lambda k: xT[:, k, cs]), (lambda k: wih[:, k, 512:1024])),
                         r=[("xT", t), ("wih", 2)], w=[("psvg",)])
                    R.dve(TT(bb4[:, :, :, 0], v4[:, :, :, 1], tab[:, 512:768].rearrange("p (a c) -> p a c", a=2), ALU.mult),
                          r=[("xp", b2)], w=[("rB", 0)], x=[("psqk",)])
                    R.act(ACP(vb[:, b4, :], psVG[:, :]), w=[("vb", b4)], x=[("psvg",)])
                    R.dve(TT(bb4[:, :, :, 1], v4[:, :, :, 0], tab[:, 768:1024].rearrange("p (a c) -> p a c", a=2), ALU.mult),
                          r=[("xp", b2)], w=[("rB", 1)], x=[("psqk",)])
                    R.pe(MM8(psVG[:, :], (lambda k: xT[:, k, cs]), (lambda k: wih[:, k, 1024:1536])),
                         r=[("xT", t), ("wih", 3)], w=[("psvg",)])
                    R.pool(TT(qk2[:, b4, :, :], rA[:, :, :], rB[:, :, :], ALU.add),
                           r=[("rA",), ("rB", 0), ("rB", 1)], w=[("qk2", b4)])
                    R.act(ACTF(sgt[:, b4, :], psVG[:, :], AF.Silu), w=[("sgt", b4)], x=[("psvg",)])

                def stA2(R, u):
                    b3, b4 = u % 3, u % 4
                    R.pe(TRS([psT[:, a, :] for a in range(4)],
                             [qk2[:, b4, 0, 0:128], qk2[:, b4, 0, 128:256], qk2[:, b4, 1, 0:128], qk2[:, b4, 1, 128:256]],
                             ident[:]), r=[("qk2", b4), ("ident",)], w=[("psT", 0)])
                    R.dve(CP(qkT[:, b3, :, :], psT[:, 0:4, :]), w=[("qkT", b3)], x=[("psT", 0)])

                    def at(e):
                        ins = None
                        for dc in range(2):
                            ins = e.matmul(psAT[:, 0:128], qkT[:, b3, 2 + dc, :], qkT[:, b3, dc, :],
                                           start=(dc == 0), stop=(dc == 1))
                        return ins
                    R.pe(at, r=[("qkT", b3)], w=[("psat",)])
                    R.dve(TT(attT[:, b3, :], psAT[:, 0:128], rmask[:, :], ALU.mult), r=[("rmask",)],
                          w=[("attT", b3)], x=[("psat",)])

                def stB(R, u):
                    h, t = divmod(u, npt)
                    hp = h % 2
                    g128 = G128[h]
                    b3, b4 = u % 3, u % 4

                    def om(e):
                        e.matmul(psO[:, :], attT[:, b3, :], vb[:, b4, :], start=True, stop=False)
                        ins = None
                        for dc in range(2):
                            ins = e.matmul(psO[:, :], qkT[:, b3, dc, :], Sbf[:, hp, dc, :], start=False, stop=(dc == 1))
                        return ins
                    R.pe(om, r=[("attT", b3), ("vb", b4), ("qkT", b3), ("Sbf", hp)], w=[("pso",)])

                    def ds(e):
                        ins = None
                        for dc in range(2):
                            ins = e.matmul(psdS[:, dc, :], qk2[:, b4, 1, dc * 128:(dc + 1) * 128], vb[:, b4, :],
                                           start=True, stop=True)
                        return ins
                    R.pe(ds, r=[("qk2", b4), ("vb", b4)], w=[("psds",)])
                    R.dve(STT(Uf[:, hp], Uf[:, hp], g128, psdS[:, :, :], ALU.mult, ALU.add), w=[("Uf", hp)], x=[("psds",)])
                    R.act(ACP(osb[:, b3, :], psO[:, :]), w=[("osb", b3)], x=[("pso",)])
                    R.act(AMUL(Sbf[:, hp], Uf[:, hp], g128), r=[("Uf", hp)], w=[("Sbf", hp)])
                    if t == npt - 1:
                        R.act(AMUL(Sout[:], Uf[:, hp], g128), r=[("Uf", hp)], w=[("Sout",)])
                        R.dma("sp", "stp", DMA(stp_o[h].rearrange("(dc p) e -> p dc e", p=128), Sout[:]), r=[("Sout",)])

                def stC1(R, u):
                    b2, b3, b4 = u % 2, u % 3, u % 4
                    j = u % 3
                    st, mv, sc = lnt["stats"], lnt["mv"], lnt["sc"]
                    R.dve(lambda e: e.bn_stats(out=st[:, j, 0, :], in_=osb[:, b3, :]), r=[("osb", b3)], w=[("st", j, 0)])
                    R.dve(lambda e: e.bn_aggr(out=mv[:, j, :], in_=st[:, j, 0, :]), r=[("st", j, 0)], w=[("mv", j)])
                    R.act(ACTF(sc[:, j, 0:1], mv[:, j, 1:2], AF.Sqrt, bias=lnt["eps"][:, 0:1]),
                          r=[("mv", j), ("eps",)], w=[("sc", j, 0)])
                    R.dve(lambda e: e.reciprocal(out=sc[:, j, 1:2], in_=sc[:, j, 0:1]), r=[("sc", j, 0)], w=[("sc", j, 1)])
                    R.dve(lambda e: e.tensor_scalar(out=osb[:, b3, :], in0=osb[:, b3, :], scalar1=mv[:, j, 0:1],
                                                    scalar2=sc[:, j, 1:2], op0=ALU.subtract, op1=ALU.mult),
                          r=[("mv", j), ("sc", j, 1)], w=[("osb", b3)])
                    R.pool(TT(yb[:, b2, :], osb[:, b3, :], sgt[:, b4, :], ALU.mult),
                           r=[("osb", b3), ("sgt", b4)], w=[("yb", b2)])

                def stC2(R, u):
                    h, t = divmod(u, npt)
                    b2 = u % 2
                    R.pe(TRS([psT[:, 4 + c, :] for c in range(4)], [yb[:, b2, c * 128:(c + 1) * 128] for c in range(4)],
                             ident[:]), r=[("yb", b2), ("ident",)], w=[("psT", 0)])
                    R.dve(CP(yT[:, b2, :, :], psT[:, 4:8, :]), w=[("yT", b2)], x=[("psT", 0)])
                    for dh in range(2):
                        def op(e, dh=dh):
                            ins = None
                            for c in range(4):
                                ins = e.matmul(psY[:, :], yT[:, b2, c, :], woh[:, c, dh * 512:(dh + 1) * 512],
                                               start=(c == 0), stop=(c == 3))
                            return ins
                        R.pe(op, r=[("yT", b2), ("woh",)], w=[("psy",)])
                        R.dve(TT(x[:, t, dh * 512:(dh + 1) * 512], psY[:, :], x[:, t, dh * 512:(dh + 1) * 512], ALU.add),
                              w=[("x", t, dh)], x=[("psy",)])

                NU = RH * npt
                stages = [(stB, 0), (stA2, 1), (stC1, -1), (stA1, 2), (stC2, -2)]
                init_at = {}
                for h in range(1, RH):
                    for k, R_ in enumerate(s_init_parts(h)):
                        init_at[h * npt - 12 + k] = R_
                for i in range(-2, NU + 3):
                    recs = []
                    if i in init_at:
                        recs.append(init_at[i])
                    for fn_, off in stages:
                        u = i + off
                        if not (0 <= u < NU):
                            continue
                        h, t = divmod(u, npt)
                        if t == 0 and h > 0:
                            if fn_ is stA1:
                                load_wih(S, h)
                            if fn_ is stC2:
                                load_woh(S, h)
                        R = Rec()
                        fn_(R, u)
                        recs.append(R)
                    interleave(S, recs)
                S.flush()

        stop = cfg.stop_after
        if stage == 1:
            ffn_phase(S, w["ffn1_wg"][0], w["ffn1_wu"][0], w["ffn1_wd"][0], 0, True)
            if stop == "ffn1":
                dump_x("dbg_x")
                return nc
            attn_phases(S)
            if stop == "attn":
                dump_x("dbg_x")
                return nc
            ffn_phase(S, w["ffn2_wg"][0], w["ffn2_wu"][0], w["ffn2_wd"][0], 2, False, pre_ln=1)
            ffn_phase(S, w["ffn1_wg"][1], w["ffn1_wu"][1], w["ffn1_wd"][1], 3, False)
            sweep1(S)
            for t in range(npt + 1):
                S.dma("sp", "tile%d" % t, DMA(xmid_o[t * 128:(t + 1) * 128, :], x[:, t, :]),
                      r=[("x", t, 0), ("x", t, 1)])
            S.flush()
        else:
            ret_prompt_all(S)
            for h in range(RH):
                ret_pass(S, h, True, host_ln=(h == 0))
            if stop == "ret":
                dump_x("dbg_x")
                return nc
            ffn_phase(S, w["ffn2_wg"][1], w["ffn2_wu"][1], w["ffn2_wd"][1], 5, False, final_out=yout, pre_ln=4,
                      pre_tiles=[cfg.samp])
    return nc


def _tables(c, cfg):
    npt, NT = cfg.npt, cfg.nt
    ntl = npt * 128
    r = np.arange(128)
    pos = np.zeros(NT * 128, np.float32)
    pos[0:ntl] = c * ntl + np.arange(ntl)
    pos[cfg.samp * 128:(cfg.samp + 1) * 128] = PAST + (r % 8)
    pos[cfg.halo * 128:(cfg.halo + 1) * 128] = (c * ntl - 128 + r) if c > 0 else r
    pos = pos.astype(np.float32)
    inv = (np.float32(10000.0) ** (-np.arange(0, HD, 2, dtype=np.float32) / np.float32(HD))).astype(np.float32)
    ang = (pos[:, None] * inv[None, :]).astype(np.float32)
    cs, sn = np.cos(ang).astype(np.float32), np.sin(ang).astype(np.float32)
    rope = np.concatenate([cs, cs, -sn, sn], axis=1).astype(np.float32)
    inv2 = (np.float32(1.0) / (np.float32(10000.0) ** np.linspace(0.0, 1.0, RQK // 2, dtype=np.float32))).astype(np.float32)
    ang2 = (pos[:, None] * inv2[None, :]).astype(np.float32)
    c2, s2 = np.cos(ang2).astype(np.float32), np.sin(ang2).astype(np.float32)
    xpos = np.concatenate([np.repeat(c2, 2, axis=1), -s2, s2], axis=1).astype(np.float32)
    k = r[:, None]
    q = r[None, :]
    m0 = (k > q).astype(np.float32)
    m1 = (k <= q).astype(np.float32)
    m2 = np.zeros_like(m0) if c == 0 else m0
    m3 = ((k // 8 == q // 8) & (k % 8 <= q % 8)).astype(np.float32)
    col = np.arange(512)[None, :]
    m4 = (k > (col % 8)).astype(np.float32)
    masks = np.stack([np.tile(m0, (1, 4)), np.tile(m1, (1, 4)), np.tile(m2, (1, 4)), np.tile(m3, (1, 4)), m4]).astype(np.float32)
    gam = np.array(GAM, np.float64)
    sc = RQK ** -0.5
    zeta1 = np.zeros((128, npt * 4), np.float64)
    for t in range(npt):
        for h in range(RH):
            zeta1[:, t * 4 + h] = sc * gam[h] ** (ntl - 1 - (t * 128 + r))
    rscal = np.zeros((128, 24), np.float64)
    for h in range(RH):
        rscal[:, 0 + h] = gam[h] ** (r + 1.0)
        rscal[:, 4 + h] = sc * gam[h] ** (-(r + 1.0))
        rscal[:, 8 + h] = sc * gam[h] ** (127.0 - r)
        i8 = (r % 8).astype(np.float64)
        rscal[:, 12 + h] = gam[h] ** (i8 + 1.0)
        rscal[:, 16 + h] = sc * gam[h] ** (-(i8 + 1.0))
        rscal[:, 20 + h] = sc * gam[h] ** (7.0 - i8)
    coef = np.zeros((128, 32), np.float64)
    for c2_ in range(N_CORES):
        for h in range(RH):
            if c2_ < c:
                coef[:, c2_ * 4 + h] = gam[h] ** (float(ntl) * (c - 1 - c2_))
    c2d = np.repeat(c2[0:ntl].astype(np.float64), 2, axis=1)
    s2d = s2[0:ntl].astype(np.float64)
    pl = (np.arange(ntl) % 128).astype(np.float64)
    xposh = np.zeros((RH, ntl, 1024), np.float32)
    for h in range(RH):
        xi = (gam[h] ** (pl + 1.0))[:, None]
        kt = (sc * gam[h] ** (-(pl + 1.0)))[:, None]
        xposh[h] = np.concatenate([c2d * xi, c2d * kt, -s2d * xi, -s2d * kt, s2d * xi, s2d * kt], axis=1)
    ind = (r[:, None] // 8 == np.arange(16)[None, :]).astype(np.float32)
    indT = np.tile((np.arange(16)[:, None] == (r[None, :] // 8)).astype(np.float32).reshape(1, 16 * 128), (128, 1))
    return dict(rope=rope, xpos=xpos, masks=masks, zeta1=zeta1.astype(np.float32), rscal=rscal.astype(np.float32),
                coef=coef.astype(np.float32), ind=ind, indT=indT.astype(np.float32), xposh=xposh,
                identf=np.eye(128, dtype=np.float32))


_PROGS = {}


def _prog(stage):
    if stage not in _PROGS:
        _PROGS[stage] = build_program(Cfg(), stage)
    return _PROGS[stage]


def kernel(x_prompt, x_sample, cache_k_win, cache_v_win, state_ret,
           ffn1_w_gate, ffn1_w_up, ffn1_w_down, ffn2_w_gate, ffn2_w_up, ffn2_w_down,
           ln_g, ln_b, attn_w_qkv, attn_w_o, attn_sinks, ret_w_in, ret_w_o):
    cfg = Cfg()
    f32 = lambda a: np.ascontiguousarray(np.asarray(a, dtype=np.float32))
    xp = f32(x_prompt)[0]
    xs = f32(x_sample)
    ck = f32(cache_k_win)[0].reshape(128, 128, 256)
    cv = f32(cache_v_win)[0].reshape(128, 128, 256)
    st = f32(state_ret)[0]
    lng = f32(ln_g).reshape(6, D)
    lnb = f32(ln_b).reshape(6, D)
    wts1 = {"ffn1_wg": f32(ffn1_w_gate), "ffn1_wu": f32(ffn1_w_up), "ffn1_wd": f32(ffn1_w_down),
            "ffn2_wg": f32(ffn2_w_gate), "ffn2_wu": f32(ffn2_w_up), "ffn2_wd": f32(ffn2_w_down),
            "attn_w_qkv": f32(attn_w_qkv)[0], "attn_w_o": f32(attn_w_o)[0],
            "sinks2": np.ascontiguousarray(f32(attn_sinks)[0].reshape(8, 2).T),
            "ret_w_in": f32(ret_w_in)[0], "ln_g": lng, "ln_b": lnb}
    tabs = [_tables(c, cfg) for c in range(N_CORES)]
    ntl = cfg.npt * 128
    in1 = []
    for c in range(N_CORES):
        halo = xp[c * ntl - 128:c * ntl] if c > 0 else xp[0:128]
        xin = np.concatenate([xp[c * ntl:(c + 1) * ntl], xs[c * 16:(c + 1) * 16].reshape(128, D), halo], axis=0)
        m = dict(wts1)
        m.update(xin=np.ascontiguousarray(xin), cache_k=ck[c * 16:(c + 1) * 16], cache_v=cv[c * 16:(c + 1) * 16])
        for k_ in ("identf", "masks", "rope", "xpos", "zeta1"):
            m[k_] = tabs[c][k_]
        in1.append(m)
    r1 = run_bass_kernel_spmd(_prog(1), in1, core_ids=list(range(N_CORES))).results
    fall = np.ascontiguousarray(np.stack([r1[c]["F"] for c in range(N_CORES)], axis=0))
    wts2 = {"ffn2_wg": wts1["ffn2_wg"], "ffn2_wu": wts1["ffn2_wu"], "ffn2_wd": wts1["ffn2_wd"],
            "ret_w_in": wts1["ret_w_in"], "ret_w_o": f32(ret_w_o)[0], "ln_g": lng, "ln_b": lnb, "fall": fall}
    in2 = []
    for c in range(N_CORES):
        m = dict(wts2)
        m.update(xmid=r1[c]["xmid"], state=st[c * 16:(c + 1) * 16])
        for k_ in ("identf", "masks", "xpos", "rscal", "coef", "ind", "indT", "xposh"):
            m[k_] = tabs[c][k_]
        in2.append(m)
    r2 = run_bass_kernel_spmd(_prog(2), in2, core_ids=list(range(N_CORES))).results
    y_prompt = np.concatenate([r2[c]["y"][0:ntl] for c in range(N_CORES)], axis=0)[None]
    y_sample = np.concatenate([r2[c]["y"][ntl:ntl + 128].reshape(16, 8, D) for c in range(N_CORES)], axis=0)
    L = N_CORES - 1
    kwp = r1[L]["kwin_p"].reshape(1, 1, 128, NKV, HD)
    vwp = r1[L]["vwin_p"].reshape(1, 1, 128, NKV, HD)
    kws = np.concatenate([r1[c]["kwin_s"] for c in range(N_CORES)], axis=0).reshape(1, 128, 128, NKV, HD)
    vws = np.concatenate([r1[c]["vwin_s"] for c in range(N_CORES)], axis=0).reshape(1, 128, 128, NKV, HD)
    stp = r2[L]["state_p"].reshape(1, 1, RH, RQK, RV)
    sts = np.concatenate([r2[c]["state_s"] for c in range(N_CORES)], axis=0).reshape(1, 128, RH, RQK, RV)
    return (y_prompt.astype(np.float32), y_sample.astype(np.float32), kwp, vwp, kws, vws, stp, sts)
```

```python
import math
from contextlib import ExitStack

import numpy as np
import concourse.bass as bass
import concourse.mybir as mybir
from concourse.bass_utils import run_bass_kernel_spmd

F32 = mybir.dt.float32
BF16 = mybir.dt.bfloat16
AF = mybir.ActivationFunctionType
ALU = mybir.AluOpType
AX = mybir.AxisListType

D = 1024
FF = 2816
NFC = FF // 128
DEPTH = 2
ALPHA = (2.0 * DEPTH) ** 0.25
LN_EPS = 1e-5
N_CORES = 8
HD = 64
NH = 16
NKV = 4
WINDOW = 128
PAST = 16384
RH = 4
RQK = 256
RV = 512


class Op:
    __slots__ = ("eng", "fn", "deps", "dma", "token", "needs_inc", "idx", "xset")


class Sched:
    ENGS = ("pe", "act", "dve", "pool", "sp")

    def __init__(self, nc, stack, n_dma_sems=100):
        self.nc = nc
        self.sem = {e: stack.enter_context(nc.semaphore("s_" + e)) for e in self.ENGS}
        self.cnt = {e: 0 for e in self.ENGS}
        self.stack = stack
        self.dma_sem = {}
        self.dma_cnt = {}
        self.waited = {e: {} for e in self.ENGS}
        self.ops = []
        self.last_w = {}
        self.readers = {}
        self.all_dma_tokens = {}

    def _sem_for(self, key):
        if key not in self.dma_sem:
            self.dma_sem[key] = self.stack.enter_context(self.nc.semaphore("dq_" + key))
            self.dma_cnt[key] = 0
        return self.dma_sem[key]

    def add(self, eng, fn, r=(), w=(), dma=None, x=()):
        op = Op()
        op.eng, op.fn, op.dma, op.needs_inc, op.token = eng, fn, dma, False, None
        op.idx = len(self.ops)
        op.xset = frozenset(x)
        deps = set()
        for res in r:
            lw = self.last_w.get(res)
            if lw is not None:
                deps.add(lw)
        for res in w:
            lw = self.last_w.get(res)
            if lw is not None:
                deps.add(lw)
            for rd in self.readers.get(res, ()):
                deps.add(rd)
        for res in x:
            lw = self.last_w.get(res)
            if lw is not None and not (lw.eng == eng and res in lw.xset):
                deps.add(lw)
            for rd in self.readers.get(res, ()):
                deps.add(rd)
        deps.discard(op)
        op.deps = deps
        for res in r:
            self.readers.setdefault(res, []).append(op)
        for res in list(w) + list(x):
            self.last_w[res] = op
            self.readers[res] = []
        if dma is not None:
            self._sem_for(dma)
        self.ops.append(op)
        return op

    def pe(self, fn, r=(), w=(), x=()):
        return self.add("pe", fn, r, w, x=x)

    def act(self, fn, r=(), w=(), x=()):
        return self.add("act", fn, r, w, x=x)

    def dve(self, fn, r=(), w=(), x=()):
        return self.add("dve", fn, r, w, x=x)

    def pool(self, fn, r=(), w=(), x=()):
        return self.add("pool", fn, r, w, x=x)

    def dma(self, q, key, fn, r=(), w=()):
        return self.add(q, fn, r, w, dma=key)

    def flush(self, final_wait_all_dma=True):
        ops = self.ops
        if not ops:
            return
        for op in ops:
            for d in op.deps:
                if d.dma is None:
                    if d.eng == "pe" and op.eng == "pe":
                        continue
                    d.needs_inc = True
        for op in ops:
            if op.dma is not None:
                self.dma_cnt[op.dma] += 16
                op.token = (self.dma_sem[op.dma], self.dma_cnt[op.dma])
                self.all_dma_tokens[op.dma] = op.token
            elif op.needs_inc:
                self.cnt[op.eng] += 1
                op.token = (self.sem[op.eng], self.cnt[op.eng])
        plan = {e: [] for e in self.ENGS}
        for op in ops:
            need = {}
            for d in op.deps:
                if d.dma is None and d.eng == "pe" and op.eng == "pe":
                    continue
                s, v = d.token
                k = id(s)
                if k not in need or need[k][1] < v:
                    need[k] = (s, v)
            waits = []
            wd = self.waited[op.eng]
            for k, (s, v) in need.items():
                if wd.get(k, 0) >= v:
                    continue
                wd[k] = v
                waits.append((s, v))
            plan[op.eng].append((op, waits))
        if final_wait_all_dma:
            fin = []
            wd = self.waited["sp"]
            for key, (s, v) in self.all_dma_tokens.items():
                if wd.get(id(s), 0) >= v:
                    continue
                wd[id(s)] = v
                fin.append((s, v))
        else:
            fin = []

        def emit(engname):
            lst = plan[engname]

            def body(e):
                for op, waits in lst:
                    for s, v in waits:
                        e.wait_ge(s, v)
                    ins = op.fn(e)
                    if op.token is not None:
                        if op.dma is not None:
                            ins.then_inc(op.token[0], 16)
                        else:
                            ins.then_inc(op.token[0], 1)
                if engname == "sp":
                    for s, v in fin:
                        e.wait_ge(s, v)
            return body

        with self.nc.Block() as blk:
            if plan["pe"]:
                blk.tensor(emit("pe"))
            if plan["act"]:
                blk.scalar(emit("act"))
            if plan["dve"]:
                blk.vector(emit("dve"))
            if plan["pool"]:
                blk.gpsimd(emit("pool"))
            if plan["sp"] or fin:
                blk.sync(emit("sp"))
        self.ops = []
        self.last_w = {}
        self.readers = {}


ATTN_SCALE = 1.0 / math.sqrt(HD)
GAM = [1.0 - 2.0 ** (-5.0 - h) for h in range(RH)]
LG = [math.log(g) for g in GAM]


class Cfg:
    def __init__(self, npt=16, n_cores=N_CORES, G=6, stop_after=None):
        self.npt = npt
        self.samp = npt
        self.halo = npt + 1
        self.nt = npt + 2
        self.n_cores = n_cores
        self.G = G
        self.stop_after = stop_after

    def groups(self, with_halo):
        g = [(i, min(4, self.npt - i)) for i in range(0, self.npt, 4)]
        g.append((self.samp, 2 if with_halo else 1))
        return g


class Prog:
    def __init__(self, cfg):
        self.cfg = cfg
        self.nc = bass.Bass("TRN2", target_bir_lowering=False)
        self.stack = ExitStack()
        self.uid = 0

    def dram_in(self, name, shape, dt=F32):
        return self.nc.dram_tensor(name, list(shape), dt, kind="ExternalInput").ap()

    def dram_out(self, name, shape, dt=F32):
        return self.nc.dram_tensor(name, list(shape), dt, kind="ExternalOutput").ap()

    def sb(self, name, shape, dt, stack=None):
        self.uid += 1
        return (stack or self.stack).enter_context(
            self.nc.sbuf_tensor("%s_%d" % (name, self.uid), list(shape), dt))

    def ps(self, name, shape, dt, stack=None):
        self.uid += 1
        return (stack or self.stack).enter_context(
            self.nc.psum_tensor("%s_%d" % (name, self.uid), list(shape), dt))


def TT(out, in0, in1, op):
    return lambda e: e.tensor_tensor(out=out, in0=in0, in1=in1, op=op)


def STT(out, in0, scalar, in1, op0, op1):
    return lambda e: e.scalar_tensor_tensor(out=out, in0=in0, scalar=scalar, in1=in1, op0=op0, op1=op1)


def TS(out, in0, s1, op0):
    return lambda e: e.tensor_scalar(out=out, in0=in0, scalar1=s1, scalar2=None, op0=op0)


def CP(out, in_):
    return lambda e: e.tensor_copy(out=out, in_=in_)


def ACP(out, in_):
    return lambda e: e.copy(out=out, in_=in_)


def AMUL(out, in_, m):
    return lambda e: e.mul(out=out, in_=in_, mul=m)


def ACTF(out, in_, func, **kw):
    return lambda e: e.activation(out=out, in_=in_, func=func, **kw)


def DMA(out, in_):
    return lambda e: e.dma_start(out=out, in_=in_)


def MM8(out, lhs_of_k, rhs_of_k, n=8):
    def f(e):
        ins = None
        for k in range(n):
            ins = e.matmul(out, lhs_of_k(k), rhs_of_k(k), start=(k == 0), stop=(k == n - 1))
        return ins
    return f


def TRS(outs, ins_, ident):
    def f(e):
        ins = None
        for o, i in zip(outs, ins_):
            ins = e.transpose(o, i, ident)
        return ins
    return f


class _GBView:
    def __init__(self, t):
        self.t = t

    def __getitem__(self, k):
        return self.t[k]


class Rec:
    def __init__(self):
        self.ops = []

    def add(self, eng, fn, r=(), w=(), dma=None, x=()):
        self.ops.append((eng, fn, tuple(r), tuple(w), dma, tuple(x)))

    def pe(self, fn, r=(), w=(), x=()):
        self.add("pe", fn, r, w, x=x)

    def act(self, fn, r=(), w=(), x=()):
        self.add("act", fn, r, w, x=x)

    def dve(self, fn, r=(), w=(), x=()):
        self.add("dve", fn, r, w, x=x)

    def pool(self, fn, r=(), w=(), x=()):
        self.add("pool", fn, r, w, x=x)

    def dma(self, q, key, fn, r=(), w=()):
        self.add(q, fn, r, w, dma=key)


def interleave(S, recs):
    recs = [r for r in recs if r is not None and r.ops]
    idx = [0] * len(recs)
    left = sum(len(r.ops) for r in recs)
    while left:
        for i, r in enumerate(recs):
            if idx[i] < len(r.ops):
                eng, fn, rr, ww, dma, xx = r.ops[idx[i]]
                S.add(eng, fn, rr, ww, dma=dma, x=xx)
                idx[i] += 1
                left -= 1


def build_program(cfg, stage):
    P = Prog(cfg)
    nc = P.nc
    NT = cfg.nt
    npt = cfg.npt
    NTOK = NT * 128
    NB = 16
    with P.stack:
        w = {}

        def IN(nm, shp):
            w[nm] = P.dram_in(nm, shp)
            return w[nm]

        IN("identf", [128, 128])
        IN("ln_g", [DEPTH * 3, D])
        IN("ln_b", [DEPTH * 3, D])
        IN("masks", [5, 128, 512])
        if stage == 1:
            xin = IN("xin", [NTOK, D])
            for nm in ("ffn1", "ffn2"):
                IN(nm + "_wg", [DEPTH, D, FF]); IN(nm + "_wu", [DEPTH, D, FF]); IN(nm + "_wd", [DEPTH, FF, D])
            IN("attn_w_qkv", [D, 1536]); IN("attn_w_o", [D, D]); IN("sinks2", [2, 8])
            IN("cache_k", [NB, 128, 256]); IN("cache_v", [NB, 128, 256])
            IN("rope", [NTOK, 128])
            IN("ret_w_in", [D, 6144])
            IN("xpos", [NTOK, 512]); IN("zeta1", [128, npt * 4])
            xmid_o = P.dram_out("xmid", [(npt + 1) * 128, D])
            F_o = P.dram_out("F", [RH, RQK, RV])
            kwp_o = P.dram_out("kwin_p", [128, 256]); vwp_o = P.dram_out("vwin_p", [128, 256])
            kws_o = P.dram_out("kwin_s", [NB, 128, 256]); vws_o = P.dram_out("vwin_s", [NB, 128, 256])
        else:
            xmid_i = IN("xmid", [(npt + 1) * 128, D])
            IN("ffn2_wg", [DEPTH, D, FF]); IN("ffn2_wu", [DEPTH, D, FF]); IN("ffn2_wd", [DEPTH, FF, D])
            IN("ret_w_in", [D, 6144]); IN("ret_w_o", [2048, D])
            IN("xpos", [NTOK, 512]); IN("rscal", [128, 24]); IN("coef", [128, 32])
            IN("xposh", [RH, npt * 128, 1024])
            IN("ind", [128, NB]); IN("indT", [128, NB * 128])
            IN("fall", [N_CORES, RH, RQK, RV])
            IN("state", [NB, RH, RQK, RV])
            yout = P.dram_out("y", [(npt + 1) * 128, D])
            stp_o = P.dram_out("state_p", [RH, RQK, RV])
            sts_o = P.dram_out("state_s", [NB, RH, RQK, RV])

        x = P.sb("x", [128, NT, D], F32)
        xT = P.sb("xT", [128, 8, NTOK], BF16)
        ident = P.sb("ident", [128, 128], BF16)

        S = Sched(nc, P.stack)

        def post_tile(S, t, xb, ti, psTl, pre_scale=None):
            b = ti % xb.shape[1]
            pb = ti % len(psTl)
            pT = psTl[pb]
            if pre_scale is None:
                S.act(ACP(xb[:, b, :], x[:, t, :]), r=[("x", t, 0), ("x", t, 1)], w=[("xb", b)])
            else:
                S.act(AMUL(xb[:, b, :], x[:, t, :], pre_scale), r=[("x", t, 0), ("x", t, 1)], w=[("xb", b)])
            S.pe(TRS([pT[:, k, :] for k in range(8)], [xb[:, b, k * 128:(k + 1) * 128] for k in range(8)], ident[:]),
                 r=[("xb", b), ("ident",)], w=[("psT", pb)])
            S.dve(CP(xT[:, :, t * 128:(t + 1) * 128], pT[:]), w=[("xT", t)], x=[("psT", pb)])
            if pre_scale is None:
                S.act(AMUL(x[:, t, :], x[:, t, :], ALPHA), r=[("xb", b)], w=[("x", t, 0), ("x", t, 1)])

        def norm_ops(S, j, lnt, src_lo, src_hi, r_lo, r_hi):
            st, mv, sc = lnt["stats"], lnt["mv"], lnt["sc"]
            S.dve(lambda e: e.bn_stats(out=st[:, j, 0, :], in_=src_lo), r=r_lo[0], w=[("st", j, 0)], x=r_lo[1])
            if src_hi is not None:
                S.dve(lambda e: e.bn_stats(out=st[:, j, 1, :], in_=src_hi), r=r_hi[0], w=[("st", j, 1)], x=r_hi[1])
                S.dve(lambda e: e.bn_aggr(out=mv[:, j, :], in_=st[:, j, :, :].rearrange("p a b -> p (a b)")),
                      r=[("st", j, 0), ("st", j, 1)], w=[("mv", j)])
            else:
                S.dve(lambda e: e.bn_aggr(out=mv[:, j, :], in_=st[:, j, 0, :]), r=[("st", j, 0)], w=[("mv", j)])
            S.act(ACTF(sc[:, j, 0:1], mv[:, j, 1:2], AF.Sqrt, bias=lnt["eps"][:, 0:1]),
                  r=[("mv", j), ("eps",)], w=[("sc", j, 0)])
            S.dve(lambda e: e.reciprocal(out=sc[:, j, 1:2], in_=sc[:, j, 0:1]), r=[("sc", j, 0)], w=[("sc", j, 1)])
            S.dve(STT(sc[:, j, 2:3], mv[:, j, 0:1], -1.0, sc[:, j, 1:2], ALU.mult, ALU.mult),
                  r=[("mv", j), ("sc", j, 1)], w=[("sc", j, 2)])

        def ln_a1(S, t, ti, lnt):
            j = ti % lnt["ring"]
            st, mv, sc = lnt["stats"], lnt["mv"], lnt["sc"]
            S.dve(lambda e: e.bn_stats(out=st[:, j, 0, :], in_=x[:, t, 0:512]), r=[("x", t, 0)], w=[("st", j, 0)])
            S.dve(lambda e: e.bn_stats(out=st[:, j, 1, :], in_=x[:, t, 512:1024]), r=[("x", t, 1)], w=[("st", j, 1)])
            S.dve(lambda e: e.bn_aggr(out=mv[:, j, :], in_=st[:, j, :, :].rearrange("p a b -> p (a b)")),
                  r=[("st", j, 0), ("st", j, 1)], w=[("mv", j)])
            S.act(ACTF(sc[:, j, 0:1], mv[:, j, 1:2], AF.Sqrt, bias=lnt["eps"][:, 0:1]),
                  r=[("mv", j), ("eps",)], w=[("sc", j, 0)])
            S.dve(lambda e: e.reciprocal(out=sc[:, j, 1:2], in_=sc[:, j, 0:1]), r=[("sc", j, 0)], w=[("sc", j, 1)])
            S.dve(STT(sc[:, j, 2:3], mv[:, j, 0:1], -1.0, sc[:, j, 1:2], ALU.mult, ALU.mult),
                  r=[("mv", j), ("sc", j, 1)], w=[("sc", j, 2)])

        def ln_a2(S, t, ti, gb, lnt, gbkey="gb"):
            j = ti % lnt["ring"]
            sc = lnt["sc"]
            S.act(ACTF(x[:, t, :], x[:, t, :], AF.Identity, bias=sc[:, j, 2:3], scale=sc[:, j, 1:2]),
                  r=[("sc", j, 1), ("sc", j, 2)], w=[("x", t, 0), ("x", t, 1)])
            S.pool(TT(x[:, t, 0:512], x[:, t, 0:512], gb[:, 0, 0:512], ALU.mult), r=[(gbkey, 0)], w=[("x", t, 0)])
            S.dve(TT(x[:, t, 512:1024], x[:, t, 512:1024], gb[:, 0, 512:1024], ALU.mult), r=[(gbkey, 0)], w=[("x", t, 1)])
            S.pool(TT(x[:, t, 0:512], x[:, t, 0:512], gb[:, 1, 0:512], ALU.add), r=[(gbkey, 1)], w=[("x", t, 0)])
            S.dve(TT(x[:, t, 512:1024], x[:, t, 512:1024], gb[:, 1, 512:1024], ALU.add), r=[(gbkey, 1)], w=[("x", t, 1)])

        def ln_a(S, t, ti, gb, lnt, gbkey="gb"):
            ln_a1(S, t, ti, lnt)
            ln_a2(S, t, ti, gb, lnt, gbkey)

        def ln_b(S, t, ti, xb, psTl, final_out=None):
            if final_out is None:
                post_tile(S, t, xb, ti, psTl)
            else:
                S.dma("sp", "yout%d" % (ti % 4), DMA(final_out[t * 128:(t + 1) * 128, :], x[:, t, :]),
                      r=[("x", t, 0), ("x", t, 1)])

        def ln_tile(S, t, ti, gb, lnt, xb, psTl, final_out=None, gbkey="gb"):
            ln_a(S, t, ti, gb, lnt, gbkey)
            ln_b(S, t, ti, xb, psTl, final_out)

        def load_gb(S, gb, idx):
            S.dma("sp", "gb", DMA(gb[:, 0, :], w["ln_g"][idx:idx + 1, :].partition_broadcast(128)), w=[("gb", 0)])
            S.dma("sp", "gb2", DMA(gb[:, 1, :], w["ln_b"][idx:idx + 1, :].partition_broadcast(128)), w=[("gb", 1)])

        def ln_bufs(ph, ring=3):
            lnt = {
                "stats": P.sb("lnstats", [128, ring, 2, 6], F32, ph),
                "mv": P.sb("lnmv", [128, ring, 2], F32, ph),
                "sc": P.sb("lnsc", [128, ring, 4], F32, ph),
                "eps": P.sb("lneps", [128, 1], F32, ph),
                "ring": ring,
            }
            S.dve(lambda e: e.memset(lnt["eps"][:], LN_EPS), w=[("eps",)])
            return lnt

        def dump_x(name):
            dbg = P.dram_out(name, [NT * 128, D])
            for t in range(NT):
                S.dma("sp", "tile%d" % t, DMA(dbg[t * 128:(t + 1) * 128, :], x[:, t, :]),
                      r=[("x", t, 0), ("x", t, 1)])
            S.flush()

        with ExitStack() as ph:
            psT0 = [P.ps("psT0", [128, 8, 128], BF16, ph) for _ in range(4)]
            xb = P.sb("xb0", [128, 4, D], BF16, ph)
            identf = P.sb("identf_sb", [128, 128], F32, ph)
            S.dma("sp", "identf", DMA(identf[:], w["identf"][:, :]), w=[("identf",)])
            S.act(ACP(ident[:], identf[:]), r=[("identf",)], w=[("ident",)])
            if stage == 1:
                for t in range(NT):
                    S.dma("sp", "tile%d" % t, DMA(x[:, t, :], xin[t * 128:(t + 1) * 128, :]),
                          w=[("x", t, 0), ("x", t, 1)])
                for t in range(NT):
                    post_tile(S, t, xb, t, psT0)
            else:
                for t in range(npt + 1):
                    S.dma("sp", "tile%d" % t, DMA(x[:, t, :], xmid_i[t * 128:(t + 1) * 128, :]),
                          w=[("x", t, 0), ("x", t, 1)])
                for t in range(npt + 1):
                    post_tile(S, t, xb, t, psT0, pre_scale=1.0 / ALPHA)
            S.flush()

        def ffn_phase(S, wg, wu, wd, ln_idx, with_halo, final_out=None, pre_ln=None, pre_tiles=None):
            G = cfg.G
            groups = cfg.groups(with_halo)
            tiles = []
            tgrp = {}
            for gi, (t0, n) in enumerate(groups):
                for t in range(t0, t0 + n):
                    tiles.append(t)
                    tgrp[t] = gi
            ntl_ = len(tiles)
            parts = [list(range(i, min(i + G, NFC))) for i in range(0, NFC, G)]
            RW = 5
            RD = RW + G - 1
            with ExitStack() as ph:
                psA = [P.ps("psA", [128, 512], F32, ph) for _ in range(6)]
                psTl = [P.ps("psT", [128, 8, 128], BF16, ph) for _ in range(2)]
                hT = P.sb("hT", [128, G, NTOK], BF16, ph)
                wgs = P.sb("wgs", [128, RW, 8, 128], BF16, ph)
                wus = P.sb("wus", [128, RW, 8, 128], BF16, ph)
                wds = P.sb("wds", [128, RD, D], BF16, ph)
                sg = P.sb("sg", [128, 2, 512], F32, ph)
                gb = P.sb("gb", [128, 2, D], F32, ph)
                xb = P.sb("xb", [128, 2, D], BF16, ph)
                lnt = ln_bufs(ph, ring=8)
                if pre_ln is not None:
                    gbp = P.sb("gbp", [128, 2, D], F32, ph)
                    S.dma("sp", "gbp", DMA(gbp[:, 0, :], w["ln_g"][pre_ln:pre_ln + 1, :].partition_broadcast(128)), w=[("gbp", 0)])
                    S.dma("sp", "gbp2", DMA(gbp[:, 1, :], w["ln_b"][pre_ln:pre_ln + 1, :].partition_broadcast(128)), w=[("gbp", 1)])
                load_gb(S, gb, ln_idx)

                def issue_w(S, fc):
                    sl = fc % RW
                    sd = fc % RD
                    S.dma("pool", "wg%d" % sl,
                          DMA(wgs[:, sl, :, :], wg[:, fc * 128:(fc + 1) * 128].rearrange("(k p) f -> p k f", p=128)),
                          w=[("wg", sl)])
                    S.dma("pool", "wu%d" % sl,
                          DMA(wus[:, sl, :, :], wu[:, fc * 128:(fc + 1) * 128].rearrange("(k p) f -> p k f", p=128)),
                          w=[("wu", sl)])
                    S.dma("pool", "wd%d" % sd, DMA(wds[:, sd, :], wd[fc * 128:(fc + 1) * 128, :]), w=[("wd", sd)])

                cnt = {"it": 0, "yi": 0}

                def gu(S, fc, fl, gi):
                    t0, n = groups[gi]
                    it = cnt["it"]
                    cnt["it"] += 1
                    sl = fc % RW
                    N = n * 128
                    c0 = t0 * 128
                    bg = psA[it % 2]
                    bu = psA[2 + it % 2]
                    sb_ = it % 2
                    xr = [("xT", t) for t in range(t0, t0 + n)]
                    S.pe(MM8(bg[:, 0:N], (lambda k: wgs[:, sl, k, :]), (lambda k: xT[:, k, c0:c0 + N])),
                         r=[("wg", sl)] + xr, w=[("ps", it % 2)])
                    S.pe(MM8(bu[:, 0:N], (lambda k: wus[:, sl, k, :]), (lambda k: xT[:, k, c0:c0 + N])),
                         r=[("wu", sl)] + xr, w=[("ps", 2 + it % 2)])
                    S.act(ACTF(sg[:, sb_, 0:N], bg[:, 0:N], AF.Silu), w=[("sg", sb_)], x=[("ps", it % 2)])
                    S.dve(TT(hT[:, fl, c0:c0 + N], bu[:, 0:N], sg[:, sb_, 0:N], ALU.mult),
                          r=[("sg", sb_)], w=[("hT", fl, gi)], x=[("ps", 2 + it % 2)])

                def down(S, part, t, bank=None):
                    for dh in range(2):
                        yi = cnt["yi"]
                        cnt["yi"] += 1
                        bi = (4 + yi % 2) if bank is None else bank
                        by = psA[bi]

                        def mmd(e, by=by, dh=dh):
                            ins = None
                            for fl, fc in enumerate(part):
                                ins = e.matmul(by[:, :], hT[:, fl, t * 128:(t + 1) * 128],
                                               wds[:, fc % RD, dh * 512:(dh + 1) * 512],
                                               start=(fl == 0), stop=(fl == len(part) - 1))
                            return ins
                        S.pe(mmd, r=[("hT", fl, tgrp[t]) for fl in range(len(part))] +
                             [("wd", fc % RD) for fc in part], w=[("ps", bi)])
                        S.dve(STT(x[:, t, dh * 512:(dh + 1) * 512], by[:, :], 0.5,
                                  x[:, t, dh * 512:(dh + 1) * 512], ALU.mult, ALU.add),
                              w=[("x", t, dh)], x=[("ps", bi)])

                for fc in range(min(RW - 1, NFC)):
                    issue_w(S, fc)

                first_done = set()
                if pre_ln is not None:
                    pre_set = set(tiles if pre_tiles is None else pre_tiles)
                    L = [ti for ti, t in enumerate(tiles) if t in pre_set]
                    Ra, Ra2, Rb = {}, {}, {}
                    for ti in L:
                        t = tiles[ti]
                        Ra[ti] = Rec(); ln_a1(Ra[ti], t, ti, lnt)
                        Ra2[ti] = Rec(); ln_a2(Ra2[ti], t, ti, gbp, lnt, "gbp")
                        Rb[ti] = Rec(); ln_b(Rb[ti], t, ti, xb, psTl)
                    nL = len(L)
                    done_at = {}
                    for k, ti in enumerate(L):
                        done_at[ti] = (k - k % 2) + 4
                    for i in range(0, nL + 6, 2):
                        grp = [Ra[L[k]] for k in (i, i + 1) if k < nL]
                        grp += [Ra2[L[k]] for k in (i - 2, i - 1) if 0 <= k < nL]
                        grp += [Rb[L[k]] for k in (i - 4, i - 3) if 0 <= k < nL]
                        for gi, (t0, n) in enumerate(groups):
                            need = [tiles.index(t) for t in range(t0, t0 + n) if t in pre_set]
                            ready = all(done_at[ti] < i for ti in need)
                            if gi not in first_done and ready:
                                R = Rec()
                                if not first_done and RW - 1 < NFC:
                                    issue_w(R, RW - 1)
                                gu(R, 0, 0, gi)
                                grp.append(R)
                                first_done.add(gi)
                                break
                        interleave(S, grp)
                    for gi in range(len(groups)):
                        if gi not in first_done:
                            if not first_done and RW - 1 < NFC:
                                issue_w(S, RW - 1)
                            gu(S, 0, 0, gi)
                            first_done.add(gi)

                for pi, part in enumerate(parts):
                    last = (pi == len(parts) - 1)
                    if not last:
                        for fl, fc in enumerate(part):
                            if fc + RW - 1 < NFC and not (fc == 0 and first_done):
                                issue_w(S, fc + RW - 1)
                            for gi in range(len(groups)):
                                if fc == 0 and gi in first_done:
                                    continue
                                gu(S, fc, fl, gi)
                        for t in tiles:
                            down(S, part, t)
                        continue
                    assert len(part) <= RW - 1 and part[-1] == NFC - 1
                    Rd, Ra, Ra2, Rb, Rgu = {}, {}, {}, {}, {}
                    for gi in range(len(groups)):
                        Rgu[gi] = Rec()
                        for fl, fc in enumerate(part):
                            gu(Rgu[gi], fc, fl, gi)
                    for ti, t in enumerate(tiles):
                        Rd[ti] = Rec(); down(Rd[ti], part, t, bank=4 + ti % 2)
                        Ra[ti] = Rec(); ln_a1(Ra[ti], t, ti, lnt)
                        Ra2[ti] = Rec(); ln_a2(Ra2[ti], t, ti, gb, lnt)
                        Rb[ti] = Rec(); ln_b(Rb[ti], t, ti, xb, psTl, final_out)
                    interleave(S, [Rgu[0]])
                    first_ti = [tiles.index(t0) for (t0, n) in groups] + [ntl_]
                    inject = {}
                    for gi in range(1, len(groups)):
                        steps = list(range(first_ti[gi - 1] - first_ti[gi - 1] % 2, first_ti[gi], 2))
                        ops = Rgu[gi].ops
                        per = (len(ops) + len(steps) - 1) // len(steps)
                        for si, st_ in enumerate(steps):
                            R = Rec()
                            R.ops = ops[si * per:(si + 1) * per]
                            inject.setdefault(st_, []).append(R)
                    for i in range(0, ntl_ + 8, 2):
                        grp = inject.get(i, [])
                        grp += [Rd[k] for k in (i, i + 1) if k < ntl_]
                        grp += [Ra[k] for k in (i - 2, i - 1) if 0 <= k < ntl_]
                        grp += [Ra2[k] for k in (i - 4, i - 3) if 0 <= k < ntl_]
                        grp += [Rb[k] for k in (i - 6, i - 5) if 0 <= k < ntl_]
                        interleave(S, grp)
                S.flush()

        def attn_phases(S):
            with ExitStack() as po:
                wo = P.sb("wo", [128, 8, D], BF16, po)
                es2 = P.sb("es2", [128, 8], F32, po)
                ones64 = P.sb("ones64", [128, 64], BF16, po)
                QTs = P.sb("QTs", [64, 16, 128], BF16, po)
                KTs = P.sb("KTs", [64, 4, 128], BF16, po)
                Vbs = P.sb("Vbs", [128, 256], BF16, po)
                krs = P.sb("krs", [128, 256], F32, po)
                vfs = P.sb("vfs", [128, 256], F32, po)
                e_ = P.sb("e", [128, 2, 512], BF16, po)
                rd = P.sb("rd", [128, 2, 256], F32, po)
                oT = P.sb("oT", [128, 2, 8, 128], BF16, po)

                def common_loads(S, masks, m0, nm):
                    S.dma("pool", "masks", DMA(masks[:], w["masks"][m0:m0 + nm].rearrange("m p n -> p m n")), w=[("masks",)])

                def normalize(S, g, ob, os_, psOD):
                    rdv = rd[:, ob, :].rearrange("p (c q) -> p c q", c=2)
                    S.dve(TT(rdv, psOD[ob][:, 256:512].rearrange("p (c q) -> p c q", c=2),
                             es2[:, 2 * g:2 * g + 2].unsqueeze(2).broadcast_to([128, 2, 128]), ALU.add),
                          r=[("es2",)], w=[("rd", ob)], x=[("psod", ob)])
                    S.dve(lambda e: e.reciprocal(out=rd[:, ob, :], in_=rd[:, ob, :]), w=[("rd", ob)])
                    S.dve(TT(oT[:, os_, 2 * g:2 * g + 2, :], psOD[ob][:, 0:256].rearrange("p (c q) -> p c q", c=2),
                             rdv, ALU.mult), r=[("rd", ob)], w=[("oT", os_, g)], x=[("psod", ob)])

                def outproj_add(S, t, os_, psY, ytag):
                    for dh in range(2):
                        def f(e, dh=dh):
                            ins = None
                            for c in range(8):
                                ins = e.matmul(psY[dh][:, :], oT[:, os_, c, :], wo[:, c, dh * 512:(dh + 1) * 512],
                                               start=(c == 0), stop=(c == 7))
                            return ins
                        S.pe(f, r=[("oT", os_, g) for g in range(4)] + [("wo",)], w=[(ytag, dh)])
                        S.dve(TT(x[:, t, dh * 512:(dh + 1) * 512], psY[dh][:, :], x[:, t, dh * 512:(dh + 1) * 512], ALU.add),
                              w=[("x", t, dh)], x=[(ytag, dh)])

                S.dma("pool", "wo", DMA(wo[:], w["attn_w_o"].rearrange("(c p) n -> p c n", p=128)), w=[("wo",)])
                S.dma("sp", "es2a", DMA(es2[0:64, :], w["sinks2"][0:1, :].partition_broadcast(64)), w=[("es2", 0)])
                S.dma("sp", "es2b", DMA(es2[64:128, :], w["sinks2"][1:2, :].partition_broadcast(64)), w=[("es2", 1)])
                S.act(ACTF(es2[:], es2[:], AF.Exp), r=[("es2", 0), ("es2", 1)], w=[("es2",)])
                S.dve(lambda e: e.memset(ones64[:], 1.0), w=[("ones",)])

                with ExitStack() as ph:
                    psQ = [P.ps("psQ", [128, 512], F32, ph) for _ in range(2)]
                    psKV = P.ps("psKV", [128, 512], F32, ph)
                    psT = P.ps("psT", [128, 8, 128], BF16, ph)
                    psS = [P.ps("psS", [128, 512], F32, ph) for _ in range(2)]
                    psOD = [P.ps("psOD", [128, 512], F32, ph) for _ in range(2)]
                    masks = P.sb("masks", [128, 3, 512], BF16, ph)
                    wqkv = P.sb("wqkv", [128, 8, 1536], BF16, ph)
                    rope_sb = P.sb("rope", [128, 2, 128], F32, ph)
                    ropeA = P.sb("ropeA", [128, 8, 64], F32, ph)
                    ropeB = P.sb("ropeB", [128, 8, 64], F32, ph)
                    kA = P.sb("kA", [128, 4, 64], F32, ph)
                    kB = P.sb("kB", [128, 4, 64], F32, ph)
                    qrot = P.sb("qrot", [128, 2, 1024], BF16, ph)
                    QT = P.sb("QT", [64, 2, 16, 128], BF16, ph)
                    krf = P.sb("krf", [128, 2, 256], F32, ph)
                    kbf = P.sb("kbf", [128, 2, 256], BF16, ph)
                    KT = P.sb("KT", [64, 5, 4, 128], BF16, ph)
                    vf = P.sb("vf", [128, 2, 256], F32, ph)
                    Vb = P.sb("Vb", [128, 5, 256], BF16, ph)
                    PT = P.sb("PT", [128, 8, 512], BF16, ph)
                    common_loads(S, masks, 0, 3)
                    S.dma("pool", "wqkv", DMA(wqkv[:], w["attn_w_qkv"].rearrange("(k p) n -> p k n", p=128)),
                          w=[("wqkv",)])
                    order = [cfg.halo] + list(range(npt)) + [cfg.samp]
                    NO = len(order)

                    def dst(ti):
                        t = order[ti]
                        b2, slot = ti % 2, ti % 5
                        if t == cfg.samp:
                            return dict(krf=krs[:, :], vf=vfs[:, :], Vb=Vbs[:, :], KT=KTs, QT=QTs,
                                        rk=("krs",), rv=("vfs",), rVb=("Vbs",), rKT=("KTs",), rQT=("QTs",))
                        return dict(krf=krf[:, b2, :], vf=vf[:, b2, :], Vb=Vb[:, slot, :], KT=KT[:, slot], QT=QT[:, b2],
                                    rk=("krf", b2), rv=("vf", b2), rVb=("Vb", slot), rKT=("KT", slot), rQT=("QT", b2))

                    def P1(R, ti):
                        t = order[ti]
                        need_q = (t != cfg.halo)
                        cs = slice(t * 128, (t + 1) * 128)
                        b2 = ti % 2
                        d_ = dst(ti)
                        R.dma("sp", "rope%d" % b2, DMA(rope_sb[:, b2, :], w["rope"][cs, :]), w=[("rope", b2)])
                        c2 = rope_sb[:, b2, 0:64]
                        nlo = rope_sb[:, b2, 64:96]
                        nhi = rope_sb[:, b2, 96:128]
                        R.pe(MM8(psKV[:, :], (lambda k: xT[:, k, cs]), (lambda k: wqkv[:, k, 1024:1536])),
                             r=[("xT", t), ("wqkv",)], w=[("pskv",)])
                        kv = psKV[:, 0:256].rearrange("p (h d) -> p h d", d=64)
                        R.dve(TT(kA[:], kv, c2.unsqueeze(1).broadcast_to([128, 4, 64]), ALU.mult),
                              r=[("rope", b2)], w=[("kA",)], x=[("pskv",)])
                        R.dve(TT(kB[:, :, 0:32], kv[:, :, 32:64], nlo.unsqueeze(1).broadcast_to([128, 4, 32]), ALU.mult),
                              r=[("rope", b2)], w=[("kB", 0)], x=[("pskv",)])
                        R.dve(TT(kB[:, :, 32:64], kv[:, :, 0:32], nhi.unsqueeze(1).broadcast_to([128, 4, 32]), ALU.mult),
                              r=[("rope", b2)], w=[("kB", 1)], x=[("pskv",)])
                        R.act(ACP(d_["vf"], psKV[:, 256:512]), w=[d_["rv"]], x=[("pskv",)])
                        R.pool(TT(d_["krf"].rearrange("p (h d) -> p h d", d=64), kA[:], kB[:], ALU.add),
                               r=[("kA",), ("kB", 0), ("kB", 1)], w=[d_["rk"]])
                        R.act(ACP(d_["Vb"], d_["vf"]), r=[d_["rv"]], w=[d_["rVb"]])
                        R.pool(CP(kbf[:, b2, :], d_["krf"]), r=[d_["rk"]], w=[("kbf", b2)])
                        if need_q:
                            for half in range(2):
                                R.pe(MM8(psQ[half][:, :], (lambda k: xT[:, k, cs]),
                                         (lambda k, half=half: wqkv[:, k, half * 512:(half + 1) * 512])),
                                     r=[("xT", t), ("wqkv",)], w=[("psq", half)])
                                qv = psQ[half][:, :].rearrange("p (h d) -> p h d", d=64)
                                R.dve(TT(ropeA[:], qv, c2.unsqueeze(1).broadcast_to([128, 8, 64]), ALU.mult),
                                      r=[("rope", b2)], w=[("ropeA",)], x=[("psq", half)])
                                R.dve(TT(ropeB[:, :, 0:32], qv[:, :, 32:64], nlo.unsqueeze(1).broadcast_to([128, 8, 32]),
                                         ALU.mult), r=[("rope", b2)], w=[("ropeB", 0)], x=[("psq", half)])
                                R.dve(TT(ropeB[:, :, 32:64], qv[:, :, 0:32], nhi.unsqueeze(1).broadcast_to([128, 8, 32]),
                                         ALU.mult), r=[("rope", b2)], w=[("ropeB", 1)], x=[("psq", half)])
                                R.pool(TT(qrot[:, b2, half * 512:(half + 1) * 512].rearrange("p (h d) -> p h d", d=64),
                                          ropeA[:], ropeB[:], ALU.add),
                                       r=[("ropeA",), ("ropeB", 0), ("ropeB", 1)], w=[("qrot", b2, half)])

                    def P2(R, ti):
                        t = order[ti]
                        need_q = (t != cfg.halo)
                        b2 = ti % 2
                        d_ = dst(ti)
                        R.pe(TRS([psT[0:64, g, :] for g in range(4)],
                                 [kbf[:, b2, g * 64:(g + 1) * 64] for g in range(4)], ident[:]),
                             r=[("kbf", b2), ("ident",)], w=[("psT", 0)])
                        R.act(ACP(d_["KT"][:, :, :], psT[0:64, 0:4, :]), w=[d_["rKT"]], x=[("psT", 0)])
                        if need_q:
                            for half in range(2):
                                R.pe(TRS([psT[0:64, h8, :] for h8 in range(8)],
                                         [qrot[:, b2, (half * 8 + h8) * 64:(half * 8 + h8 + 1) * 64] for h8 in range(8)],
                                         ident[:]), r=[("qrot", b2, half), ("ident",)], w=[("psT", 0)])
                                R.act(ACP(d_["QT"][:, half * 8:(half + 1) * 8, :], psT[0:64, :, :]),
                                      w=[d_["rQT"] + (half,)], x=[("psT", 0)])
                        if t == npt - 1:
                            R.dma("sp", "kwp", DMA(kwp_o[:, :], d_["krf"]), r=[d_["rk"]])
                            R.dma("sp", "vwp", DMA(vwp_o[:, :], d_["vf"]), r=[d_["rv"]])

                    def P3(R, ti):
                        t = order[ti]
                        b2, slot, prev = ti % 2, ti % 5, (ti - 1) % 5
                        u = 0
                        for g in range(4):
                            for kb in range(2):
                                sl_ = prev if kb == 0 else slot
                                midx = (2 if t == 0 else 0) if kb == 0 else 1
                                pb = g * 2 + kb
                                sb_ = u % 2
                                R.pe((lambda sl_=sl_, g=g, sb_=sb_: lambda e: e.matmul(
                                    psS[sb_][:, :], KT[:, sl_, g, :],
                                    QT[:, b2, 4 * g:4 * g + 4, :].rearrange("p h q -> p (h q)"), start=True, stop=True))(),
                                    r=[("KT", sl_), ("QT", b2, 0), ("QT", b2, 1)], w=[("pss", sb_)])
                                R.act(ACTF(e_[:, sb_, :], psS[sb_][:, :], AF.Exp, scale=ATTN_SCALE),
                                      w=[("e", sb_)], x=[("pss", sb_)])
                                R.pool(TT(PT[:, pb, :], e_[:, sb_, :], masks[:, midx, :], ALU.mult),
                                       r=[("e", sb_), ("masks",)], w=[("PT", pb)])
                                u += 1
                        for g in range(4):
                            ob = g % 2

                            def pv(e, g=g, ob=ob):
                                ins = None
                                for half in range(2):
                                    for kind in range(2):
                                        for kb in range(2):
                                            sl_ = prev if kb == 0 else slot
                                            lhs = Vb[:, sl_, g * 64:(g + 1) * 64] if kind == 0 else ones64[:, :]
                                            rhs = PT[:, g * 2 + kb, :].rearrange("p (c j q) -> p c j q", c=2, j=2)[:, :, half, :]
                                            ins = e.matmul(
                                                psOD[ob][half * 64:(half + 1) * 64,
                                                         kind * 256:(kind + 1) * 256].rearrange("p (c q) -> p c q", c=2),
                                                lhs, rhs, start=(kb == 0), stop=(kb == 1))
                                return ins
                            R.pe(pv, r=[("Vb", prev), ("Vb", slot), ("PT", g * 2), ("PT", g * 2 + 1), ("ones",)],
                                 w=[("psod", ob)])
                            normalize(R, g, ob, b2, psOD)

                    def P4(R, ti):
                        outproj_add(R, order[ti], ti % 2, psQ, "psq")

                    for i in range(-2, NO + 2):
                        recs = []
                        for fn_, off in ((P3, 0), (P2, 1), (P4, -1), (P1, 2)):
                            ti = i + off
                            if not (0 <= ti < NO):
                                continue
                            t = order[ti]
                            if fn_ in (P3, P4) and (t == cfg.halo or t == cfg.samp):
                                continue
                            R = Rec()
                            fn_(R, ti)
                            recs.append(R)
                        interleave(S, recs)
                    S.flush()

                with ExitStack() as ph:
                    psY = [P.ps("psY", [128, 512], F32, ph) for _ in range(2)]
                    psT = P.ps("psT", [128, 8, 128], BF16, ph)
                    psSn = P.ps("psSn", [128, 512], F32, ph)
                    psS2 = [P.ps("psS2", [128, 512], F32, ph) for _ in range(2)]
                    psOD = [P.ps("psOD", [128, 512], F32, ph) for _ in range(2)]
                    masks = P.sb("masks", [128, 2, 512], BF16, ph)
                    KcT = P.sb("KcT", [64, NB, 4, 128], BF16, ph)
                    Vc = P.sb("Vc", [128, NB, 256], BF16, ph)
                    kcb = P.sb("kcb", [128, 2, 256], BF16, ph)
                    PT = P.sb("PT", [128, 2, 512], BF16, ph)
                    PTc = P.sb("PTc", [128, 2, 512], BF16, ph)
                    common_loads(S, masks, 3, 2)
                    t = cfg.samp
                    S.dma("pool", "vc", DMA(Vc[:], w["cache_v"].rearrange("b p c -> p b c")), w=[("Vc",)])
                    for b in range(NB):
                        S.dma("pool", "kc%d" % (b % 2), DMA(kcb[:, b % 2, :], w["cache_k"][b]), w=[("kcb", b % 2)])
                        S.pe(TRS([psT[0:64, g, :] for g in range(4)],
                                 [kcb[:, b % 2, g * 64:(g + 1) * 64] for g in range(4)], ident[:]),
                             r=[("kcb", b % 2), ("ident",)], w=[("psT", 0)])
                        S.act(ACP(KcT[:, b, :, :], psT[0:64, 0:4, :]), w=[("KcT", b)], x=[("psT", 0)])
                    S.dma("sp", "kwo", DMA(kws_o[:, 0:120, :], w["cache_k"][:, 8:128, :]))
                    S.dma("sp", "vwo", DMA(vws_o[:, 0:120, :], w["cache_v"][:, 8:128, :]))
                    for b in range(NB):
                        S.dma("sp", "kwn", DMA(kws_o[b, 120:128, :], krs[b * 8:(b + 1) * 8, :]))
                        S.dma("sp", "vwn", DMA(vws_o[b, 120:128, :], vfs[b * 8:(b + 1) * 8, :]))
                    for g in range(4):
                        S.pe((lambda g=g: lambda e: e.matmul(psSn[:, :], KTs[:, g, :],
                                                             QTs[:, 4 * g:4 * g + 4, :].rearrange("p h q -> p (h q)"),
                                                             start=True, stop=True))(),
                             w=[("psn",)])
                        S.act(ACTF(e_[:, 0, :], psSn[:, :], AF.Exp, scale=ATTN_SCALE), w=[("e", 0)], x=[("psn",)])
                        S.pool(TT(PT[:, g % 2, :], e_[:, 0, :], masks[:, 0, :], ALU.mult),
                               r=[("e", 0), ("masks",)], w=[("PT", g % 2)])

                        def sc(e, g=g):
                            ins = None
                            for b in range(NB):
                                ins = e.matmul(psS2[g % 2][:, b * 32:(b + 1) * 32].rearrange("p (h i) -> p h i", h=4),
                                               KcT[:, b, g, :], QTs[:, 4 * g:4 * g + 4, b * 8:(b + 1) * 8],
                                               start=True, stop=True)
                            return ins
                        S.pe(sc, r=[("KcT", b) for b in range(NB)], w=[("ps2", g % 2)])
                        S.act(ACTF(e_[:, 1, :], psS2[g % 2][:, :], AF.Exp, scale=ATTN_SCALE),
                              w=[("e", 1)], x=[("ps2", g % 2)])
                        S.pool(TT(PTc[:, g % 2, :], e_[:, 1, :], masks[:, 1, :], ALU.mult),
                               r=[("e", 1), ("masks",)], w=[("PTc", g % 2)])
                        ob = g % 2

                        def pv2(e, g=g, ob=ob):
                            ins = None
                            for j in range(4):
                                half, cc = j % 2, j // 2
                                for kind in range(2):
                                    reg = psOD[ob][half * 64:(half + 1) * 64,
                                                   kind * 256 + cc * 128:kind * 256 + (cc + 1) * 128]
                                    lhs = Vbs[:, g * 64:(g + 1) * 64] if kind == 0 else ones64[:, :]
                                    ins = e.matmul(reg, lhs, PT[:, g % 2, j * 128:(j + 1) * 128], start=True, stop=False)
                                    for b in range(NB):
                                        lhs = Vc[:, b, g * 64:(g + 1) * 64] if kind == 0 else ones64[:, :]
                                        ins = e.matmul(reg[:, b * 8:(b + 1) * 8], lhs,
                                                       PTc[:, g % 2, b * 32 + j * 8:b * 32 + j * 8 + 8],
                                                       start=False, stop=(b == NB - 1))
                            return ins
                        S.pe(pv2, r=[("Vc",), ("PT", g % 2), ("PTc", g % 2), ("ones",)], w=[("psod", ob)])
                        normalize(S, g, ob, 0, psOD)
                    outproj_add(S, t, 0, psY, "psy")
                    S.flush()

        def xpos_rot(S, src, nqk, xp_t, rA, rB, rR, rtag, srcx):
            v3 = src.rearrange("p (a c) -> p a c", a=nqk)
            v4 = src.rearrange("p (a i two) -> p a i two", a=nqk, two=2)
            b4 = rB[:, 0:nqk, :].rearrange("p a (i two) -> p a i two", two=2)
            S.dve(TT(rA[:, 0:nqk, :], v3, xp_t[:, 0:256].unsqueeze(1).broadcast_to([128, nqk, 256]), ALU.mult),
                  r=[rtag], w=[("rA",)], x=srcx)
            S.dve(TT(b4[:, :, :, 0], v4[:, :, :, 1], xp_t[:, 256:384].unsqueeze(1).broadcast_to([128, nqk, 128]),
                     ALU.mult), r=[rtag], w=[("rB", 0)], x=srcx)
            S.dve(TT(b4[:, :, :, 1], v4[:, :, :, 0], xp_t[:, 384:512].unsqueeze(1).broadcast_to([128, nqk, 128]),
                     ALU.mult), r=[rtag], w=[("rB", 1)], x=srcx)
            S.pool(TT(rR[:, 0:nqk, :], rA[:, 0:nqk, :], rB[:, 0:nqk, :], ALU.add),
                   r=[("rA",), ("rB", 0), ("rB", 1)], w=[("rR",)])

        def sweep1(S):
            with ExitStack() as ph:
                psK = [P.ps("psK", [128, 512], F32, ph) for _ in range(2)]
                psV = [P.ps("psV", [128, 512], F32, ph) for _ in range(2)]
                psF = P.ps("psF", [128, 2, 512], F32, ph)
                wkv = P.sb("wkv", [128, 8, 3072], BF16, ph)
                xp = P.sb("xp", [128, 4, 512], F32, ph)
                z1 = P.sb("z1", [128, npt * 4], F32, ph)
                rA = P.sb("rA", [128, 2, 256], F32, ph)
                rB = P.sb("rB", [128, 2, 256], F32, ph)
                rR = P.sb("rR", [128, 2, 256], F32, ph)
                kz = P.sb("kz", [128, 6, 256], BF16, ph)
                vb = P.sb("vb", [128, 6, 512], BF16, ph)
                Fsb = P.sb("Fsb", [128, 2, 2, 512], F32, ph)
                S.dma("pool", "wk", DMA(wkv[:, :, 0:1024], w["ret_w_in"][:, 1024:2048].rearrange("(k p) n -> p k n", p=128)),
                      w=[("wkv", 0)])
                S.dma("pool", "wv", DMA(wkv[:, :, 1024:3072], w["ret_w_in"][:, 2048:4096].rearrange("(k p) n -> p k n", p=128)),
                      w=[("wkv", 1)])
                S.dma("sp", "z1", DMA(z1[:], w["zeta1"][:, :]), w=[("z1",)])
                units = [(h, t) for h in range(RH) for t in range(npt)]
                NU = len(units)

                def X(R, i):
                    h, t = units[i]
                    cs = slice(t * 128, (t + 1) * 128)
                    b, b4, b6 = i % 2, i % 4, i % 6
                    R.pe(MM8(psK[b][:, 0:256], (lambda k: xT[:, k, cs]), (lambda k: wkv[:, k, h * 256:(h + 1) * 256])),
                         r=[("xT", t), ("wkv", 0)], w=[("psk", b)])
                    R.dma("sp", "xp%d" % b4, DMA(xp[:, b4, :], w["xpos"][cs, :]), w=[("xp", b4)])
                    R.pe(MM8(psV[b][:, :], (lambda k: xT[:, k, cs]),
                             (lambda k: wkv[:, k, 1024 + h * 512:1024 + (h + 1) * 512])),
                         r=[("xT", t), ("wkv", 1)], w=[("psv", b)])
                    src = psK[b][:, 0:256]
                    tab = xp[:, b4, :]
                    v4 = src.rearrange("p (i two) -> p i two", two=2)
                    bb = rB[:, b, :].rearrange("p (i two) -> p i two", two=2)
                    R.dve(TT(rA[:, b, :], src, tab[:, 0:256], ALU.mult), r=[("xp", b4)], w=[("rA", b)], x=[("psk", b)])
                    R.dve(TT(bb[:, :, 0], v4[:, :, 1], tab[:, 256:384], ALU.mult), r=[("xp", b4)], w=[("rB", b, 0)],
                          x=[("psk", b)])
                    R.dve(TT(bb[:, :, 1], v4[:, :, 0], tab[:, 384:512], ALU.mult), r=[("xp", b4)], w=[("rB", b, 1)],
                          x=[("psk", b)])
                    R.act(ACP(vb[:, b6, :], psV[b][:, :]), w=[("vb", b6)], x=[("psv", b)])
                    R.pool(TT(rR[:, b, :], rA[:, b, :], rB[:, b, :], ALU.add),
                           r=[("rA", b), ("rB", b, 0), ("rB", b, 1)], w=[("rR", b)])
                    R.act(AMUL(kz[:, b6, :], rR[:, b, :], z1[:, t * 4 + h:t * 4 + h + 1]),
                          r=[("rR", b), ("z1",)], w=[("kz", b6)])

                def Y(R, i):
                    h, t = units[i]
                    b6 = i % 6

                    def df(e):
                        ins = None
                        for dc in range(2):
                            ins = e.matmul(psF[:, dc, :], kz[:, b6, dc * 128:(dc + 1) * 128], vb[:, b6, :],
                                           start=(t == 0), stop=(t == npt - 1))
                        return ins
                    R.pe(df, r=[("kz", b6), ("vb", b6)], w=[("psf",)])
                    if t == npt - 1:
                        R.act(ACP(Fsb[:, h % 2, :, :], psF[:, :, :]), w=[("Fsb", h % 2)], x=[("psf",)])
                        R.dma("sp", "fo%d" % (h % 2), DMA(F_o[h].rearrange("(dc p) e -> p dc e", p=128), Fsb[:, h % 2, :, :]),
                              r=[("Fsb", h % 2)])

                for i in range(-2, NU, 2):
                    recs = []
                    Ry = Rec()
                    for k in (i, i + 1):
                        if 0 <= k < NU:
                            Y(Ry, k)
                    recs.append(Ry)
                    for k in (i + 2, i + 3):
                        if 0 <= k < NU:
                            R = Rec(); X(R, k); recs.append(R)
                    interleave(S, recs)
                S.flush()

        def ret_pass(S, h, sample, host_ln=False):
            g128 = math.exp(LG[h] * 128.0)
            g8 = math.exp(LG[h] * 8.0)
            last_head = (h == RH - 1)
            with ExitStack() as ph:
                psA = P.ps("psA", [128, 512], F32, ph)
                psB = P.ps("psB", [128, 512], F32, ph)
                psC = P.ps("psC", [128, 512], F32, ph)
                psT = P.ps("psT", [128, 8, 128], BF16, ph)
                psAT = P.ps("psAT", [128, 512], F32, ph)
                psO = P.ps("psO", [128, 512], F32, ph)
                psdS = P.ps("psdS", [128, 2, 512], F32, ph)
                wih = P.sb("wih", [128, 8, 1536], BF16, ph)
                woh = P.sb("woh", [128, 4, D], BF16, ph)
                xp = P.sb("xp", [128, 2, 512], F32, ph)
                rsc = P.sb("rsc", [128, 24], F32, ph)
                rmask = P.sb("rmask", [128, 128], BF16, ph)
                rA = P.sb("rA", [128, 2, 256], F32, ph)
                rB = P.sb("rB", [128, 2, 256], F32, ph)
                rR = P.sb("rR", [128, 2, 256], F32, ph)
                qk3 = P.sb("qk3", [128, 2, 3, 256], BF16, ph)
                qkT = P.sb("qkT", [128, 2, 4, 128], BF16, ph)
                vb = P.sb("vb", [128, 2, 512], BF16, ph)
                attT = P.sb("attT", [128, 2, 128], BF16, ph)
                on = P.sb("on", [128, 2, 512], F32, ph)
                sgt = P.sb("sgt", [128, 2, 512], F32, ph)
                yb = P.sb("yb", [128, 2, 512], BF16, ph)
                yT = P.sb("yT", [128, 2, 4, 128], BF16, ph)
                lnt = ln_bufs(ph)
                wi = w["ret_w_in"]
                for ci, (c0, n, d0) in enumerate(((h * 256, 256, 0), (1024 + h * 256, 256, 256),
                                                  (2048 + h * 512, 512, 512), (4096 + h * 512, 512, 1024))):
                    S.dma("pool", "wi%d" % ci, DMA(wih[:, :, d0:d0 + n],
                                                  wi[:, c0:c0 + n].rearrange("(k p) n -> p k n", p=128)),
                          w=[("wih", ci)])
                S.dma("pool", "woh", DMA(woh[:], w["ret_w_o"][h * 512:(h + 1) * 512, :].rearrange("(c p) n -> p c n", p=128)),
                      w=[("woh",)])
                S.dma("sp", "rsc", DMA(rsc[:], w["rscal"][:, :]), w=[("rsc",)])
                S.dma("pool", "rmask", DMA(rmask[:], w["masks"][3 if sample else 1, :, 0:128]), w=[("rmask",)])
                rbase = 12 if sample else 0
                xi_ap = rsc[:, rbase + 0 * 4 + h:rbase + 0 * 4 + h + 1]
                kt_ap = rsc[:, rbase + 1 * 4 + h:rbase + 1 * 4 + h + 1]
                kz_ap = rsc[:, rbase + 2 * 4 + h:rbase + 2 * 4 + h + 1]
                if not sample:
                    Sf = P.sb("Sf", [128, 2, 512], F32, ph)
                    Sbf = P.sb("Sbf", [128, 2, 512], BF16, ph)
                    Fin = P.sb("Fin", [128, 2, 2, 512], F32, ph)
                    coef = P.sb("coef", [128, 32], F32, ph)
                    S.dma("sp", "coef", DMA(coef[:], w["coef"][:, :]), w=[("coef",)])
                    for c2 in range(N_CORES):
                        fb = c2 % 2
                        S.dma("sp", "fin%d" % fb, DMA(Fin[:, fb, :, :], w["fall"][c2, h].rearrange("(dc p) e -> p dc e", p=128)),
                              w=[("Fin", fb)])
                        cf = coef[:, c2 * 4 + h:c2 * 4 + h + 1]
                        if c2 == 0:
                            S.dve(TS(Sf[:], Fin[:, fb, :, :], cf, ALU.mult), r=[("Fin", fb), ("coef",)], w=[("Sf",)])
                        else:
                            S.dve(STT(Sf[:], Fin[:, fb, :, :], cf, Sf[:], ALU.mult, ALU.add),
                                  r=[("Fin", fb), ("coef",)], w=[("Sf",)])
                    S.pool(CP(Sbf[:], Sf[:]), r=[("Sf",)], w=[("Sbf",)])
                    tiles = list(range(npt))
                else:
                    SR = 3 if host_ln else 5
                    Sb = P.sb("Sb", [128, SR, 2, 512], F32, ph)
                    if host_ln:
                        gbp = P.sb("gbp", [128, 2, D], F32, ph)
                        xbp = P.sb("xbp", [128, 2, D], BF16, ph)
                        lntp = ln_bufs(ph, ring=4)
                        S.dma("sp", "gbp", DMA(gbp[:, 0, :], w["ln_g"][4:5, :].partition_broadcast(128)), w=[("gbp", 0)])
                        S.dma("sp", "gbp2", DMA(gbp[:, 1, :], w["ln_b"][4:5, :].partition_broadcast(128)), w=[("gbp", 1)])
                    Sbb = P.sb("Sbb", [128, 2, 2, 512], BF16, ph)
                    qTm = P.sb("qTm", [128, 2, 2, 128], BF16, ph)
                    kzm = P.sb("kzm", [128, 2, 256], BF16, ph)
                    ind = P.sb("ind", [128, NB], F32, ph)
                    indT = P.sb("indT", [128, NB, 128], BF16, ph)
                    S.dma("sp", "ind", DMA(ind[:], w["ind"][:, :]), w=[("ind",)])
                    S.dma("pool", "indT", DMA(indT[:], w["indT"].rearrange("p (b q) -> p b q", b=NB)), w=[("indT",)])
                    tiles = [cfg.samp]

                for ti, t in enumerate(tiles):
                    cs = slice(t * 128, (t + 1) * 128)
                    b = ti % 2
                    S.pe(MM8(psA[:, :], (lambda k, cs=cs: xT[:, k, cs]), (lambda k: wih[:, k, 0:512])),
                         r=[("xT", t), ("wih", 0), ("wih", 1)], w=[("psa",)])
                    S.pe(MM8(psB[:, :], (lambda k, cs=cs: xT[:, k, cs]), (lambda k: wih[:, k, 512:1024])),
                         r=[("xT", t), ("wih", 2)], w=[("psb",)])
                    S.pe(MM8(psC[:, :], (lambda k, cs=cs: xT[:, k, cs]), (lambda k: wih[:, k, 1024:1536])),
                         r=[("xT", t), ("wih", 3)], w=[("psc",)])
                    S.dma("sp", "xp%d" % b, DMA(xp[:, b, :], w["xpos"][cs, :]), w=[("xp", b)])
                    xpos_rot(S, psA[:, :], 2, xp[:, b, :], rA, rB, rR, ("xp", b), [("psa",)])
                    S.act(AMUL(qk3[:, b, 0, :], rR[:, 0, :], xi_ap), r=[("rR",), ("rsc",)], w=[("qk3", b, 0)])
                    S.act(AMUL(qk3[:, b, 1, :], rR[:, 1, :], kt_ap), r=[("rR",), ("rsc",)], w=[("qk3", b, 1)])
                    S.act(AMUL(qk3[:, b, 2, :], rR[:, 1, :], kz_ap), r=[("rR",), ("rsc",)], w=[("qk3", b, 2)])
                    S.act(ACP(vb[:, b, :], psB[:, :]), w=[("vb", b)], x=[("psb",)])
                    S.pe(TRS([psT[:, a, :] for a in range(4)],
                             [qk3[:, b, 0, 0:128], qk3[:, b, 0, 128:256], qk3[:, b, 1, 0:128], qk3[:, b, 1, 128:256]],
                             ident[:]), r=[("qk3", b, 0), ("qk3", b, 1), ("ident",)], w=[("psT", 0)])
                    S.dve(CP(qkT[:, b, :, :], psT[:, 0:4, :]), w=[("qkT", b)], x=[("psT", 0)])

                    def at(e, b=b):
                        ins = None
                        for dc in range(2):
                            ins = e.matmul(psAT[:, 0:128], qkT[:, b, 2 + dc, :], qkT[:, b, dc, :],
                                           start=(dc == 0), stop=(dc == 1))
                        return ins
                    S.pe(at, r=[("qkT", b)], w=[("psat",)])
                    S.dve(TT(attT[:, b, :], psAT[:, 0:128], rmask[:, :], ALU.mult), r=[("rmask",)],
                          w=[("attT", b)], x=[("psat",)])
                    if not sample:
                        def om(e, b=b):
                            e.matmul(psO[:, :], attT[:, b, :], vb[:, b, :], start=True, stop=False)
                            ins = None
                            for dc in range(2):
                                ins = e.matmul(psO[:, :], qkT[:, b, dc, :], Sbf[:, dc, :], start=False, stop=(dc == 1))
                            return ins
                        S.pe(om, r=[("attT", b), ("vb", b), ("qkT", b), ("Sbf",)], w=[("pso",)])

                        def ds(e, b=b):
                            ins = None
                            for dc in range(2):
                                ins = e.matmul(psdS[:, dc, :], qk3[:, b, 2, dc * 128:(dc + 1) * 128], vb[:, b, :],
                                               start=True, stop=True)
                            return ins
                        S.pe(ds, r=[("qk3", b, 2), ("vb", b)], w=[("psds",)])
                        S.dve(STT(Sf[:], Sf[:], g128, psdS[:, :, :], ALU.mult, ALU.add), w=[("Sf",)], x=[("psds",)])
                        S.pool(CP(Sbf[:], Sf[:]), r=[("Sf",)], w=[("Sbf",)])
                        if t == npt - 1:
                            S.dma("sp", "stp", DMA(stp_o[h].rearrange("(dc p) e -> p dc e", p=128), Sf[:]), r=[("Sf",)])
                    else:
                        S.pe((lambda b=b: lambda e: e.matmul(psO[:, :], attT[:, b, :], vb[:, b, :],
                                                             start=True, stop=False))(),
                             r=[("attT", b), ("vb", b)], w=[("pso",)])
                        def U(R, sq):
                            sb_, s3 = sq % 2, sq % SR
                            R.dma("sp", "sin%d" % s3, DMA(Sb[:, s3, :, :],
                                                          w["state"][sq, h].rearrange("(dc p) e -> p dc e", p=128)),
                                  w=[("Sb", s3)])
                            R.act(ACP(Sbb[:, sb_, :, :], Sb[:, s3, :, :]), r=[("Sb", s3)], w=[("Sbb", sb_)])
                            R.dve(TT(qTm[:, sb_, :, :], qkT[:, b, 0:2, :],
                                     indT[:, sq, :].unsqueeze(1).broadcast_to([128, 2, 128]), ALU.mult),
                                  r=[("qkT", b), ("indT",)], w=[("qTm", sb_)])
                            R.act(AMUL(kzm[:, sb_, :], qk3[:, b, 2, :], ind[:, sq:sq + 1]),
                                  r=[("qk3", b, 2), ("ind",)], w=[("kzm", sb_)])

                        def V(R, sq):
                            sb_, s3 = sq % 2, sq % SR

                            def om2(e):
                                ins = None
                                for dc in range(2):
                                    ins = e.matmul(psO[:, :], qTm[:, sb_, dc, :], Sbb[:, sb_, dc, :], start=False,
                                                   stop=(sq == NB - 1 and dc == 1))
                                return ins
                            R.pe(om2, r=[("qTm", sb_), ("Sbb", sb_)], w=[("pso",)])
                            if sq % 2 == 0:
                                def ds2(e):
                                    ins = None
                                    for dc in range(2):
                                        ins = e.matmul(psdS[:, dc, :], kzm[:, sb_, dc * 128:(dc + 1) * 128], vb[:, b, :],
                                                       start=True, stop=True)
                                    return ins
                                R.pe(ds2, r=[("kzm", sb_), ("vb", b)], w=[("psds",)])
                                R.dve(STT(Sb[:, s3, :, :], Sb[:, s3, :, :], g8, psdS[:, :, :], ALU.mult, ALU.add),
                                      w=[("Sb", s3)], x=[("psds",)])
                            else:
                                for dc, (bank, tag) in enumerate(((psA, ("psa",)), (psB, ("psb",)))):
                                    R.pe((lambda dc=dc, bank=bank: lambda e: e.matmul(
                                        bank[:, :], kzm[:, sb_, dc * 128:(dc + 1) * 128], vb[:, b, :],
                                        start=True, stop=True))(), r=[("kzm", sb_), ("vb", b)], w=[tag])
                                    R.dve(STT(Sb[:, s3, dc, :], Sb[:, s3, dc, :], g8, bank[:, :], ALU.mult, ALU.add),
                                          w=[("Sb", s3)], x=[tag])
                            R.dma("pool", "sout%d" % s3, DMA(sts_o[sq, h].rearrange("(dc p) e -> p dc e", p=128),
                                                             Sb[:, s3, :, :]), r=[("Sb", s3)])

                        def ln_steps(s):
                            out = []
                            if not host_ln:
                                return out
                            if 0 <= s < npt:
                                R = Rec(); ln_a1(R, s, s, lntp); out.append(R)
                            if 0 <= s - 1 < npt:
                                R = Rec(); ln_a2(R, s - 1, s - 1, gbp, lntp, "gbp"); out.append(R)
                            if 0 <= s - 2 < npt:
                                R = Rec(); ln_b(R, s - 2, s - 2, xbp, [psT]); out.append(R)
                            return out

                        for i in range(-1, max(NB, npt + 2)):
                            recs = []
                            if 0 <= i < NB:
                                R = Rec(); V(R, i); recs.append(R)
                            if i + 1 < NB:
                                R = Rec(); U(R, i + 1); recs.append(R)
                            recs += ln_steps(i + 1)
                            interleave(S, recs)
                    j = ti % 3
                    sc = lnt["sc"]
                    norm_ops(S, j, lnt, psO[:, :], None, ([], [("pso",)]), None)
                    S.act(ACTF(on[:, b, :], psO[:, :], AF.Identity, bias=sc[:, j, 2:3], scale=sc[:, j, 1:2]),
                          r=[("sc", j, 1), ("sc", j, 2)], w=[("on", b)], x=[("pso",)])
                    S.act(ACTF(sgt[:, b, :], psC[:, :], AF.Silu), w=[("sgt", b)], x=[("psc",)])
                    S.pool(TT(yb[:, b, :], on[:, b, :], sgt[:, b, :], ALU.mult), r=[("on", b), ("sgt", b)], w=[("yb", b)])
                    S.pe(TRS([psT[:, 4 + c, :] for c in range(4)], [yb[:, b, c * 128:(c + 1) * 128] for c in range(4)],
                             ident[:]), r=[("yb", b), ("ident",)], w=[("psT", 0)])
                    S.dve(CP(yT[:, b, :, :], psT[:, 4:8, :]), w=[("yT", b)], x=[("psT", 0)])

                    def op(e, b=b):
                        ins = None
                        for dh, bank in enumerate((psA, psB)):
                            for c in range(4):
                                ins = e.matmul(bank[:, :], yT[:, b, c, :], woh[:, c, dh * 512:(dh + 1) * 512],
                                               start=(c == 0), stop=(c == 3))
                        return ins
                    S.pe(op, r=[("yT", b), ("woh",)], w=[("psa",), ("psb",)])
                    S.dve(TT(x[:, t, 0:512], psA[:, :], x[:, t, 0:512], ALU.add), w=[("x", t, 0)], x=[("psa",)])
                    S.dve(TT(x[:, t, 512:1024], psB[:, :], x[:, t, 512:1024], ALU.add), w=[("x", t, 1)], x=[("psb",)])
                S.flush()

        def ret_prompt_all(S):
            with ExitStack() as ph:
                psQK = P.ps("psQK", [128, 512], F32, ph)
                psVG = P.ps("psVG", [128, 512], F32, ph)
                psT = P.ps("psT", [128, 8, 128], BF16, ph)
                psAT = P.ps("psAT", [128, 512], F32, ph)
                psO = P.ps("psO", [128, 512], F32, ph)
                psdS = P.ps("psdS", [128, 2, 512], F32, ph)
                psY = P.ps("psY", [128, 512], F32, ph)
                wih = P.sb("wih", [128, 8, 1536], BF16, ph)
                woh = P.sb("woh", [128, 4, D], BF16, ph)
                xp = P.sb("xp", [128, 2, 1024], F32, ph)
                rmask = P.sb("rmask", [128, 128], BF16, ph)
                rA = P.sb("rA", [128, 2, 256], F32, ph)
                rB = P.sb("rB", [128, 2, 256], F32, ph)
                qk2 = P.sb("qk2", [128, 4, 2, 256], BF16, ph)
                qkT = P.sb("qkT", [128, 3, 4, 128], BF16, ph)
                vb = P.sb("vb", [128, 4, 512], BF16, ph)
                attT = P.sb("attT", [128, 3, 128], BF16, ph)
                sgt = P.sb("sgt", [128, 4, 512], F32, ph)
                osb = P.sb("osb", [128, 3, 512], F32, ph)
                yb = P.sb("yb", [128, 2, 512], BF16, ph)
                yT = P.sb("yT", [128, 2, 4, 128], BF16, ph)
                Uf = P.sb("Uf", [128, 2, 2, 512], F32, ph)
                Sbf = P.sb("Sbf", [128, 2, 2, 512], BF16, ph)
                Sout = P.sb("Sout", [128, 2, 512], F32, ph)
                Fin = P.sb("Fin", [128, 2, 2, 512], F32, ph)
                coef = P.sb("coef", [128, 32], F32, ph)
                lnt = ln_bufs(ph)
                wi = w["ret_w_in"]
                G128 = [math.exp(LG[h] * 128.0) for h in range(RH)]

                def load_wih(S, h):
                    for ci, (c0, n, d0) in enumerate(((h * 256, 256, 0), (1024 + h * 256, 256, 256),
                                                      (2048 + h * 512, 512, 512), (4096 + h * 512, 512, 1024))):
                        S.dma("pool", "wi%d" % ci, DMA(wih[:, :, d0:d0 + n],
                                                      wi[:, c0:c0 + n].rearrange("(k p) n -> p k n", p=128)),
                              w=[("wih", ci)])

                def load_woh(S, h):
                    S.dma("pool", "woh", DMA(woh[:], w["ret_w_o"][h * 512:(h + 1) * 512, :].rearrange("(c p) n -> p c n", p=128)),
                          w=[("woh",)])

                def s_init_parts(h):
                    hp = h % 2
                    parts = []

                    def ld(R, c2):
                        fb = c2 % 2
                        R.dma("sp", "fin%d" % fb, DMA(Fin[:, fb], w["fall"][c2, h].rearrange("(dc p) e -> p dc e", p=128)),
                              w=[("Fin", fb)])
                    R = Rec(); ld(R, 0); ld(R, 1); parts.append(R)
                    for c2 in range(N_CORES):
                        R = Rec()
                        fb = c2 % 2
                        cf = coef[:, c2 * 4 + h:c2 * 4 + h + 1]
                        if c2 == 0:
                            R.dve(TS(Uf[:, hp], Fin[:, fb], cf, ALU.mult), r=[("Fin", fb), ("coef",)], w=[("Uf", hp)])
                        else:
                            R.dve(STT(Uf[:, hp], Fin[:, fb], cf, Uf[:, hp], ALU.mult, ALU.add),
                                  r=[("Fin", fb), ("coef",)], w=[("Uf", hp)])
                        if c2 + 2 < N_CORES:
                            ld(R, c2 + 2)
                        parts.append(R)
                    R = Rec()
                    R.act(ACP(Sbf[:, hp], Uf[:, hp]), r=[("Uf", hp)], w=[("Sbf", hp)])
                    R.act(AMUL(Uf[:, hp], Uf[:, hp], 1.0 / G128[h]), r=[("Sbf", hp)], w=[("Uf", hp)])
                    parts.append(R)
                    return parts

                S.dma("pool", "rmask", DMA(rmask[:], w["masks"][1, :, 0:128]), w=[("rmask",)])
                S.dma("sp", "coef", DMA(coef[:], w["coef"][:, :]), w=[("coef",)])
                load_wih(S, 0)
                load_woh(S, 0)
                interleave(S, s_init_parts(0)[0:1])
                for R_ in s_init_parts(0)[1:]:
                    interleave(S, [R_])

                def stA1(R, u):
                    h, t = divmod(u, npt)
                    cs = slice(t * 128, (t + 1) * 128)
                    b2, b4 = u % 2, u % 4
                    R.pe(MM8(psQK[:, :], (lambda k: xT[:, k, cs]), (lambda k: wih[:, k, 0:512])),
                         r=[("xT", t), ("wih", 0), ("wih", 1)], w=[("psqk",)])
                    R.dma("sp", "xp%d" % b2, DMA(xp[:, b2, :], w["xposh"][h, cs, :]), w=[("xp", b2)])
                    tab = xp[:, b2, :]
                    v3 = psQK[:, :].rearrange("p (a c) -> p a c", a=2)
                    v4 = psQK[:, :].rearrange("p (a i two) -> p a i two", a=2, two=2)
                    bb4 = rB[:, :, :].rearrange("p a (i two) -> p a i two", two=2)
                    R.dve(TT(rA[:, :, :], v3, tab[:, 0:512].rearrange("p (a c) -> p a c", a=2), ALU.mult),
                          r=[("xp", b2)], w=[("rA",)], x=[("psqk",)])
                    R.pe(MM8(psVG[:, :], (lambda k: xT[:, k, cs]), (lambda k: wih[:, k, 512:1024])),
                         r=[("xT", t), ("wih", 2)], w=[("psvg",)])
                    R.dve(TT(bb4[:, :, :, 0], v4[:, :, :, 1], tab[:, 512:768].rearrange("p (a c) -> p a c", a=2), ALU.mult),
                          r=[("xp", b2)], w=[("rB", 0)], x=[("psqk",)])
                    R.act(ACP(vb[:, b4, :], psVG[:, :]), w=[("vb", b4)], x=[("psvg",)])
                    R.dve(TT(bb4[:, :, :, 1], v4[:, :, :, 0], tab[:, 768:1024].rearrange("p (a c) -> p a c", a=2), ALU.mult),
                          r=[("xp", b2)], w=[("rB", 1)], x=[("psqk",)])
                    R.pe(MM8(psVG[:, :], (lambda k: xT[:, k, cs]), (lambda k: wih[:, k, 1024:1536])),
                         r=[("xT", t), ("wih", 3)], w=[("psvg",)])
                    R.pool(TT(qk2[:, b4, :, :], rA[:, :, :], rB[:, :, :], ALU.add),
                           r=[("rA",), ("rB", 0), ("rB", 1)], w=[("qk2", b4)])
                    R.act(ACTF(sgt[:, b4, :], psVG[:, :], AF.Silu), w=[("sgt", b4)], x=[("psvg",)])

                def stA2(R, u):
                    b3, b4 = u % 3, u % 4
                    R.pe(TRS([psT[:, a, :] for a in range(4)],
                             [qk2[:, b4, 0, 0:128], qk2[:, b4, 0, 128:256], qk2[:, b4, 1, 0:128], qk2[:, b4, 1, 128:256]],
                             ident[:]), r=[("qk2", b4), ("ident",)], w=[("psT", 0)])
                    R.dve(CP(qkT[:, b3, :, :], psT[:, 0:4, :]), w=[("qkT", b3)], x=[("psT", 0)])

                    def at(e):
                        ins = None
                        for dc in range(2):
                            ins = e.matmul(psAT[:, 0:128], qkT[:, b3, 2 + dc, :], qkT[:, b3, dc, :],
                                           start=(dc == 0), stop=(dc == 1))
                        return ins
                    R.pe(at, r=[("qkT", b3)], w=[("psat",)])
                    R.dve(TT(attT[:, b3, :], psAT[:, 0:128], rmask[:, :], ALU.mult), r=[("rmask",)],
                          w=[("attT", b3)], x=[("psat",)])

                def stB(R, u):
                    h, t = divmod(u, npt)
                    hp = h % 2
                    g128 = G128[h]
                    b3, b4 = u % 3, u % 4

                    def om(e):
                        e.matmul(psO[:, :], attT[:, b3, :], vb[:, b4, :], start=True, stop=False)
                        ins = None
                        for dc in range(2):
                            ins = e.matmul(psO[:, :], qkT[:, b3, dc, :], Sbf[:, hp, dc, :], start=False, stop=(dc == 1))
                        return ins
                    R.pe(om, r=[("attT", b3), ("vb", b4), ("qkT", b3), ("Sbf", hp)], w=[("pso",)])

                    def ds(e):
                        ins = None
                        for dc in range(2):
                            ins = e.matmul(psdS[:, dc, :], qk2[:, b4, 1, dc * 128:(dc + 1) * 128], vb[:, b4, :],
                                           start=True, stop=True)
                        return ins
                    R.pe(ds, r=[("qk2", b4), ("vb", b4)], w=[("psds",)])
                    R.dve(STT(Uf[:, hp], Uf[:, hp], g128, psdS[:, :, :], ALU.mult, ALU.add), w=[("Uf", hp)], x=[("psds",)])
                    R.act(ACP(osb[:, b3, :], psO[:, :]), w=[("osb", b3)], x=[("pso",)])
                    R.act(AMUL(Sbf[:, hp], Uf[:, hp], g128), r=[("Uf", hp)], w=[("Sbf", hp)])
                    if t == npt - 1:
                        R.act(AMUL(Sout[:], Uf[:, hp], g128), r=[("Uf", hp)], w=[("Sout",)])
                        R.dma("sp", "stp", DMA(stp_o[h].rearrange("(dc p) e -> p dc e", p=128), Sout[:]), r=[("Sout",)])

                def stC1(R, u):
                    b2, b3, b4 = u % 2, u % 3, u % 4
                    j = u % 3
                    st, mv, sc = lnt["stats"], lnt["mv"], lnt["sc"]
                    R.dve(lambda e: e.bn_stats(out=st[:, j, 0, :], in_=osb[:, b3, :]), r=[("osb", b3)], w=[("st", j, 0)])
                    R.dve(lambda e: e.bn_aggr(out=mv[:, j, :], in_=st[:, j, 0, :]), r=[("st", j, 0)], w=[("mv", j)])
                    R.act(ACTF(sc[:, j, 0:1], mv[:, j, 1:2], AF.Sqrt, bias=lnt["eps"][:, 0:1]),
                          r=[("mv", j), ("eps",)], w=[("sc", j, 0)])
                    R.dve(lambda e: e.reciprocal(out=sc[:, j, 1:2], in_=sc[:, j, 0:1]), r=[("sc", j, 0)], w=[("sc", j, 1)])
                    R.dve(lambda e: e.tensor_scalar(out=osb[:, b3, :], in0=osb[:, b3, :], scalar1=mv[:, j, 0:1],
                                                    scalar2=sc[:, j, 1:2], op0=ALU.subtract, op1=ALU.mult),
                          r=[("mv", j), ("sc", j, 1)], w=[("osb", b3)])
                    R.pool(TT(yb[:, b2, :], osb[:, b3, :], sgt[:, b4, :], ALU.mult),
                           r=[("osb", b3), ("sgt", b4)], w=[("yb", b2)])

                def stC2(R, u):
                    h, t = divmod(u, npt)
                    b2 = u % 2
                    R.pe(TRS([psT[:, 4 + c, :] for c in range(4)], [yb[:, b2, c * 128:(c + 1) * 128] for c in range(4)],
                             ident[:]), r=[("yb", b2), ("ident",)], w=[("psT", 0)])
                    R.dve(CP(yT[:, b2, :, :], psT[:, 4:8, :]), w=[("yT", b2)], x=[("psT", 0)])
                    for dh in range(2):
                        def op(e, dh=dh):
                            ins = None
                            for c in range(4):
                                ins = e.matmul(psY[:, :], yT[:, b2, c, :], woh[:, c, dh * 512:(dh + 1) * 512],
                                               start=(c == 0), stop=(c == 3))
                            return ins
                        R.pe(op, r=[("yT", b2), ("woh",)], w=[("psy",)])
                        R.dve(TT(x[:, t, dh * 512:(dh + 1) * 512], psY[:, :], x[:, t, dh * 512:(dh + 1) * 512], ALU.add),
                              w=[("x", t, dh)], x=[("psy",)])

                NU = RH * npt
                stages = [(stB, 0), (stA2, 1), (stC1, -1), (stA1, 2), (stC2, -2)]
                init_at = {}
                for h in range(1, RH):
                    for k, R_ in enumerate(s_init_parts(h)):
                        init_at[h * npt - 12 + k] = R_
                for i in range(-2, NU + 3):
                    recs = []
                    if i in init_at:
                        recs.append(init_at[i])
                    for fn_, off in stages:
                        u = i + off
                        if not (0 <= u < NU):
                            continue
                        h, t = divmod(u, npt)
                        if t == 0 and h > 0:
                            if fn_ is stA1:
                                load_wih(S, h)
                            if fn_ is stC2:
                                load_woh(S, h)
                        R = Rec()
                        fn_(R, u)
                        recs.append(R)
                    interleave(S, recs)
                S.flush()

        stop = cfg.stop_after
        if stage == 1:
            ffn_phase(S, w["ffn1_wg"][0], w["ffn1_wu"][0], w["ffn1_wd"][0], 0, True)
            if stop == "ffn1":
                dump_x("dbg_x")
                return nc
            attn_phases(S)
            if stop == "attn":
                dump_x("dbg_x")
                return nc
            ffn_phase(S, w["ffn2_wg"][0], w["ffn2_wu"][0], w["ffn2_wd"][0], 2, False, pre_ln=1)
            ffn_phase(S, w["ffn1_wg"][1], w["ffn1_wu"][1], w["ffn1_wd"][1], 3, False)
            sweep1(S)
            for t in range(npt + 1):
                S.dma("sp", "tile%d" % t, DMA(xmid_o[t * 128:(t + 1) * 128, :], x[:, t, :]),
                      r=[("x", t, 0), ("x", t, 1)])
            S.flush()
        else:
            ret_prompt_all(S)
            for h in range(RH):
                ret_pass(S, h, True, host_ln=(h == 0))
            if stop == "ret":
                dump_x("dbg_x")
                return nc
            ffn_phase(S, w["ffn2_wg"][1], w["ffn2_wu"][1], w["ffn2_wd"][1], 5, False, final_out=yout, pre_ln=4,
                      pre_tiles=[cfg.samp])
    return nc


def _tables(c, cfg):
    npt, NT = cfg.npt, cfg.nt
    ntl = npt * 128
    r = np.arange(128)
    pos = np.zeros(NT * 128, np.float32)
    pos[0:ntl] = c * ntl + np.arange(ntl)
    pos[cfg.samp * 128:(cfg.samp + 1) * 128] = PAST + (r % 8)
    pos[cfg.halo * 128:(cfg.halo + 1) * 128] = (c * ntl - 128 + r) if c > 0 else r
    pos = pos.astype(np.float32)
    inv = (np.float32(10000.0) ** (-np.arange(0, HD, 2, dtype=np.float32) / np.float32(HD))).astype(np.float32)
    ang = (pos[:, None] * inv[None, :]).astype(np.float32)
    cs, sn = np.cos(ang).astype(np.float32), np.sin(ang).astype(np.float32)
    rope = np.concatenate([cs, cs, -sn, sn], axis=1).astype(np.float32)
    inv2 = (np.float32(1.0) / (np.float32(10000.0) ** np.linspace(0.0, 1.0, RQK // 2, dtype=np.float32))).astype(np.float32)
    ang2 = (pos[:, None] * inv2[None, :]).astype(np.float32)
    c2, s2 = np.cos(ang2).astype(np.float32), np.sin(ang2).astype(np.float32)
    xpos = np.concatenate([np.repeat(c2, 2, axis=1), -s2, s2], axis=1).astype(np.float32)
    k = r[:, None]
    q = r[None, :]
    m0 = (k > q).astype(np.float32)
    m1 = (k <= q).astype(np.float32)
    m2 = np.zeros_like(m0) if c == 0 else m0
    m3 = ((k // 8 == q // 8) & (k % 8 <= q % 8)).astype(np.float32)
    col = np.arange(512)[None, :]
    m4 = (k > (col % 8)).astype(np.float32)
    masks = np.stack([np.tile(m0, (1, 4)), np.tile(m1, (1, 4)), np.tile(m2, (1, 4)), np.tile(m3, (1, 4)), m4]).astype(np.float32)
    gam = np.array(GAM, np.float64)
    sc = RQK ** -0.5
    zeta1 = np.zeros((128, npt * 4), np.float64)
    for t in range(npt):
        for h in range(RH):
            zeta1[:, t * 4 + h] = sc * gam[h] ** (ntl - 1 - (t * 128 + r))
    rscal = np.zeros((128, 24), np.float64)
    for h in range(RH):
        rscal[:, 0 + h] = gam[h] ** (r + 1.0)
        rscal[:, 4 + h] = sc * gam[h] ** (-(r + 1.0))
        rscal[:, 8 + h] = sc * gam[h] ** (127.0 - r)
        i8 = (r % 8).astype(np.float64)
        rscal[:, 12 + h] = gam[h] ** (i8 + 1.0)
        rscal[:, 16 + h] = sc * gam[h] ** (-(i8 + 1.0))
        rscal[:, 20 + h] = sc * gam[h] ** (7.0 - i8)
    coef = np.zeros((128, 32), np.float64)
    for c2_ in range(N_CORES):
        for h in range(RH):
            if c2_ < c:
                coef[:, c2_ * 4 + h] = gam[h] ** (float(ntl) * (c - 1 - c2_))
    c2d = np.repeat(c2[0:ntl].astype(np.float64), 2, axis=1)
    s2d = s2[0:ntl].astype(np.float64)
    pl = (np.arange(ntl) % 128).astype(np.float64)
    xposh = np.zeros((RH, ntl, 1024), np.float32)
    for h in range(RH):
        xi = (gam[h] ** (pl + 1.0))[:, None]
        kt = (sc * gam[h] ** (-(pl + 1.0)))[:, None]
        xposh[h] = np.concatenate([c2d * xi, c2d * kt, -s2d * xi, -s2d * kt, s2d * xi, s2d * kt], axis=1)
    ind = (r[:, None] // 8 == np.arange(16)[None, :]).astype(np.float32)
    indT = np.tile((np.arange(16)[:, None] == (r[None, :] // 8)).astype(np.float32).reshape(1, 16 * 128), (128, 1))
    return dict(rope=rope, xpos=xpos, masks=masks, zeta1=zeta1.astype(np.float32), rscal=rscal.astype(np.float32),
                coef=coef.astype(np.float32), ind=ind, indT=indT.astype(np.float32), xposh=xposh,
                identf=np.eye(128, dtype=np.float32))


_PROGS = {}


def _prog(stage):
    if stage not in _PROGS:
        _PROGS[stage] = build_program(Cfg(), stage)
    return _PROGS[stage]


def kernel(x_prompt, x_sample, cache_k_win, cache_v_win, state_ret,
           ffn1_w_gate, ffn1_w_up, ffn1_w_down, ffn2_w_gate, ffn2_w_up, ffn2_w_down,
           ln_g, ln_b, attn_w_qkv, attn_w_o, attn_sinks, ret_w_in, ret_w_o):
    cfg = Cfg()
    f32 = lambda a: np.ascontiguousarray(np.asarray(a, dtype=np.float32))
    xp = f32(x_prompt)[0]
    xs = f32(x_sample)
    ck = f32(cache_k_win)[0].reshape(128, 128, 256)
    cv = f32(cache_v_win)[0].reshape(128, 128, 256)
    st = f32(state_ret)[0]
    lng = f32(ln_g).reshape(6, D)
    lnb = f32(ln_b).reshape(6, D)
    wts1 = {"ffn1_wg": f32(ffn1_w_gate), "ffn1_wu": f32(ffn1_w_up), "ffn1_wd": f32(ffn1_w_down),
            "ffn2_wg": f32(ffn2_w_gate), "ffn2_wu": f32(ffn2_w_up), "ffn2_wd": f32(ffn2_w_down),
            "attn_w_qkv": f32(attn_w_qkv)[0], "attn_w_o": f32(attn_w_o)[0],
            "sinks2": np.ascontiguousarray(f32(attn_sinks)[0].reshape(8, 2).T),
            "ret_w_in": f32(ret_w_in)[0], "ln_g": lng, "ln_b": lnb}
    tabs = [_tables(c, cfg) for c in range(N_CORES)]
    ntl = cfg.npt * 128
    in1 = []
    for c in range(N_CORES):
        halo = xp[c * ntl - 128:c * ntl] if c > 0 else xp[0:128]
        xin = np.concatenate([xp[c * ntl:(c + 1) * ntl], xs[c * 16:(c + 1) * 16].reshape(128, D), halo], axis=0)
        m = dict(wts1)
        m.update(xin=np.ascontiguousarray(xin), cache_k=ck[c * 16:(c + 1) * 16], cache_v=cv[c * 16:(c + 1) * 16])
        for k_ in ("identf", "masks", "rope", "xpos", "zeta1"):
            m[k_] = tabs[c][k_]
        in1.append(m)
    r1 = run_bass_kernel_spmd(_prog(1), in1, core_ids=list(range(N_CORES))).results
    fall = np.ascontiguousarray(np.stack([r1[c]["F"] for c in range(N_CORES)], axis=0))
    wts2 = {"ffn2_wg": wts1["ffn2_wg"], "ffn2_wu": wts1["ffn2_wu"], "ffn2_wd": wts1["ffn2_wd"],
            "ret_w_in": wts1["ret_w_in"], "ret_w_o": f32(ret_w_o)[0], "ln_g": lng, "ln_b": lnb, "fall": fall}
    in2 = []
    for c in range(N_CORES):
        m = dict(wts2)
        m.update(xmid=r1[c]["xmid"], state=st[c * 16:(c + 1) * 16])
        for k_ in ("identf", "masks", "xpos", "rscal", "coef", "ind", "indT", "xposh"):
            m[k_] = tabs[c][k_]
        in2.append(m)
    r2 = run_bass_kernel_spmd(_prog(2), in2, core_ids=list(range(N_CORES))).results
    y_prompt = np.concatenate([r2[c]["y"][0:ntl] for c in range(N_CORES)], axis=0)[None]
    y_sample = np.concatenate([r2[c]["y"][ntl:ntl + 128].reshape(16, 8, D) for c in range(N_CORES)], axis=0)
    L = N_CORES - 1
    kwp = r1[L]["kwin_p"].reshape(1, 1, 128, NKV, HD)
    vwp = r1[L]["vwin_p"].reshape(1, 1, 128, NKV, HD)
    kws = np.concatenate([r1[c]["kwin_s"] for c in range(N_CORES)], axis=0).reshape(1, 128, 128, NKV, HD)
    vws = np.concatenate([r1[c]["vwin_s"] for c in range(N_CORES)], axis=0).reshape(1, 128, 128, NKV, HD)
    stp = r2[L]["state_p"].reshape(1, 1, RH, RQK, RV)
    sts = np.concatenate([r2[c]["state_s"] for c in range(N_CORES)], axis=0).reshape(1, 128, RH, RQK, RV)
    return (y_prompt.astype(np.float32), y_sample.astype(np.float32), kwp, vwp, kws, vws, stp, sts)
```
